# Optimizing a Trainium2 kernel written in Bass

```python
import math
import jax, jax.numpy as jnp
from jax import lax
import numpy as np

D_MODEL = 4096
BATCH = 4
SEQ = 4096
DEPTH = 2

N_HEADS = 32
HEAD_DIM = 128
N_KV_GROUPS = 4
HEADS_PER_GROUP = N_HEADS // N_KV_GROUPS
ATTN_WIDTH = N_HEADS * HEAD_DIM
KV_WIDTH = N_KV_GROUPS * HEAD_DIM
CMP_BLOCK = 32
CMP_STRIDE = 16
SEL_BLOCK = 64
SEL_TOPK = 16
WINDOW = 512
Q_CHUNK = 32
CONV_CH = D_MODEL
CONV_WIDTH = 31
D_FF = -(-8 * D_MODEL // (3 * 256)) * 256
NUM_BUCKETS = 32
MAX_EXACT = NUM_BUCKETS // 2
MAX_DISTANCE = 128
N_BRANCH_GATES = 3 * N_HEADS
IN_COLS = ATTN_WIDTH + 6 * KV_WIDTH + 2 * CONV_CH + 2 * D_MODEL + N_BRANCH_GATES
EPS = 1e-6
NEG_INF = -1e30

kernel_name = 'hybrid_nsa_conformer_griffin_merge'


def _rmsnorm(x, g):
    xf = x.astype(jnp.float32)
    y = xf * lax.rsqrt(jnp.mean(xf * xf, axis=-1, keepdims=True) + EPS) * g.astype(jnp.float32)
    return y.astype(x.dtype)


def _t5_bucket(dist):
    n = jnp.maximum(dist, 0)
    nf = jnp.maximum(n, MAX_EXACT).astype(jnp.float32)
    large = MAX_EXACT + (jnp.log(nf / MAX_EXACT) / math.log(MAX_DISTANCE / MAX_EXACT)
                         * (NUM_BUCKETS - MAX_EXACT)).astype(jnp.int32)
    large = jnp.minimum(large, NUM_BUCKETS - 1)
    return jnp.where(n < MAX_EXACT, n, large)


def _masked_softmax(s, mask, axis=-1):
    p = jax.nn.softmax(jnp.where(mask, s, NEG_INF), axis=axis)
    return jnp.where(mask, p, 0.0)


def _cmp_to_sel_matrix(seq):
    ratio = CMP_BLOCK // CMP_STRIDE
    n_cmp = seq // CMP_STRIDE - ratio + 1
    nsb = seq // SEL_BLOCK
    c_start = np.arange(n_cmp) * CMP_STRIDE
    s_start = np.arange(nsb) * SEL_BLOCK
    ov = np.minimum(c_start[:, None] + CMP_BLOCK, s_start[None, :] + SEL_BLOCK) - np.maximum(c_start[:, None], s_start[None, :])
    ov = np.clip(ov, 0, None) // CMP_STRIDE
    return jnp.asarray(ov.astype(np.float32))


def _compress(k, pos, w1, w2):
    B, G, T, dh = k.shape
    ratio = CMP_BLOCK // CMP_STRIDE
    ns = T // CMP_STRIDE
    n_cmp = ns - ratio + 1
    kr = k.reshape(B, G, ns, CMP_STRIDE, dh)
    blocks = jnp.concatenate([kr[:, :, i:i + n_cmp] for i in range(ratio)], axis=3)
    blocks = (blocks + pos).reshape(B, G, n_cmp, CMP_BLOCK * dh)
    return jax.nn.gelu(blocks @ w1) @ w2


def _nsa(q, kv, branch_gates, cmp_pos, cmp_w1, cmp_w2, rel_bias):
    B, T, _ = q.shape
    G, R, dh = N_KV_GROUPS, HEADS_PER_GROUP, HEAD_DIM
    qh = (q * (HEAD_DIM ** -0.5)).reshape(B, T, G, R, dh).transpose(0, 2, 3, 1, 4)
    k_cmp_in, v_cmp_in, k_slc, v_slc, k_win, v_win = [
        a.reshape(B, T, G, dh).transpose(0, 2, 1, 3) for a in jnp.split(kv, 6, axis=-1)]
    k_cmp = _compress(k_cmp_in, cmp_pos[0], cmp_w1[0], cmp_w2[0])
    v_cmp = _compress(v_cmp_in, cmp_pos[1], cmp_w1[1], cmp_w2[1])
    n_cmp = k_cmp.shape[2]
    cmp_end = jnp.arange(n_cmp, dtype=jnp.int32) * CMP_STRIDE + CMP_BLOCK - 1
    nsb = T // SEL_BLOCK
    n_sel = min(SEL_TOPK, nsb)
    sel_map = _cmp_to_sel_matrix(T)
    k_blocks = k_slc.reshape(B, G, nsb, SEL_BLOCK, dh)
    v_blocks = v_slc.reshape(B, G, nsb, SEL_BLOCK, dh)
    pad = ((0, 0), (0, 0), (WINDOW, 0), (0, 0))
    k_win_pad = jnp.pad(k_win, pad)
    v_win_pad = jnp.pad(v_win, pad)
    gates = jax.nn.sigmoid(branch_gates.astype(jnp.float32)).reshape(B, T, G, R, 3).transpose(0, 2, 3, 1, 4)
    table = rel_bias.astype(jnp.float32).T.reshape(G, R, NUM_BUCKETS)
    b_ix = jnp.arange(B)[:, None, None, None]
    g_ix = jnp.arange(G)[None, :, None, None]
    g6 = jnp.arange(G)[None, :, None, None, None, None]
    r6 = jnp.arange(R)[None, None, :, None, None, None]
    blk = jnp.arange(nsb, dtype=jnp.int32)

    def chunk(c):
        t0 = c * Q_CHUNK
        qc = lax.dynamic_slice_in_dim(qh, t0, Q_CHUNK, axis=3)
        gc = lax.dynamic_slice_in_dim(gates, t0, Q_CHUNK, axis=3)
        tpos = t0 + jnp.arange(Q_CHUNK, dtype=jnp.int32)
        d_c = tpos[:, None] - cmp_end[None, :]
        s = jnp.einsum('bgrqd,bgnd->bgrqn', qc, k_cmp).astype(jnp.float32) + table[:, :, _t5_bucket(d_c)]
        p_cmp = _masked_softmax(s, d_c >= 0)
        o_cmp = jnp.einsum('bgrqn,bgnd->bgrqd', p_cmp.astype(v_cmp.dtype), v_cmp)
        imp = jnp.einsum('bgrqn,nj->bgqj', p_cmp, sel_map)
        cur = tpos // SEL_BLOCK
        forced = (blk[None, :] == 0) | (blk[None, :] == cur[:, None]) | (blk[None, :] == cur[:, None] - 1)
        valid = blk[None, :] <= cur[:, None]
        imp = jnp.where(valid, jnp.where(forced, jnp.inf, imp), -jnp.inf)
        _, idx = lax.top_k(imp, n_sel)
        blk_valid = idx <= cur[None, None, :, None]
        kb = k_blocks[b_ix, g_ix, idx]
        vb = v_blocks[b_ix, g_ix, idx]
        kpos = idx[..., None] * SEL_BLOCK + jnp.arange(SEL_BLOCK, dtype=jnp.int32)
        d_s = tpos[None, None, :, None, None] - kpos
        m_s = ((d_s >= 0) & blk_valid[..., None])[:, :, None]
        s = jnp.einsum('bgrqd,bgqkld->bgrqkl', qc, kb).astype(jnp.float32) + table[g6, r6, _t5_bucket(d_s)[:, :, None]]
        p_s = _masked_softmax(s, m_s, axis=(-2, -1))
        o_slc = jnp.einsum('bgrqkl,bgqkld->bgrqd', p_s.astype(vb.dtype), vb)
        kw = lax.dynamic_slice_in_dim(k_win_pad, t0, Q_CHUNK + WINDOW, axis=2)
        vw = lax.dynamic_slice_in_dim(v_win_pad, t0, Q_CHUNK + WINDOW, axis=2)
        wpos = t0 - WINDOW + jnp.arange(Q_CHUNK + WINDOW, dtype=jnp.int32)
        d_w = tpos[:, None] - wpos[None, :]
        m_w = (d_w >= 0) & (d_w < WINDOW) & (wpos[None, :] >= 0)
        s = jnp.einsum('bgrqd,bgkd->bgrqk', qc, kw).astype(jnp.float32) + table[:, :, _t5_bucket(d_w)]
        p_w = _masked_softmax(s, m_w)
        o_win = jnp.einsum('bgrqk,bgkd->bgrqd', p_w.astype(vw.dtype), vw)
        o = gc[..., 0:1] * o_cmp + gc[..., 1:2] * o_slc + gc[..., 2:3] * o_win
        return o.astype(q.dtype)

    outs = lax.map(chunk, jnp.arange(T // Q_CHUNK, dtype=jnp.int32))
    return outs.transpose(1, 0, 4, 2, 3, 5).reshape(B, T, ATTN_WIDTH)


def _conformer_conv(u, b_glu, w_dw, b_dw, ln_g, ln_b):
    u = u + b_glu
    a, g = jnp.split(u, 2, axis=-1)
    h = a * jax.nn.sigmoid(g)
    h = lax.conv_general_dilated(h, w_dw[:, None, :], (1,), ((CONV_WIDTH - 1, 0),),
                                 dimension_numbers=('NWC', 'WIO', 'NWC'),
                                 feature_group_count=CONV_CH) + b_dw
    hf = h.astype(jnp.float32)
    mu = jnp.mean(hf, axis=-1, keepdims=True)
    var = jnp.mean(jnp.square(hf - mu), axis=-1, keepdims=True)
    hf = (hf - mu) * lax.rsqrt(var + EPS) * ln_g.astype(jnp.float32) + ln_b.astype(jnp.float32)
    return jax.nn.silu(hf).astype(u.dtype)


def setup_inputs(seed: int = 0) -> dict:
    key = jax.random.key(seed)
    ks = jax.random.split(key, 24)
    f32 = jnp.float32

    def nrm(k, shape, scale):
        return jax.random.normal(k, shape, f32) * scale

    L = DEPTH
    return {
        'x': nrm(ks[0], (BATCH, SEQ, D_MODEL), 1.0),
        'w_in': nrm(ks[1], (L, D_MODEL, IN_COLS), D_MODEL ** -0.5),
        'cmp_pos': nrm(ks[2], (L, 2, CMP_BLOCK, HEAD_DIM), 0.02),
        'cmp_w1': nrm(ks[3], (L, 2, CMP_BLOCK * HEAD_DIM, HEAD_DIM), (CMP_BLOCK * HEAD_DIM) ** -0.5),
        'cmp_w2': nrm(ks[4], (L, 2, HEAD_DIM, HEAD_DIM), HEAD_DIM ** -0.5),
        'rel_bias': nrm(ks[5], (NUM_BUCKETS, N_HEADS), 0.5),
        'w_attn_out': nrm(ks[6], (L, ATTN_WIDTH, D_MODEL), ATTN_WIDTH ** -0.5),
        'b_glu': nrm(ks[7], (L, 2 * CONV_CH), 0.02),
        'w_dw': nrm(ks[8], (L, CONV_WIDTH, CONV_CH), CONV_WIDTH ** -0.5),
        'b_dw': nrm(ks[9], (L, CONV_CH), 0.02),
        'conv_ln_g': 1.0 + nrm(ks[10], (L, CONV_CH), 0.02),
        'conv_ln_b': nrm(ks[11], (L, CONV_CH), 0.02),
        'w_conv_out': nrm(ks[12], (L, CONV_CH, D_MODEL), CONV_CH ** -0.5),
        'b_conv_out': nrm(ks[13], (L, D_MODEL), 0.02),
        'w_out': nrm(ks[14], (L, D_MODEL, D_MODEL), D_MODEL ** -0.5),
        'norm_mix': 1.0 + nrm(ks[15], (L, D_MODEL), 0.02),
        'norm_ffn': 1.0 + nrm(ks[16], (L, D_MODEL), 0.02),
        'w_ffn_gate': nrm(ks[17], (L, D_MODEL, D_FF), D_MODEL ** -0.5),
        'w_ffn_up': nrm(ks[18], (L, D_MODEL, D_FF), D_MODEL ** -0.5),
        'w_ffn_down': nrm(ks[19], (L, D_FF, D_MODEL), D_FF ** -0.5),
        'norm_final': 1.0 + nrm(ks[20], (D_MODEL,), 0.02),
    }


def reference(x, w_in, cmp_pos, cmp_w1, cmp_w2, rel_bias, w_attn_out, b_glu, w_dw, b_dw,
              conv_ln_g, conv_ln_b, w_conv_out, b_conv_out, w_out, norm_mix, norm_ffn,
              w_ffn_gate, w_ffn_up, w_ffn_down, norm_final):
    split_points = np.cumsum([ATTN_WIDTH, 6 * KV_WIDTH, 2 * CONV_CH, D_MODEL, D_MODEL]).tolist()
    for l in range(DEPTH):
        h = _rmsnorm(x, norm_mix[l])
        proj = h @ w_in[l]
        q, kv, conv_in, gate_a, gate_b, br_gates = jnp.split(proj, split_points, axis=-1)
        y_a = _nsa(q, kv, br_gates, cmp_pos[l], cmp_w1[l], cmp_w2[l], rel_bias) @ w_attn_out[l]
        y_b = _conformer_conv(conv_in, b_glu[l], w_dw[l], b_dw[l], conv_ln_g[l], conv_ln_b[l]) @ w_conv_out[l] + b_conv_out[l]
        merged = jax.nn.sigmoid(gate_a) * y_a + jax.nn.sigmoid(gate_b) * y_b
        x = x + merged @ w_out[l]
        h = _rmsnorm(x, norm_ffn[l])
        x = x + (jax.nn.silu(h @ w_ffn_gate[l]) * (h @ w_ffn_up[l])) @ w_ffn_down[l]
    return _rmsnorm(x, norm_final)
```

```python
import contextlib
import numpy as np
import ml_dtypes
import concourse.bass as bass
import concourse.mybir as mybir
from concourse.bass_utils import run_bass_kernel_spmd

F32 = mybir.dt.float32
BF16 = mybir.dt.bfloat16
AF = mybir.ActivationFunctionType
ALU = mybir.AluOpType
ENGS = ["sync", "scalar", "vector", "gpsimd", "tensor"]

D = 4096
NH = 32
DH = 128
NG = 4
DFF = 11008
INC = 23648
KC = 32
EPS = 1e-6
NEG = -30000.0
C_Q, C_KV, C_A, C_G, C_GA, C_GB, C_BR = 0, 4096, 7168, 11264, 15360, 19456, 23552
PROJ_ROWS = 23680


class Prog:
    def __init__(self, nc, es):
        self.nc = nc
        self.es = es
        self.sems = {}
        self.cnt = {}
        self.waited = {e: {} for e in ENGS}
        self.streams = {e: [] for e in ENGS}
        self.bufs = {}
        self.last_out = []

    def _need(self, eng, dep, kind):
        if dep is None:
            return
        sem, val = dep
        if sem == "c_" + eng and (eng == "tensor" or kind != "RAW"):
            return
        if self.waited[eng].get(sem, 0) >= val:
            return
        self.waited[eng][sem] = val
        self.streams[eng].append(("wait", sem, val))

    def op(self, eng, fn, reads=(), writes=(), dma=None):
        for k in reads:
            st = self.bufs.get(k)
            if st is not None:
                self._need(eng, st[0], "RAW")
        for k in writes:
            st = self.bufs.get(k)
            if st is not None:
                self._need(eng, st[0], "WAW")
                for r in st[1].items():
                    self._need(eng, r, "WAR")
        if dma is not None:
            sem, inc = dma, 16
        else:
            sem, inc = "c_" + eng, 1
        val = self.cnt.get(sem, 0) + inc
        self.cnt[sem] = val
        self.streams[eng].append(("op", fn, sem, inc))
        tok = (sem, val)
        for k in reads:
            st = self.bufs.setdefault(k, [None, {}])
            st[1][sem] = val
        for k in writes:
            self.bufs[k] = [tok, {}]
        return tok

    def end_phase(self):
        for e in ENGS:
            for s, v in self.cnt.items():
                if self.waited[e].get(s, 0) < v:
                    self.waited[e][s] = v
                    self.streams[e].append(("wait", s, v))
        for s in self.cnt:
            if s not in self.sems:
                self.sems[s] = self.es.enter_context(self.nc.semaphore(s))
        sems = self.sems
        streams = self.streams

        def run(name):
            def body(e):
                for it in streams[name]:
                    if it[0] == "wait":
                        e.wait_ge(sems[it[1]], it[2])
                    else:
                        it[1](e).then_inc(sems[it[2]], it[3])
            return body

        with self.nc.Block() as block:
            block.sync(run("sync"))
            block.scalar(run("scalar"))
            block.vector(run("vector"))
            block.gpsimd(run("gpsimd"))
            block.tensor(run("tensor"))
        self.streams = {e: [] for e in ENGS}
        self.bufs = {}


class Rot:
    def __init__(self, items):
        self.items = items
        self.i = 0

    def next(self):
        it = self.items[self.i % len(self.items)]
        self.i += 1
        return it


class KB:
    def __init__(self):
        self.nc = bass.Bass("TRN2", target_bir_lowering=False)
        self.es = contextlib.ExitStack()
        self.P = Prog(self.nc, self.es)
        self.uid = 0

    def dram(self, name, shape, dt, kind):
        return self.nc.dram_tensor(name, list(shape), dt, kind=kind).ap()

    def sb(self, ph, name, shape, dt):
        self.uid += 1
        return ph.enter_context(self.nc.sbuf_tensor(f"{name}_{self.uid}", list(shape), dt))

    def psum(self, ph, n=8):
        self.uid += 1
        return [ph.enter_context(self.nc.psum_tensor(f"ps{i}_{self.uid}", [128, 512], F32)) for i in range(n)]

    def load(self, q, out_ap, in_ap, key, sem, reads=()):
        return self.P.op(q, lambda e: e.dma_start(out=out_ap, in_=in_ap), reads=list(reads), writes=[key], dma=sem)

    def store(self, q, out_ap, in_ap, key_src, sem, wkey):
        return self.P.op(q, lambda e: e.dma_start(out=out_ap, in_=in_ap), reads=[key_src], writes=[wkey], dma=sem)

    def stage_pool(self, ph, name, n, dt, width=512):
        items = []
        for i in range(n):
            t = self.sb(ph, f"{name}{i}", [128, width], dt)
            items.append((t, (name, i), f"d_{name}{i}"))
        return Rot(items)

    def rmsnorm(self, ps, srcT, Tc, gain, ones_bf, out_fn, TB=1024):
        P = self.P
        with contextlib.ExitStack() as ph:
            xs = [self.sb(ph, f"nxs{b}", [128, TB], F32) for b in range(2)]
            sq = [self.sb(ph, f"nsq{b}", [128, TB], BF16) for b in range(2)]
            rstd = self.sb(ph, "nrstd", [128, TB], F32)
            ntg = TB // 512
            it = 0
            for t0 in range(0, Tc, TB):
                for c in range(KC):
                    b = it % 2
                    it += 1
                    self.load("gpsimd", xs[b][:, :], srcT[c * 128:(c + 1) * 128, t0:t0 + TB], ("nxs", b), f"d_nxs{b}", reads=[("dram", srcT.tensor.name)])
                    P.op("scalar", (lambda b=b: lambda e: e.activation(out=sq[b][:, :], in_=xs[b][:, :], func=AF.Square))(), reads=[("nxs", b)], writes=[("nsq", b)])
                    for tg in range(ntg):
                        P.op("tensor", (lambda b=b, tg=tg, c=c: lambda e: e.matmul(ps[tg][:, :], lhsT=ones_bf[0][:, :], rhs=sq[b][:, tg * 512:(tg + 1) * 512], start=(c == 0), stop=(c == KC - 1)))(),
                             reads=[("nsq", b), ones_bf[1]], writes=[("ps", tg)])
                for tg in range(ntg):
                    sl = slice(tg * 512, (tg + 1) * 512)
                    P.op("vector", (lambda tg=tg, sl=sl: lambda e: e.tensor_scalar(out=rstd[:, sl], in0=ps[tg][:, :], scalar1=1.0 / D, scalar2=EPS, op0=ALU.mult, op1=ALU.add))(), reads=[("ps", tg)], writes=[("nrstd", tg)])
                    P.op("scalar", (lambda sl=sl: lambda e: e.activation(out=rstd[:, sl], in_=rstd[:, sl], func=AF.Sqrt))(), reads=[("nrstd", tg)], writes=[("nrstd", tg)])
                    P.op("vector", (lambda sl=sl: lambda e: e.reciprocal(out=rstd[:, sl], in_=rstd[:, sl]))(), reads=[("nrstd", tg)], writes=[("nrstd", tg)])
                for c in range(KC):
                    b = it % 2
                    it += 1
                    self.load("gpsimd", xs[b][:, :], srcT[c * 128:(c + 1) * 128, t0:t0 + TB], ("nxs", b), f"d_nxs{b}", reads=[("dram", srcT.tensor.name)])
                    out_fn(c, t0, xs[b], ("nxs", b), rstd, [("nrstd", tg) for tg in range(ntg)], TB)
            P.end_phase()

    def wbufs(self, ph, SEG=16):
        wst = [self.sb(ph, f"wst{b}", [128, SEG, 128], F32) for b in range(2)]
        wbf = [self.sb(ph, f"wbf{b}", [128, SEG, 128], BF16) for b in range(2)]
        return (wst, wbf, SEG, [0])

    def proj(self, wb, ps, res, res_keyfn, KCn, TW, chunks):
        P = self.P
        NTG = TW // 512
        wst, wbf, SEG, sic = wb
        segs = []
        for ci, (W, c0, ncol, epi) in enumerate(chunks):
            pb = ci % 2
            banks = [(ps[pb * 4 + tg], ("ps", pb * 4 + tg)) for tg in range(NTG)]
            for s0 in range(0, KCn, SEG):
                ns = min(SEG, KCn - s0)
                segs.append((ci, W, c0, ncol, epi, banks, s0, ns, s0 + ns >= KCn))

        def prefetch(i):
            ci, W, c0, ncol, epi, banks, s0, ns, last = segs[i]
            b = (sic[0] + i) % 2
            src = W[s0 * 128:(s0 + ns) * 128, c0:c0 + ncol].rearrange("(k p) n -> p k n", p=128)
            h = SEG // 2
            halves = [(0, min(ns, h))] + ([(h, ns)] if ns > h else [])
            for hh, (k0, k1) in enumerate(halves):
                P.op("sync", (lambda b=b, k0=k0, k1=k1, src=src, ncol=ncol: lambda e: e.dma_start(out=wst[b][:, k0:k1, 0:ncol], in_=src[:, k0:k1, :]))(),
                     writes=[("wst", b, hh)], dma=f"d_wst{b}{hh}")
                if hh == 0:
                    P.op("vector", (lambda b=b, k0=k0, k1=k1, ncol=ncol: lambda e: e.tensor_copy(out=wbf[b][:, k0:k1, 0:ncol], in_=wst[b][:, k0:k1, 0:ncol]))(),
                         reads=[("wst", b, hh)], writes=[("wbf", b, hh)])
                else:
                    P.op("scalar", (lambda b=b, k0=k0, k1=k1, ncol=ncol: lambda e: e.activation(out=wbf[b][:, k0:k1, 0:ncol], in_=wst[b][:, k0:k1, 0:ncol], func=AF.Copy))(),
                         reads=[("wst", b, hh)], writes=[("wbf", b, hh)])

        prefetch(0)
        for i, (ci, W, c0, ncol, epi, banks, s0, ns, last) in enumerate(segs):
            if i + 1 < len(segs):
                prefetch(i + 1)
            b = (sic[0] + i) % 2
            h = SEG // 2
            for k in range(ns):
                hh = 0 if k < h else 1
                kk = s0 + k
                for tg in range(NTG):
                    P.op("tensor", (lambda b=b, k=k, kk=kk, tg=tg, ncol=ncol, banks=banks: lambda e: e.matmul(
                        banks[tg][0][0:ncol, :], lhsT=wbf[b][:, k, 0:ncol], rhs=res[:, kk, tg * 512:(tg + 1) * 512],
                        start=(kk == 0), stop=(kk == KCn - 1)))(),
                        reads=[("wbf", b, hh), res_keyfn(kk)], writes=[banks[tg][1]])
            if last:
                epi(ci, banks)
        sic[0] += len(segs)

    def load_res(self, res, srcT, TW, t0=0, nk=KC, q="sync", kname="res", grp=4):
        tok = None
        for c0 in range(0, nk, grp):
            c1 = min(nk, c0 + grp)
            src = srcT[c0 * 128:c1 * 128, t0:t0 + TW].rearrange("(f p) t -> p f t", p=128)
            keys = [(kname, c) for c in range(c0, c1)]
            tok = self.P.op(q, (lambda c0=c0, c1=c1, src=src: lambda e: e.dma_start(out=res[:, c0:c1, :], in_=src))(),
                            reads=[("dram", srcT.tensor.name)], writes=keys, dma="d_hres")
        for c in range(nk):
            self.P.bufs[(kname, c)][0] = tok


def build_A(Tc):
    kb = KB()
    nc, P = kb.nc, kb.P
    xT = kb.dram("xT", [D, Tc], F32, "ExternalInput")
    w_in = kb.dram("w_in", [D, INC], F32, "ExternalInput")
    gmix = kb.dram("gmix", [128, KC], F32, "ExternalInput")
    bglu = kb.dram("bglu", [128, 64], F32, "ExternalInput")
    projT = kb.dram("projT", [PROJ_ROWS, Tc], BF16, "ExternalOutput")
    with kb.es:
        emit_A(kb, Tc, xT, w_in, gmix, bglu, projT)
    return nc


def emit_A(kb, Tc, xT, w_in, gmix, bglu, projT):
    nc, P = kb.nc, kb.P
    NTG = Tc // 512
    with contextlib.ExitStack() as outer:
        res = kb.sb(outer, "res", [128, KC, Tc], BF16)
        ones = kb.sb(outer, "ones", [128, 128], BF16)
        gm = kb.sb(outer, "gm", [128, KC], F32)
        bg = kb.sb(outer, "bg", [128, 64], F32)
        ps = kb.psum(outer)
        P.op("vector", lambda e: e.memset(ones[:, :], 1.0), writes=["ones"])
        kb.load("sync", gm[:, :], gmix[:, :], "gm", "d_c0")
        kb.load("sync", bg[:, :], bglu[:, :], "bg", "d_c1")

        def norm_out(c, t0, xs, xk, rstd, rks, TB):
            P.op("vector", lambda e: e.scalar_tensor_tensor(out=res[:, c, t0:t0 + TB], in0=xs[:, :], scalar=gm[:, c:c + 1], in1=rstd[:, :], op0=ALU.mult, op1=ALU.mult),
                 reads=[xk, "gm"] + rks, writes=[("res", c)])
        kb.rmsnorm(ps, xT, Tc, gm, (ones, "ones"), norm_out, TB=min(1024, Tc))

        with contextlib.ExitStack() as ph:
            stb = kb.stage_pool(ph, "stb", 4, BF16)
            stf = kb.stage_pool(ph, "stf", 2, F32)
            ast = [kb.sb(ph, f"ast{tg}", [128, 512], F32) for tg in range(NTG)]
            flip = [0]

            def mk_epi(kind, row0, bi=None):
                def epi(ci, banks, n=128):
                    def _one(tg, pt, pk):
                        dst = projT[row0:row0 + n, tg * 512:(tg + 1) * 512]
                        if kind == "a":
                            P.op("vector", lambda e: e.tensor_scalar(out=ast[tg][:, :], in0=pt[:, :], scalar1=bg[:, bi:bi + 1], scalar2=None, op0=ALU.add),
                                 reads=[pk, "bg"], writes=[("ast", tg)])
                            return
                        st, sk, ssem = stb.next()
                        if kind == "g":
                            sf, fk, _ = stf.next()
                            P.op("scalar", lambda e: e.activation(out=sf[:, :], in_=pt[:, :], func=AF.Sigmoid, bias=bg[:, 32 + bi:33 + bi], scale=1.0),
                                 reads=[pk, "bg"], writes=[fk])
                            P.op("gpsimd", lambda e: e.tensor_tensor(out=st[:, :], in0=ast[tg][:, :], in1=sf[:, :], op=ALU.mult),
                                 reads=[fk, ("ast", tg)], writes=[sk])
                        elif kind == "sig":
                            P.op("scalar", lambda e: e.activation(out=st[0:n, :], in_=pt[0:n, :], func=AF.Sigmoid), reads=[pk], writes=[sk])
                        else:
                            sc = DH ** -0.5 if kind == "q" else 1.0
                            flip[0] ^= 1
                            if flip[0]:
                                P.op("scalar", lambda e: e.activation(out=st[:, :], in_=pt[:, :], func=AF.Copy, scale=sc), reads=[pk], writes=[sk])
                            else:
                                P.op("vector", lambda e: e.tensor_scalar(out=st[:, :], in0=pt[:, :], scalar1=sc, scalar2=None, op0=ALU.mult), reads=[pk], writes=[sk])
                        kb.store("gpsimd", dst, st[0:n, :], sk, ssem, ("dram", "projT"))
                    for tg, (pt, pk) in enumerate(banks):
                        _one(tg, pt, pk)
                return epi

            chunks = []
            for i in range(32):
                chunks.append((w_in, C_Q + i * 128, 128, mk_epi("q", C_Q + i * 128)))
            for i in range(24):
                chunks.append((w_in, C_KV + i * 128, 128, mk_epi("copy", C_KV + i * 128)))
            for i in range(32):
                chunks.append((w_in, C_A + i * 128, 128, mk_epi("a", 0, i)))
                chunks.append((w_in, C_G + i * 128, 128, mk_epi("g", C_A + i * 128, i)))
            for i in range(32):
                chunks.append((w_in, C_GA + i * 128, 128, mk_epi("sig", C_GA + i * 128)))
            for i in range(32):
                chunks.append((w_in, C_GB + i * 128, 128, mk_epi("sig", C_GB + i * 128)))
            e96 = mk_epi("sig", C_BR)
            chunks.append((w_in, C_BR, 96, lambda ci, banks: e96(ci, banks, n=96)))
            kb.proj(kb.wbufs(ph), ps, res, lambda kk: ("res", kk), KC, Tc, chunks)
            P.end_phase()


def build_C(Tc, final):
    kb = KB()
    nc, P = kb.nc, kb.P
    t = {}
    t["xT"] = kb.dram("xT", [D, Tc], F32, "ExternalInput")
    t["gluT"] = kb.dram("gluT", [D, Tc + 32], BF16, "ExternalInput")
    t["attnT"] = kb.dram("attnT", [D, Tc], BF16, "ExternalInput")
    t["gaT"] = kb.dram("gaT", [D, Tc], BF16, "ExternalInput")
    t["gbT"] = kb.dram("gbT", [D, Tc], BF16, "ExternalInput")
    t["wdw"] = kb.dram("wdw", [128, KC, 31], F32, "ExternalInput")
    t["vecs"] = kb.dram("vecs", [128, 6, KC], F32, "ExternalInput")
    t["w_co"] = kb.dram("w_co", [D, D], F32, "ExternalInput")
    t["w_ao"] = kb.dram("w_ao", [D, D], F32, "ExternalInput")
    t["w_out"] = kb.dram("w_out", [D, D], F32, "ExternalInput")
    t["w_fg"] = kb.dram("w_fg", [D, DFF], F32, "ExternalInput")
    t["w_fu"] = kb.dram("w_fu", [D, DFF], F32, "ExternalInput")
    t["w_fd"] = kb.dram("w_fd", [DFF, D], F32, "ExternalInput")
    t["outT"] = kb.dram("outT", [D, Tc], F32, "ExternalOutput")
    t["zT"] = kb.dram("zT", [D, Tc], BF16, "Internal")
    t["mbT"] = kb.dram("mbT", [D, Tc], F32, "Internal")
    t["mT"] = kb.dram("mT", [D, Tc], BF16, "Internal")
    t["x1T"] = kb.dram("x1T", [D, Tc], F32, "Internal")
    t["actT"] = kb.dram("actT", [DFF, Tc], BF16, "Internal")
    if final:
        t["x2T"] = kb.dram("x2T", [D, Tc], F32, "Internal")
    with kb.es:
        emit_C(kb, Tc, final, t)
    return nc


def emit_C(kb, Tc, final, t):
    nc, P = kb.nc, kb.P
    NTG = Tc // 512
    NF = DFF // 128
    with contextlib.ExitStack() as outer:
        ones = kb.sb(outer, "ones", [128, 128], BF16)
        vec = kb.sb(outer, "vec", [128, 6, KC], F32)
        wdw = kb.sb(outer, "wdw", [128, KC, 31], F32)
        ps = kb.psum(outer)
        P.op("vector", lambda e: e.memset(ones[:, :], 1.0), writes=["ones"])
        kb.load("sync", vec[:, :, :], t["vecs"][:, :, :], "vec", "d_c0")
        kb.load("sync", wdw[:, :, :], t["wdw"][:, :, :], "wdw", "d_c1")
        P.end_phase()

        with contextlib.ExitStack() as ph:
            acc = kb.sb(ph, "cacc", [128, KC, 512], F32)
            gl = [kb.sb(ph, f"cgl{b}", [128, 544], BF16) for b in range(4)]
            cb = [kb.sb(ph, f"ccb{b}", [128, 512], BF16) for b in range(2)]
            cq = [kb.sb(ph, f"ccq{b}", [128, 512], BF16) for b in range(2)]
            mean = kb.sb(ph, "cmean", [128, 512], F32)
            rstd = kb.sb(ph, "crstd", [128, 512], F32)
            tmp = [kb.sb(ph, f"ctmp{b}", [128, 512], F32) for b in range(2)]
            zst = kb.stage_pool(ph, "zst", 2, BF16)
            for tg in range(NTG):
                for c4 in range(0, KC, 4):
                    for j in range(4):
                        c = c4 + j
                        kb.load("sync", gl[j][:, :], t["gluT"][c * 128:(c + 1) * 128, tg * 512:tg * 512 + 544], ("cgl", j), f"d_cgl{j}")
                    for k in range(31):
                        for j in range(4):
                            c = c4 + j
                            if k == 0:
                                P.op("vector", (lambda j=j, c=c: lambda e: e.tensor_scalar(out=acc[:, c, :], in0=gl[j][:, 2:514], scalar1=wdw[:, c, 0:1], scalar2=vec[:, 0, c:c + 1], op0=ALU.mult, op1=ALU.add))(),
                                     reads=[("cgl", j), "wdw", "vec"], writes=[("cacc", c)])
                            else:
                                P.op("vector", (lambda j=j, c=c, k=k: lambda e: e.scalar_tensor_tensor(out=acc[:, c, :], in0=gl[j][:, 2 + k:514 + k], scalar=wdw[:, c, k:k + 1], in1=acc[:, c, :], op0=ALU.mult, op1=ALU.add))(),
                                     reads=[("cgl", j), "wdw", ("cacc", c)], writes=[("cacc", c)])
                    for j in range(4):
                        c = c4 + j
                        b = c % 2
                        P.op("gpsimd", (lambda c=c, b=b: lambda e: e.tensor_copy(out=cb[b][:, :], in_=acc[:, c, :]))(), reads=[("cacc", c)], writes=[("ccb", b)])
                        P.op("scalar", (lambda c=c, b=b: lambda e: e.activation(out=cq[b][:, :], in_=acc[:, c, :], func=AF.Square))(), reads=[("cacc", c)], writes=[("ccq", b)])
                        P.op("tensor", (lambda c=c, b=b: lambda e: e.matmul(ps[0][:, :], lhsT=ones[:, :], rhs=cb[b][:, :], start=(c == 0), stop=(c == KC - 1)))(), reads=[("ccb", b), "ones"], writes=[("ps", 0)])
                        P.op("tensor", (lambda c=c, b=b: lambda e: e.matmul(ps[1][:, :], lhsT=ones[:, :], rhs=cq[b][:, :], start=(c == 0), stop=(c == KC - 1)))(), reads=[("ccq", b), "ones"], writes=[("ps", 1)])
                P.op("vector", lambda e: e.tensor_scalar(out=mean[:, :], in0=ps[0][:, :], scalar1=1.0 / D, scalar2=None, op0=ALU.mult), reads=[("ps", 0)], writes=["cmean"])
                P.op("vector", lambda e: e.tensor_scalar(out=rstd[:, :], in0=ps[1][:, :], scalar1=1.0 / D, scalar2=EPS, op0=ALU.mult, op1=ALU.add), reads=[("ps", 1)], writes=["crstd"])
                P.op("vector", lambda e: e.tensor_tensor(out=tmp[0][:, :], in0=mean[:, :], in1=mean[:, :], op=ALU.mult), reads=["cmean"], writes=[("ctmp", 0)])
                P.op("vector", lambda e: e.tensor_tensor(out=rstd[:, :], in0=rstd[:, :], in1=tmp[0][:, :], op=ALU.subtract), reads=["crstd", ("ctmp", 0)], writes=["crstd"])
                P.op("scalar", lambda e: e.activation(out=rstd[:, :], in_=rstd[:, :], func=AF.Sqrt), reads=["crstd"], writes=["crstd"])
                P.op("vector", lambda e: e.reciprocal(out=rstd[:, :], in_=rstd[:, :]), reads=["crstd"], writes=["crstd"])
                for c in range(KC):
                    b = c % 2
                    st, sk, ssem = zst.next()
                    P.op("vector", (lambda c=c, b=b: lambda e: e.tensor_tensor(out=tmp[b][:, :], in0=acc[:, c, :], in1=mean[:, :], op=ALU.subtract))(), reads=[("cacc", c), "cmean"], writes=[("ctmp", b)])
                    P.op("gpsimd", (lambda b=b: lambda e: e.tensor_tensor(out=tmp[b][:, :], in0=tmp[b][:, :], in1=rstd[:, :], op=ALU.mult))(), reads=[("ctmp", b), "crstd"], writes=[("ctmp", b)])
                    P.op("scalar", (lambda c=c, b=b, st=st: lambda e: e.activation(out=st[:, :], in_=tmp[b][:, :], func=AF.Silu, scale=vec[:, 1, c:c + 1], bias=vec[:, 2, c:c + 1]))(), reads=[("ctmp", b), "vec"], writes=[sk])
                    kb.store("gpsimd", t["zT"][c * 128:(c + 1) * 128, tg * 512:(tg + 1) * 512], st[:, :], sk, ssem, ("dram", "zT"))
            P.end_phase()

        with contextlib.ExitStack() as ph2:
            res = kb.sb(ph2, "res", [128, KC, Tc], BF16)
            rk = lambda kk: ("res", kk)
            with contextlib.ExitStack() as ph:
                gin = kb.stage_pool(ph, "gin", 2, BF16)
                fin = kb.stage_pool(ph, "fin", 2, F32)
                stf = kb.stage_pool(ph, "stf", 2, F32)
                stb = kb.stage_pool(ph, "stb", 2, BF16)
                tf = kb.stage_pool(ph, "tf", 2, F32)

                def epi_b(n):
                    def epi(ci, banks):
                        def _one(tg, pt, pk):
                            g_t, gk, gsem = gin.next()
                            kb.load("gpsimd", g_t[:, :], t["gbT"][n * 128:(n + 1) * 128, tg * 512:(tg + 1) * 512], gk, gsem)
                            st, sk, ssem = stf.next()
                            P.op("vector", lambda e: e.scalar_tensor_tensor(out=st[:, :], in0=pt[:, :], scalar=vec[:, 3, n:n + 1], in1=g_t[:, :], op0=ALU.add, op1=ALU.mult), reads=[pk, gk, "vec"], writes=[sk])
                            kb.store("gpsimd", t["mbT"][n * 128:(n + 1) * 128, tg * 512:(tg + 1) * 512], st[:, :], sk, ssem, ("dram", "mbT"))
                        for tg, (pt, pk) in enumerate(banks):
                            _one(tg, pt, pk)
                    return epi

                def epi_a(n):
                    def epi(ci, banks):
                        def _one(tg, pt, pk):
                            g_t, gk, gsem = gin.next()
                            kb.load("gpsimd", g_t[:, :], t["gaT"][n * 128:(n + 1) * 128, tg * 512:(tg + 1) * 512], gk, gsem)
                            f_t, fk, fsem = fin.next()
                            kb.load("gpsimd", f_t[:, :], t["mbT"][n * 128:(n + 1) * 128, tg * 512:(tg + 1) * 512], fk, fsem, reads=[("dram", "mbT")])
                            tt, tk, _ = tf.next()
                            st, sk, ssem = stb.next()
                            P.op("vector", lambda e: e.tensor_tensor(out=tt[:, :], in0=pt[:, :], in1=g_t[:, :], op=ALU.mult), reads=[pk, gk], writes=[tk])
                            P.op("gpsimd", lambda e: e.tensor_tensor(out=st[:, :], in0=tt[:, :], in1=f_t[:, :], op=ALU.add), reads=[tk, fk], writes=[sk])
                            kb.store("gpsimd", t["mT"][n * 128:(n + 1) * 128, tg * 512:(tg + 1) * 512], st[:, :], sk, ssem, ("dram", "mT"))
                        for tg, (pt, pk) in enumerate(banks):
                            _one(tg, pt, pk)
                    return epi

                def epi_o(n):
                    def epi(ci, banks):
                        def _one(tg, pt, pk):
                            f_t, fk, fsem = fin.next()
                            kb.load("gpsimd", f_t[:, :], t["xT"][n * 128:(n + 1) * 128, tg * 512:(tg + 1) * 512], fk, fsem)
                            st, sk, ssem = stf.next()
                            P.op("vector", lambda e: e.tensor_tensor(out=st[:, :], in0=pt[:, :], in1=f_t[:, :], op=ALU.add), reads=[pk, fk], writes=[sk])
                            kb.store("gpsimd", t["x1T"][n * 128:(n + 1) * 128, tg * 512:(tg + 1) * 512], st[:, :], sk, ssem, ("dram", "x1T"))
                        for tg, (pt, pk) in enumerate(banks):
                            _one(tg, pt, pk)
                    return epi

                wb = kb.wbufs(ph)
                kb.load_res(res, t["zT"], Tc)
                kb.proj(wb, ps, res, rk, KC, Tc, [(t["w_co"], n * 128, 128, epi_b(n)) for n in range(KC)])
                P.end_phase()
                kb.load_res(res, t["attnT"], Tc)
                kb.proj(wb, ps, res, rk, KC, Tc, [(t["w_ao"], n * 128, 128, epi_a(n)) for n in range(KC)])
                P.end_phase()
                kb.load_res(res, t["mT"], Tc)
                kb.proj(wb, ps, res, rk, KC, Tc, [(t["w_out"], n * 128, 128, epi_o(n)) for n in range(KC)])
                P.end_phase()

            def norm_out(c, t0, xs, xk, rstd, rks, TB):
                P.op("vector", lambda e: e.scalar_tensor_tensor(out=res[:, c, t0:t0 + TB], in0=xs[:, :], scalar=vec[:, 4, c:c + 1], in1=rstd[:, :], op0=ALU.mult, op1=ALU.mult),
                     reads=[xk, "vec"] + rks, writes=[("res", c)])
            kb.rmsnorm(ps, t["x1T"], Tc, None, (ones, "ones"), norm_out, TB=min(1024, Tc))

            with contextlib.ExitStack() as ph:
                gs = [kb.sb(ph, f"gs{tg}", [128, 512], F32) for tg in range(NTG)]
                stb = kb.stage_pool(ph, "stb", 4, BF16)

                def epi_gate(f):
                    def epi(ci, banks):
                        def _one(tg, pt, pk):
                            P.op("scalar", lambda e: e.activation(out=gs[tg][:, :], in_=pt[:, :], func=AF.Silu), reads=[pk], writes=[("gs", tg)])
                        for tg, (pt, pk) in enumerate(banks):
                            _one(tg, pt, pk)
                    return epi

                def epi_up(f):
                    def epi(ci, banks):
                        def _one(tg, pt, pk):
                            st, sk, ssem = stb.next()
                            P.op("vector", lambda e: e.tensor_tensor(out=st[:, :], in0=pt[:, :], in1=gs[tg][:, :], op=ALU.mult), reads=[pk, ("gs", tg)], writes=[sk])
                            kb.store("gpsimd", t["actT"][f * 128:(f + 1) * 128, tg * 512:(tg + 1) * 512], st[:, :], sk, ssem, ("dram", "actT"))
                        for tg, (pt, pk) in enumerate(banks):
                            _one(tg, pt, pk)
                    return epi
                chunks = []
                for f in range(NF):
                    chunks.append((t["w_fg"], f * 128, 128, epi_gate(f)))
                    chunks.append((t["w_fu"], f * 128, 128, epi_up(f)))
                kb.proj(kb.wbufs(ph), ps, res, rk, KC, Tc, chunks)
                P.end_phase()

        dstT = t["x2T"] if final else t["outT"]
        with contextlib.ExitStack() as ph:
            act = kb.sb(ph, "act", [128, NF, 512], BF16)
            fin = kb.stage_pool(ph, "fin", 2, F32)
            stf = kb.stage_pool(ph, "stf", 2, F32)
            wb = kb.wbufs(ph)
            for tg in range(NTG):
                kb.load_res(act, t["actT"], 512, t0=tg * 512, nk=NF, kname="act", grp=8)

                def epi_d(n, tg=tg):
                    def epi(ci, banks):
                        pt, pk = banks[0]
                        f_t, fk, fsem = fin.next()
                        kb.load("gpsimd", f_t[:, :], t["x1T"][n * 128:(n + 1) * 128, tg * 512:(tg + 1) * 512], fk, fsem, reads=[("dram", "x1T")])
                        st, sk, ssem = stf.next()
                        P.op("vector", lambda e: e.tensor_tensor(out=st[:, :], in0=pt[:, :], in1=f_t[:, :], op=ALU.add), reads=[pk, fk], writes=[sk])
                        kb.store("gpsimd", dstT[n * 128:(n + 1) * 128, tg * 512:(tg + 1) * 512], st[:, :], sk, ssem, ("dram", "dst"))
                    return epi
                kb.proj(wb, ps, act, lambda kk: ("act", kk), NF, 512, [(t["w_fd"], n * 128, 128, epi_d(n)) for n in range(KC)])
            P.end_phase()

        if final:
            with contextlib.ExitStack() as ph:
                stf = kb.stage_pool(ph, "stf", 2, F32, width=min(1024, Tc))

                def norm_out2(c, t0, xs, xk, rstd, rks, TB):
                    st, sk, ssem = stf.next()
                    P.op("vector", lambda e: e.scalar_tensor_tensor(out=st[:, :], in0=xs[:, :], scalar=vec[:, 5, c:c + 1], in1=rstd[:, :], op0=ALU.mult, op1=ALU.mult),
                         reads=[xk, "vec"] + rks, writes=[sk])
                    kb.store("gpsimd", t["outT"][c * 128:(c + 1) * 128, t0:t0 + TB], st[:, :], sk, ssem, ("dram", "outT"))
                kb.rmsnorm(ps, t["x2T"], Tc, None, (ones, "ones"), norm_out2, TB=min(1024, Tc))


def lay_vec(v):
    v = np.asarray(v, dtype=np.float32)
    n = v.shape[0] // 128
    return np.ascontiguousarray(v.reshape(n, 128).T)


def lay_wdw(w):
    return np.ascontiguousarray(np.asarray(w, np.float32).T.reshape(KC, 128, 31).transpose(1, 0, 2))


_NC_CACHE = {}


def get_nc(kind, *args):
    key = (kind,) + tuple(args)
    if key not in _NC_CACHE:
        _NC_CACHE[key] = {"A": build_A, "C": build_C}[kind](*args)
    return _NC_CACHE[key]


def run_A(xT_list, w_in_l, norm_mix_l, b_glu_l):
    Tc = xT_list[0].shape[1]
    nc = get_nc("A", Tc)
    gm = lay_vec(norm_mix_l)
    bg = lay_vec(b_glu_l)
    w = np.ascontiguousarray(w_in_l, dtype=np.float32)
    ins = [{"xT": np.ascontiguousarray(xT), "w_in": w, "gmix": gm, "bglu": bg} for xT in xT_list]
    res = run_bass_kernel_spmd(nc, ins, core_ids=list(range(len(ins))))
    return [r["projT"] for r in res.results]


def run_C(final, xT_list, gluT_list, attnT_list, gaT_list, gbT_list, W):
    Tc = xT_list[0].shape[1]
    nc = get_nc("C", Tc, final)
    vecs = np.ascontiguousarray(np.stack([lay_vec(W[k]) for k in ("b_dw", "conv_ln_g", "conv_ln_b", "b_conv_out", "norm_ffn", "norm_final")], axis=1))
    common = {"wdw": lay_wdw(W["w_dw"]), "vecs": vecs}
    for k, n in (("w_co", "w_conv_out"), ("w_ao", "w_attn_out"), ("w_out", "w_out"), ("w_fg", "w_ffn_gate"), ("w_fu", "w_ffn_up"), ("w_fd", "w_ffn_down")):
        common[k] = np.ascontiguousarray(W[n], dtype=np.float32)
    ins = []
    for i in range(len(xT_list)):
        d = dict(common)
        d.update({"xT": np.ascontiguousarray(xT_list[i]), "gluT": np.ascontiguousarray(gluT_list[i]), "attnT": np.ascontiguousarray(attnT_list[i]),
                  "gaT": np.ascontiguousarray(gaT_list[i]), "gbT": np.ascontiguousarray(gbT_list[i])})
        ins.append(d)
    res = run_bass_kernel_spmd(nc, ins, core_ids=list(range(len(ins))))
    return [r["outT"] for r in res.results]


def emit_B(kb, T, projT, attnT, cw, cs):
    nc, P = kb.nc, kb.P
    NQT = T // 128
    NCMP = T // 16 - 1
    NCT = (NCMP + 1 + 127) // 128
    NCP = NCT * 128
    with contextlib.ExitStack() as outer:
        identf = kb.sb(outer, "identf", [128, 128], F32)
        identb = kb.sb(outer, "identb", [128, 128], BF16)
        eext = kb.sb(outer, "eext", [128, NQT, 128], BF16)
        b31 = kb.sb(outer, "b31", [128, NH], F32)
        b31h = kb.sb(outer, "b31h", [128, NH], BF16)
        b31hf = kb.sb(outer, "b31hf", [128, NH], F32)
        b31l = kb.sb(outer, "b31l", [128, NH], BF16)
        kcmpT = [kb.sb(outer, f"kcmpT{g}", [128, NCP], BF16) for g in range(NG)]
        vcaug = [kb.sb(outer, f"vcaug{g}", [128, NCT, 193], BF16) for g in range(NG)]
        kb.load("sync", identf[:, :], cs["identf"][:, :], "identf", "d_c0")
        kb.load("sync", identb[:, :], cs["identb"][:, :], "identb", "d_c1")
        kb.load("sync", eext[0:97, :, :], cs["eext"][:, :, :], "eext", "d_c2")
        kb.load("sync", b31[:, :], cs["b31"][:, :], "b31", "d_c3")
        for g in range(NG):
            kb.load("sync", vcaug[g][:, :, 128:193], cs["vcc"].rearrange("c n k -> n c k"), ("vcaug", g), f"d_sres{g}")
        P.op("vector", lambda e: e.tensor_copy(out=b31h[:, :], in_=b31[:, :]), reads=["b31"], writes=["b31h"])
        P.op("vector", lambda e: e.tensor_copy(out=b31hf[:, :], in_=b31h[:, :]), reads=["b31h"], writes=["b31hf"])
        P.op("vector", lambda e: e.tensor_tensor(out=b31l[:, :], in0=b31[:, :], in1=b31hf[:, :], op=ALU.subtract), reads=["b31", "b31hf"], writes=["b31l"])

        with contextlib.ExitStack() as ph:
            ps = kb.psum(ph)
            xin = [kb.sb(ph, f"xin{b}", [128, T], BF16) for b in range(2)]
            w1s = kb.sb(ph, "w1s", [128, 32, 128], F32)
            w1b = kb.sb(ph, "w1b", [128, 32, 128], BF16)
            w2s = kb.sb(ph, "w2s", [128, 128], F32)
            w2b = kb.sb(ph, "w2b", [128, 128], BF16)
            poss = kb.sb(ph, "poss", [128, 32], F32)
            posb = kb.sb(ph, "posb", [128, 32], BF16)
            pbias = kb.sb(ph, "pbias", [128, 1], F32)
            u = [kb.sb(ph, f"cu{i}", [128, NCP], F32) for i in range(4)]
            geT = kb.sb(ph, "geT", [128, NCP], BF16)
            P.op("vector", lambda e: e.memset(geT[:, :], 0.0), writes=["geT"])
            it = 0
            for kind in range(2):
                kb.load("sync", w1s[:, :, :], cw["w1"][kind].rearrange("(j d) o -> d j o", d=128), "w1s", "d_wst00")
                kb.load("sync", w2s[:, :], cw["w2"][kind], "w2s", "d_wst01")
                kb.load("sync", poss[:, :], cw["posT"][kind], "poss", "d_wst10")
                P.op("vector", lambda e: e.tensor_copy(out=w1b[:, :, :], in_=w1s[:, :, :]), reads=["w1s"], writes=["w1b"])
                P.op("vector", lambda e: e.tensor_copy(out=w2b[:, :], in_=w2s[:, :]), reads=["w2s"], writes=["w2b"])
                P.op("vector", lambda e: e.tensor_copy(out=posb[:, :], in_=poss[:, :]), reads=["poss"], writes=["posb"])
                for j in range(32):
                    P.op("tensor", (lambda j=j: lambda e: e.matmul(ps[6][:, 0:1], lhsT=w1b[:, j, :], rhs=posb[:, j:j + 1], start=(j == 0), stop=(j == 31)))(), reads=["w1b", "posb"], writes=[("ps", 6)])
                P.op("vector", lambda e: e.tensor_copy(out=pbias[:, :], in_=ps[6][:, 0:1]), reads=[("ps", 6)], writes=["pbias"])
                for g in range(NG):
                    b = it % 2
                    it += 1
                    r0 = C_KV + kind * 512 + g * 128
                    kb.load("gpsimd", xin[b][:, :], projT[r0:r0 + 128, :], ("xin", b), f"d_nxs{b}", reads=[("dram", "projT")])
                    x3 = xin[b][:, :].rearrange("p (n s) -> p n s", s=16)
                    for j in range(32):
                        P.op("tensor", (lambda j=j, x3=x3: lambda e: e.matmul(ps[0][:, 0:NCMP], lhsT=w1b[:, j, :], rhs=x3[:, j // 16:j // 16 + NCMP, j % 16], start=(j == 0), stop=(j == 31)))(),
                             reads=["w1b", ("xin", b)], writes=[("ps", 0)])
                    P.op("vector", lambda e: e.tensor_scalar(out=u[0][:, 0:NCMP], in0=ps[0][:, 0:NCMP], scalar1=pbias[:, 0:1], scalar2=None, op0=ALU.add), reads=[("ps", 0), "pbias"], writes=[("cu", 0)])
                    P.op("vector", lambda e: e.tensor_tensor(out=u[1][:, 0:NCMP], in0=u[0][:, 0:NCMP], in1=u[0][:, 0:NCMP], op=ALU.mult), reads=[("cu", 0)], writes=[("cu", 1)])
                    P.op("vector", lambda e: e.tensor_scalar(out=u[1][:, 0:NCMP], in0=u[1][:, 0:NCMP], scalar1=0.044715, scalar2=1.0, op0=ALU.mult, op1=ALU.add), reads=[("cu", 1)], writes=[("cu", 1)])
                    P.op("vector", lambda e: e.tensor_tensor(out=u[2][:, 0:NCMP], in0=u[1][:, 0:NCMP], in1=u[0][:, 0:NCMP], op=ALU.mult), reads=[("cu", 1), ("cu", 0)], writes=[("cu", 2)])
                    P.op("scalar", lambda e: e.activation(out=u[3][:, 0:NCMP], in_=u[2][:, 0:NCMP], func=AF.Sigmoid, scale=1.5957691216057308), reads=[("cu", 2)], writes=[("cu", 3)])
                    P.op("vector", lambda e: e.tensor_tensor(out=geT[:, 0:NCMP], in0=u[0][:, 0:NCMP], in1=u[3][:, 0:NCMP], op=ALU.mult), reads=[("cu", 0), ("cu", 3)], writes=["geT"])
                    if kind == 0:
                        P.op("tensor", lambda e: e.matmul(ps[1][:, 0:NCP], lhsT=w2b[:, :], rhs=geT[:, :], start=True, stop=True), reads=["w2b", "geT"], writes=[("ps", 1)])
                        P.op("scalar", (lambda g=g: lambda e: e.activation(out=kcmpT[g][:, :], in_=ps[1][:, 0:NCP], func=AF.Copy))(), reads=[("ps", 1)], writes=[("kcmpT", g)])
                    else:
                        for ct in range(NCT):
                            P.op("tensor", (lambda ct=ct: lambda e: e.matmul(ps[2 + ct][:, 0:128], lhsT=geT[:, ct * 128:(ct + 1) * 128], rhs=w2b[:, :], start=True, stop=True))(), reads=["w2b", "geT"], writes=[("ps", 2 + ct)])
                            P.op("scalar", (lambda g=g, ct=ct: lambda e: e.activation(out=vcaug[g][:, ct, 0:128], in_=ps[2 + ct][:, 0:128], func=AF.Copy))(), reads=[("ps", 2 + ct)], writes=[("vcaug", g)])
            P.end_phase()

        for g in range(NG):
            with contextlib.ExitStack() as gph:
                qT = kb.sb(gph, "qT", [128, 8, T], BF16)
                ksT = kb.sb(gph, "ksT", [128, T], BF16)
                kwT = kb.sb(gph, "kwT", [128, T], BF16)
                Vs = kb.sb(gph, "Vs", [128, NQT, 129], BF16)
                Vw = kb.sb(gph, "Vw", [128, NQT, 129], BF16)
                brg = kb.sb(gph, "brg", [128, NQT, 24], F32)
                Bn = [kb.sb(gph, f"Bn{i}", [128, 8, 128], F32) for i in range(3)]
                rbuf = kb.sb(gph, "rbuf", [128, 8, 128], BF16)
                with contextlib.ExitStack() as ph:
                    vT = [kb.sb(ph, f"vT{i}", [128, T], BF16) for i in range(2)]
                    gT = kb.sb(ph, "gT", [128, T], BF16)
                    kb.uid += 1
                    self_ps = ph.enter_context(nc.psum_tensor(f"psb_{g}_{kb.uid}", [128, 8, 128], BF16))
                    kb.load("sync", qT[:, :, :], projT[C_Q + g * 1024:C_Q + (g + 1) * 1024, :].rearrange("(h d) t -> d h t", d=128), "qT", "d_c0", reads=[("dram", "projT")])
                    kb.load("sync", ksT[:, :], projT[C_KV + 1024 + g * 128:C_KV + 1024 + (g + 1) * 128, :], "ksT", "d_c1", reads=[("dram", "projT")])
                    kb.load("sync", kwT[:, :], projT[C_KV + 2048 + g * 128:C_KV + 2048 + (g + 1) * 128, :], "kwT", "d_c2", reads=[("dram", "projT")])
                    kb.load("gpsimd", vT[0][:, :], projT[C_KV + 1536 + g * 128:C_KV + 1536 + (g + 1) * 128, :], ("vT", 0), "d_nxs0", reads=[("dram", "projT")])
                    kb.load("gpsimd", vT[1][:, :], projT[C_KV + 2560 + g * 128:C_KV + 2560 + (g + 1) * 128, :], ("vT", 1), "d_nxs1", reads=[("dram", "projT")])
                    kb.load("gpsimd", gT[0:24, :], projT[C_BR + g * 24:C_BR + (g + 1) * 24, :], "gT", "d_gT", reads=[("dram", "projT")])
                    for i, nm in enumerate(("B0", "B1", "Bw4")):
                        kb.load("sync", Bn[i][:, :, :], cs[nm][:, g * 8:(g + 1) * 8, :], ("Bn", i), f"d_sres{i}")
                    P.op("vector", lambda e: e.memset(rbuf[:, :, :], 0.0), writes=["rbuf"])
                    P.op("vector", lambda e: e.memset(Vs[:, :, 128:129], 1.0), writes=["Vs1"])
                    P.op("vector", lambda e: e.memset(Vw[:, :, 128:129], 1.0), writes=["Vw1"])
                    P.op("vector", (lambda g=g: lambda e: e.tensor_copy(out=rbuf[64:65, :, :], in_=b31h[64:65, g * 8:(g + 1) * 8].unsqueeze(2).broadcast_to([1, 8, 128])))(), reads=["b31h", "rbuf"], writes=["rbuf"])
                    P.op("vector", (lambda g=g: lambda e: e.tensor_copy(out=rbuf[96:97, :, :], in_=b31l[96:97, g * 8:(g + 1) * 8].unsqueeze(2).broadcast_to([1, 8, 128])))(), reads=["b31l", "rbuf"], writes=["rbuf"])
                    for vi, Vt in enumerate((Vs, Vw)):
                        for k0 in range(0, NQT, 8):
                            for kk in range(8):
                                kt = k0 + kk
                                P.op("tensor", (lambda vi=vi, kt=kt, kk=kk: lambda e: e.transpose(out=self_ps[:, kk, :], in_=vT[vi][:, kt * 128:(kt + 1) * 128], identity=identb[:, :]))(),
                                     reads=[("vT", vi), "identb"], writes=["psb"])
                            P.op("scalar", (lambda Vt=Vt, k0=k0: lambda e: e.activation(out=Vt[:, k0:k0 + 8, 0:128], in_=self_ps[:, :, :], func=AF.Copy))(), reads=["psb"], writes=[("V", vi)])
                    for k0 in range(0, NQT, 8):
                        for kk in range(8):
                            kt = k0 + kk
                            P.op("tensor", (lambda kt=kt, kk=kk: lambda e: e.transpose(out=self_ps[:, kk, 0:24], in_=gT[0:24, kt * 128:(kt + 1) * 128], identity=identb[0:24, 0:24]))(),
                                 reads=["gT", "identb"], writes=["psb"])
                        P.op("vector", (lambda k0=k0: lambda e: e.tensor_copy(out=brg[:, k0:k0 + 8, :], in_=self_ps[:, :, 0:24]))(), reads=["psb"], writes=["brg"])
                    P.end_phase()

                with contextlib.ExitStack() as ph:
                    ps = kb.psum(ph)
                    bc = [[kb.sb(ph, f"bc{b}{ct}", [128, 8, 128], F32) for ct in range(NCT)] for b in range(2)]
                    mvfb = [kb.sb(ph, f"mvfb{b}", [128, 2, 64], F32) for b in range(2)]
                    ssb = [kb.sb(ph, f"ssb{i}", [128, 512], F32) for i in range(2)]
                    pts = [kb.sb(ph, f"pt{i}", [128, 512], BF16) for i in range(4)]
                    ocmp = kb.sb(ph, "ocmp", [128, 8, 128], F32)
                    otok = kb.sb(ph, "otok", [128, 8, 128], F32)
                    sm = kb.sb(ph, "sm", [128, 8, 8], F32)
                    imp = kb.sb(ph, "imp", [128, 64], F32)
                    imp2 = kb.sb(ph, "imp2", [128, 64], F32)
                    m8 = kb.sb(ph, "m8", [128, 16], F32)
                    selb = kb.sb(ph, "selb", [128, 64], F32)
                    ost = [kb.sb(ph, f"ost{i}", [128, 4, 128], BF16) for i in range(2)]
                    cnts = {"s": 0, "p": 0, "o": 0}

                    def qk(lhsT_ap, lkey, quad, t0, extra, near_bias):
                        si = cnts["s"] % 2
                        cnts["s"] += 1
                        sp, spk = ps[si], ("ps", si)
                        rhs = qT[:, quad * 4:(quad + 1) * 4, t0:t0 + 128]
                        P.op("tensor", lambda e: e.matmul(sp[:, :].rearrange("p (a b) -> p a b", a=4), lhsT=lhsT_ap, rhs=rhs, start=True, stop=(extra is None)), reads=[lkey, "qT"], writes=[spk])
                        if extra is not None:
                            el, er, ekeys = extra
                            P.op("tensor", lambda e: e.matmul(sp[:, :].rearrange("p (a b) -> p a b", a=4), lhsT=el, rhs=er, start=False, stop=True), reads=ekeys, writes=[spk])
                        pi = cnts["p"] % 4
                        cnts["p"] += 1
                        pt, ptk = pts[pi], ("pt", pi)
                        if near_bias is not None:
                            btile, bkey = near_bias
                            sb_, sbk = ssb[si], ("ssb", si)
                            P.op("vector", lambda e: e.tensor_tensor(out=sb_[:, :], in0=sp[:, :], in1=btile, op=ALU.add), reads=[spk, bkey], writes=[sbk])
                            P.op("scalar", lambda e: e.activation(out=pt[:, :], in_=sb_[:, :], func=AF.Exp), reads=[sbk], writes=[ptk])
                        else:
                            P.op("scalar", lambda e: e.activation(out=pt[:, :], in_=sp[:, :], func=AF.Exp), reads=[spk], writes=[ptk])
                        return pt, ptk

                    def pv(pt, ptk, obase, width, rhs_ap, rkey, first, last):
                        for hq in range(4):
                            bi = obase + hq // 2
                            c0 = (hq % 2) * 256
                            P.op("tensor", (lambda hq=hq, bi=bi, c0=c0: lambda e: e.matmul(ps[bi][:, c0:c0 + width], lhsT=pt[:, hq * 128:(hq + 1) * 128], rhs=rhs_ap, start=(first and hq % 2 == 0), stop=last))(),
                                 reads=[ptk, rkey], writes=[("ps", bi, hq % 2)])

                    def _do_qt(qt):
                        t0 = qt * 128
                        bb = qt % 2
                        nct = min(NCT, (8 * qt + 6) // 128 + 1)
                        for ct in range(nct):
                            kb.load("gpsimd", bc[bb][ct][:, :, :], cs["bcmp"][qt, ct, :, g * 8:(g + 1) * 8, :], ("bc", bb, ct), f"d_bc{bb}{ct}")
                        kb.load("gpsimd", mvfb[bb][:, :, :], cs["mvfb"][qt], ("mvfb", bb), f"d_mvfb{bb}")
                        for quad in range(2):
                            for ct in range(nct):
                                pt, ptk = qk(kcmpT[g][:, ct * 128:(ct + 1) * 128], ("kcmpT", g), quad, t0, None,
                                             (bc[bb][ct][:, quad * 4:(quad + 1) * 4, :].rearrange("p a b -> p (a b)"), ("bc", bb, ct)))
                                pv(pt, ptk, 2 + quad * 2, 193, vcaug[g][:, ct, :], ("vcaug", g), ct == 0, ct == nct - 1)
                            for hq in range(4):
                                hl = quad * 4 + hq
                                bi = 2 + quad * 2 + hq // 2
                                c0 = (hq % 2) * 256
                                ok = ("ps", bi, hq % 2)

                                def _cmp_epi(hl=hl, bi=bi, c0=c0, ok=ok):
                                    P.op("vector", lambda e: e.tensor_scalar(out=sm[:, hl, 0:1], in0=ps[bi][:, c0 + 192:c0 + 193], scalar1=1.0e-30, scalar2=None, op0=ALU.max), reads=[ok], writes=[("sm", hl)])
                                    P.op("vector", lambda e: e.reciprocal(out=sm[:, hl, 0:1], in_=sm[:, hl, 0:1]), reads=[("sm", hl)], writes=[("sm", hl)])
                                    P.op("vector", lambda e: e.tensor_tensor(out=sm[:, hl, 1:2], in0=sm[:, hl, 0:1], in1=brg[:, qt, hl * 3:hl * 3 + 1], op=ALU.mult), reads=[("sm", hl), "brg"], writes=[("sm", hl)])
                                    P.op("vector", lambda e: e.tensor_scalar(out=ocmp[:, hl, :], in0=ps[bi][:, c0:c0 + 128], scalar1=sm[:, hl, 1:2], scalar2=None, op0=ALU.mult), reads=[ok, ("sm", hl)], writes=[("ocmp", hl)])
                                    if hl == 0:
                                        P.op("vector", lambda e: e.tensor_scalar(out=imp[:, :], in0=ps[bi][:, c0 + 128:c0 + 192], scalar1=sm[:, hl, 0:1], scalar2=None, op0=ALU.mult), reads=[ok, ("sm", hl)], writes=["imp"])
                                    else:
                                        P.op("vector", lambda e: e.scalar_tensor_tensor(out=imp[:, :], in0=ps[bi][:, c0 + 128:c0 + 192], scalar=sm[:, hl, 0:1], in1=imp[:, :], op0=ALU.mult, op1=ALU.add), reads=[ok, ("sm", hl), "imp"], writes=["imp"])
                                _cmp_epi()
                        P.op("vector", lambda e: e.tensor_tensor(out=imp[:, :], in0=imp[:, :], in1=mvfb[bb][:, 0, :], op=ALU.mult), reads=["imp", ("mvfb", bb)], writes=["imp"])
                        P.op("vector", lambda e: e.tensor_tensor(out=imp[:, :], in0=imp[:, :], in1=mvfb[bb][:, 1, :], op=ALU.add), reads=["imp", ("mvfb", bb)], writes=["imp"])
                        P.op("vector", lambda e: e.max(out=m8[:, 0:8], in_=imp[:, :]), reads=["imp"], writes=["m8a"])
                        P.op("vector", lambda e: e.match_replace(out=imp2[:, :], in_to_replace=m8[:, 0:8], in_values=imp[:, :], imm_value=-3.0e9), reads=["imp", "m8a"], writes=["imp2"])
                        P.op("vector", lambda e: e.max(out=m8[:, 8:16], in_=imp2[:, :]), reads=["imp2"], writes=["m8b"])
                        P.op("vector", lambda e: e.tensor_scalar(out=selb[:, :], in0=imp[:, :], scalar1=m8[:, 15:16], scalar2=NEG, op0=ALU.is_lt, op1=ALU.mult), reads=["imp", "m8b"], writes=["selb"])
                        P.op("tensor", lambda e: e.transpose(out=ps[6][0:64, 0:128], in_=selb[:, :], identity=identf[:, :]), reads=["selb", "identf"], writes=[("ps", 6)])
                        P.op("vector", lambda e: e.tensor_copy(out=rbuf[0:64, :, :], in_=ps[6][0:64, 0:128].unsqueeze(1).broadcast_to([64, 8, 128])), reads=[("ps", 6)], writes=["rbuf"])
                        for quad in range(2):
                            rq = rbuf[:, quad * 4:(quad + 1) * 4, :]
                            for o in range(qt + 1):
                                kt = qt - o
                                if o == 0:
                                    extra, nb = None, (Bn[0][:, quad * 4:(quad + 1) * 4, :].rearrange("p a b -> p (a b)"), ("Bn", 0))
                                elif o == 1:
                                    extra, nb = (eext[0:64, kt, :], rq[0:64], ["eext", "rbuf"]), (Bn[1][:, quad * 4:(quad + 1) * 4, :].rearrange("p a b -> p (a b)"), ("Bn", 1))
                                else:
                                    extra, nb = (eext[0:97, kt, :], rq[0:97], ["eext", "rbuf"]), None
                                pt, ptk = qk(ksT[:, kt * 128:(kt + 1) * 128], "ksT", quad, t0, extra, nb)
                                pv(pt, ptk, 2, 129, Vs[:, kt, :], ("V", 0), o == 0, o == qt)
                            nw = min(4, qt)
                            for o in range(nw + 1):
                                kt = qt - o
                                if o == 0:
                                    extra, nb = None, (Bn[0][:, quad * 4:(quad + 1) * 4, :].rearrange("p a b -> p (a b)"), ("Bn", 0))
                                elif o == 1:
                                    extra, nb = None, (Bn[1][:, quad * 4:(quad + 1) * 4, :].rearrange("p a b -> p (a b)"), ("Bn", 1))
                                elif o == 4:
                                    extra, nb = None, (Bn[2][:, quad * 4:(quad + 1) * 4, :].rearrange("p a b -> p (a b)"), ("Bn", 2))
                                else:
                                    extra, nb = (eext[64:97, kt, :], rq[64:97], ["eext", "rbuf"]), None
                                pt, ptk = qk(kwT[:, kt * 128:(kt + 1) * 128], "kwT", quad, t0, extra, nb)
                                pv(pt, ptk, 4, 129, Vw[:, kt, :], ("V", 1), o == 0, o == nw)
                            oi = cnts["o"] % 2
                            cnts["o"] += 1
                            for hq in range(4):
                                hl = quad * 4 + hq

                                def _epi(hl=hl, hq=hq):
                                    c0 = (hq % 2) * 256
                                    bs, bw = 2 + hq // 2, 4 + hq // 2
                                    ks_, kw_ = ("ps", bs, hq % 2), ("ps", bw, hq % 2)
                                    P.op("vector", lambda e: e.reciprocal(out=sm[:, hl, 2:3], in_=ps[bs][:, c0 + 128:c0 + 129]), reads=[ks_], writes=[("sm2", hl)])
                                    P.op("vector", lambda e: e.tensor_tensor(out=sm[:, hl, 3:4], in0=sm[:, hl, 2:3], in1=brg[:, qt, hl * 3 + 1:hl * 3 + 2], op=ALU.mult), reads=[("sm2", hl), "brg"], writes=[("sm2", hl)])
                                    P.op("vector", lambda e: e.reciprocal(out=sm[:, hl, 4:5], in_=ps[bw][:, c0 + 128:c0 + 129]), reads=[kw_], writes=[("sm3", hl)])
                                    P.op("vector", lambda e: e.tensor_tensor(out=sm[:, hl, 5:6], in0=sm[:, hl, 4:5], in1=brg[:, qt, hl * 3 + 2:hl * 3 + 3], op=ALU.mult), reads=[("sm3", hl), "brg"], writes=[("sm3", hl)])
                                    P.op("vector", lambda e: e.scalar_tensor_tensor(out=otok[:, hl, :], in0=ps[bs][:, c0:c0 + 128], scalar=sm[:, hl, 3:4], in1=ocmp[:, hl, :], op0=ALU.mult, op1=ALU.add), reads=[ks_, ("sm2", hl), ("ocmp", hl)], writes=[("otok", hl)])
                                    P.op("vector", lambda e: e.scalar_tensor_tensor(out=otok[:, hl, :], in0=ps[bw][:, c0:c0 + 128], scalar=sm[:, hl, 5:6], in1=otok[:, hl, :], op0=ALU.mult, op1=ALU.add), reads=[kw_, ("sm3", hl), ("otok", hl)], writes=[("otok", hl)])
                                    P.op("tensor", lambda e: e.transpose(out=ps[7][:, hq * 128:(hq + 1) * 128], in_=otok[:, hl, :], identity=identf[:, :]), reads=[("otok", hl), "identf"], writes=[("ps", 7)])
                                _epi()
                            P.op("scalar", (lambda oi=oi: lambda e: e.activation(out=ost[oi][:, :, :].rearrange("p a b -> p (a b)"), in_=ps[7][:, :], func=AF.Copy))(), reads=[("ps", 7)], writes=[("ost", oi)])
                            r0 = g * 1024 + quad * 512
                            kb.store("sync", attnT[r0:r0 + 512, t0:t0 + 128].rearrange("(h d) q -> d h q", d=128), ost[oi][:, :, :], ("ost", oi), f"d_ost{oi}", ("dram", "attnT"))
                    for qt in range(NQT):
                        _do_qt(qt)
                    P.end_phase()


def t5_bucket_np(d):
    n = np.maximum(d, 0)
    nf = np.maximum(n, 16).astype(np.float32)
    large = 16 + (np.log(nf / np.float32(16)) / np.float32(np.log(128 / 16)) * np.float32(16)).astype(np.int32)
    large = np.minimum(large, 31)
    return np.where(n < 16, n, large)


def make_consts(T, rel_bias):
    bf = ml_dtypes.bfloat16
    NQT = T // 128
    NCMP = T // 16 - 1
    NCT = (NCMP + 1 + 127) // 128
    nsb = T // 64
    tab = np.concatenate([np.asarray(rel_bias, np.float32), np.full((1, NH), NEG, np.float32)], axis=0)
    cs = {}
    cs["identf"] = np.eye(128, dtype=np.float32)
    cs["identb"] = np.eye(128, dtype=np.float32).astype(bf)
    ee = np.zeros((97, NQT, 128), np.float32)
    for kt in range(NQT):
        if 2 * kt < 64:
            ee[2 * kt, kt, 0:64] = 1.0
        if 2 * kt + 1 < 64:
            ee[2 * kt + 1, kt, 64:128] = 1.0
    ee[64] = 1.0
    ee[96] = 1.0
    cs["eext"] = ee.astype(bf)
    cs["b31"] = np.ascontiguousarray(np.broadcast_to(tab[31][None, :], (128, NH)))
    c_start = np.arange(NCMP) * 16
    s_start = np.arange(nsb) * 64
    ov = np.minimum(c_start[:, None] + 32, s_start[None, :] + 64) - np.maximum(c_start[:, None], s_start[None, :])
    ov = np.clip(ov, 0, None) // 16
    vcc = np.zeros((NCT * 128, 65), np.float32)
    vcc[:NCMP, :nsb] = ov
    vcc[:, 64] = 1.0
    cs["vcc"] = vcc.reshape(NCT, 128, 65).astype(bf)
    kk = np.arange(128)[:, None]
    qq = np.arange(128)[None, :]
    def near(off, extra_mask=None):
        d = qq - kk + off
        idx = np.where(d >= 0, t5_bucket_np(d), 32)
        if extra_mask is not None:
            idx = np.where(extra_mask, idx, 32)
        return np.ascontiguousarray(tab[idx].transpose(0, 2, 1))
    cs["B0"] = near(0)
    cs["B1"] = near(128)
    cs["Bw4"] = near(512, extra_mask=(qq - kk + 512 < 512))
    i_all = np.arange(NCT * 128)
    t_all = np.arange(T)
    d = t_all[None, :] - (16 * i_all[:, None] + 31)
    idx = np.where((d >= 0) & (i_all[:, None] < NCMP), t5_bucket_np(d), 32)
    bc = tab[idx]
    bc = bc.reshape(NCT, 128, NQT, 128, NH).transpose(2, 0, 1, 4, 3)
    cs["bcmp"] = np.ascontiguousarray(bc)
    cur = t_all // 64
    blk = np.arange(64)[None, :]
    forced = (blk == 0) | (blk == cur[:, None]) | (blk == cur[:, None] - 1)
    valid = (blk <= cur[:, None]) & (blk < nsb)
    mv = (valid & ~forced).astype(np.float32)
    fb = np.where(~valid, -1.0e9, np.where(forced, 1.0e9, 0.0)).astype(np.float32)
    cs["mvfb"] = np.ascontiguousarray(np.stack([mv, fb], axis=1).reshape(NQT, 128, 2, 64))
    return cs


CONST_SPECS = lambda T: {
    "identf": ([128, 128], F32), "identb": ([128, 128], BF16), "eext": ([97, T // 128, 128], BF16), "b31": ([128, NH], F32),
    "vcc": ([((T // 16) + 127) // 128, 128, 65], BF16), "B0": ([128, NH, 128], F32), "B1": ([128, NH, 128], F32), "Bw4": ([128, NH, 128], F32),
    "bcmp": ([T // 128, ((T // 16) + 127) // 128, 128, NH, 128], F32), "mvfb": ([T // 128, 128, 2, 64], F32)}


def build_fused(T, L):
    kb = KB()
    nc, P = kb.nc, kb.P
    Tb = min(2048, T)
    I = "ExternalInput"
    xT = kb.dram("xT", [D, T], F32, I)
    w = {}
    for nm, shp in (("w_in", [L, D, INC]), ("gmix", [L, 128, KC]), ("bglu", [L, 128, 64]), ("posT", [L, 2, 128, 32]), ("w1", [L, 2, D, 128]), ("w2", [L, 2, 128, 128]),
                    ("wdw", [L, 128, KC, 31]), ("vecs", [L, 128, 6, KC]), ("w_co", [L, D, D]), ("w_ao", [L, D, D]), ("w_out", [L, D, D]),
                    ("w_fg", [L, D, DFF]), ("w_fu", [L, D, DFF]), ("w_fd", [L, DFF, D])):
        w[nm] = kb.dram(nm, shp, F32, I)
    cs = {k: kb.dram("c_" + k, shp, dt, I) for k, (shp, dt) in CONST_SPECS(T).items()}
    outT = kb.dram("outT", [D, T], F32, "ExternalOutput")
    N = "Internal"
    DBG = "ExternalOutput" if DEBUG else N
    projT = kb.dram("projT", [PROJ_ROWS, 32 + T], BF16, DBG)
    attnT = kb.dram("attnT", [D, T], BF16, DBG)
    xbuf = [kb.dram(f"xbuf{i}", [D, T], F32, N) for i in range(2)]
    scr = {"zT": kb.dram("zT", [D, Tb], BF16, DBG), "mbT": kb.dram("mbT", [D, Tb], F32, N), "mT": kb.dram("mT", [D, Tb], BF16, N),
           "x1T": kb.dram("x1T", [D, Tb], F32, DBG), "actT": kb.dram("actT", [DFF, Tb], BF16, N), "x2T": kb.dram("x2T", [D, Tb], F32, N)}
    with kb.es:
        with contextlib.ExitStack() as ph:
            z = kb.sb(ph, "zpad", [128, KC, 32], BF16)
            P.op("vector", lambda e: e.memset(z[:, :, :], 0.0), writes=["zpad"])
            kb.store("sync", projT[C_A:C_A + D, 0:32].rearrange("(c p) t -> p c t", p=128), z[:, :, :], "zpad", "d_c0", ("dram", "projT"))
            P.end_phase()
        src = xT
        for l in range(L):
            for t0 in range(0, T, Tb):
                emit_A(kb, Tb, src[:, t0:t0 + Tb], w["w_in"][l], w["gmix"][l], w["bglu"][l], projT[:, 32 + t0:32 + t0 + Tb])
            emit_B(kb, T, projT[:, 32:32 + T], attnT, {"w1": w["w1"][l], "w2": w["w2"][l], "posT": w["posT"][l]}, cs)
            final = (l == L - 1)
            dst = outT if final else xbuf[l % 2]
            for t0 in range(0, T, Tb):
                t = dict(scr)
                t.update({"xT": src[:, t0:t0 + Tb], "gluT": projT[C_A:C_A + D, t0:t0 + Tb + 32], "attnT": attnT[:, t0:t0 + Tb],
                          "gaT": projT[C_GA:C_GA + D, 32 + t0:32 + t0 + Tb], "gbT": projT[C_GB:C_GB + D, 32 + t0:32 + t0 + Tb],
                          "wdw": w["wdw"][l], "vecs": w["vecs"][l], "w_co": w["w_co"][l], "w_ao": w["w_ao"][l], "w_out": w["w_out"][l],
                          "w_fg": w["w_fg"][l], "w_fu": w["w_fu"][l], "w_fd": w["w_fd"][l], "outT": dst[:, t0:t0 + Tb]})
                emit_C(kb, Tb, final, t)
            src = dst
    return nc


def host_inputs(T, L, inp, cs):
    f = lambda a: np.ascontiguousarray(np.asarray(a, dtype=np.float32))
    d = {}
    d["w_in"] = f(inp["w_in"][:L])
    d["gmix"] = np.stack([lay_vec(inp["norm_mix"][l]) for l in range(L)])
    d["bglu"] = np.stack([lay_vec(inp["b_glu"][l]) for l in range(L)])
    d["posT"] = f(np.asarray(inp["cmp_pos"])[:L].transpose(0, 1, 3, 2))
    d["w1"] = f(inp["cmp_w1"][:L])
    d["w2"] = f(inp["cmp_w2"][:L])
    d["wdw"] = np.stack([lay_wdw(inp["w_dw"][l]) for l in range(L)])
    d["vecs"] = np.stack([np.stack([lay_vec(np.asarray(inp[k])[l] if k != "norm_final" else np.asarray(inp[k])) for k in ("b_dw", "conv_ln_g", "conv_ln_b", "b_conv_out", "norm_ffn", "norm_final")], axis=1) for l in range(L)])
    for k, n in (("w_co", "w_conv_out"), ("w_ao", "w_attn_out"), ("w_out", "w_out"), ("w_fg", "w_ffn_gate"), ("w_fu", "w_ffn_up"), ("w_fd", "w_ffn_down")):
        d[k] = f(inp[n][:L])
    for k, v in cs.items():
        d["c_" + k] = np.ascontiguousarray(v)
    return d


_FUSED = {}
DEBUG = False
LAST = {}


def run_fused(inp, L=2):
    x = np.asarray(inp["x"], dtype=np.float32)
    B, T, _ = x.shape
    key = (T, L)
    if key not in _FUSED:
        _FUSED[key] = build_fused(T, L)
    nc = _FUSED[key]
    cs = make_consts(T, inp["rel_bias"])
    common = host_inputs(T, L, inp, cs)
    ins = []
    for b in range(B):
        dd = dict(common)
        dd["xT"] = np.ascontiguousarray(x[b].T)
        ins.append(dd)
    res = run_bass_kernel_spmd(nc, ins, core_ids=list(range(B)))
    if DEBUG:
        LAST.update(res.results[0])
        LAST["all"] = res.results
    out = np.stack([np.ascontiguousarray(r["outT"].T) for r in res.results], axis=0)
    return out.astype(np.float32)


def kernel(**inputs):
    return run_fused(inputs, L=2)
```

```python
import contextlib
import numpy as np
import ml_dtypes
import concourse.bass as bass
import concourse.mybir as mybir
from concourse.bass_utils import run_bass_kernel_spmd

F32 = mybir.dt.float32
BF16 = mybir.dt.bfloat16
AF = mybir.ActivationFunctionType
ALU = mybir.AluOpType
ENGS = ["sync", "scalar", "vector", "gpsimd", "tensor"]

D = 4096
NH = 32
DH = 128
NG = 4
DFF = 11008
INC = 23648
KC = 32
EPS = 1e-6
NEG = -30000.0
C_Q, C_KV, C_A, C_G, C_GA, C_GB, C_BR = 0, 4096, 7168, 11264, 15360, 19456, 23552
PROJ_ROWS = 23680


class Prog:
    def __init__(self, nc, es):
        self.nc = nc
        self.es = es
        self.sems = {}
        self.cnt = {}
        self.waited = {e: {} for e in ENGS}
        self.streams = {e: [] for e in ENGS}
        self.bufs = {}
        self.last_out = []

    def _need(self, eng, dep, kind):
        if dep is None:
            return
        sem, val = dep
        if sem == "c_" + eng and (eng == "tensor" or kind != "RAW"):
            return
        if self.waited[eng].get(sem, 0) >= val:
            return
        self.waited[eng][sem] = val
        self.streams[eng].append(("wait", sem, val))

    def op(self, eng, fn, reads=(), writes=(), dma=None):
        for k in reads:
            st = self.bufs.get(k)
            if st is not None:
                self._need(eng, st[0], "RAW")
        for k in writes:
            st = self.bufs.get(k)
            if st is not None:
                self._need(eng, st[0], "WAW")
                for r in st[1].items():
                    self._need(eng, r, "WAR")
        if dma is not None:
            sem, inc = dma, 16
        else:
            sem, inc = "c_" + eng, 1
        val = self.cnt.get(sem, 0) + inc
        self.cnt[sem] = val
        self.streams[eng].append(("op", fn, sem, inc))
        tok = (sem, val)
        for k in reads:
            st = self.bufs.setdefault(k, [None, {}])
            st[1][sem] = val
        for k in writes:
            self.bufs[k] = [tok, {}]
        return tok

    def end_phase(self):
        for e in ENGS:
            for s, v in self.cnt.items():
                if self.waited[e].get(s, 0) < v:
                    self.waited[e][s] = v
                    self.streams[e].append(("wait", s, v))
        for s in self.cnt:
            if s not in self.sems:
                self.sems[s] = self.es.enter_context(self.nc.semaphore(s))
        sems = self.sems
        streams = self.streams

        def run(name):
            def body(e):
                for it in streams[name]:
                    if it[0] == "wait":
                        e.wait_ge(sems[it[1]], it[2])
                    else:
                        it[1](e).then_inc(sems[it[2]], it[3])
            return body

        with self.nc.Block() as block:
            block.sync(run("sync"))
            block.scalar(run("scalar"))
            block.vector(run("vector"))
            block.gpsimd(run("gpsimd"))
            block.tensor(run("tensor"))
        self.streams = {e: [] for e in ENGS}
        self.bufs = {}


class Rot:
    def __init__(self, items):
        self.items = items
        self.i = 0

    def next(self):
        it = self.items[self.i % len(self.items)]
        self.i += 1
        return it


class KB:
    def __init__(self):
        self.nc = bass.Bass("TRN2", target_bir_lowering=False)
        self.es = contextlib.ExitStack()
        self.P = Prog(self.nc, self.es)
        self.uid = 0

    def dram(self, name, shape, dt, kind):
        return self.nc.dram_tensor(name, list(shape), dt, kind=kind).ap()

    def sb(self, ph, name, shape, dt):
        self.uid += 1
        return ph.enter_context(self.nc.sbuf_tensor(f"{name}_{self.uid}", list(shape), dt))

    def psum(self, ph, n=8):
        self.uid += 1
        return [ph.enter_context(self.nc.psum_tensor(f"ps{i}_{self.uid}", [128, 512], F32)) for i in range(n)]

    def load(self, q, out_ap, in_ap, key, sem, reads=()):
        return self.P.op(q, lambda e: e.dma_start(out=out_ap, in_=in_ap), reads=list(reads), writes=[key], dma=sem)

    def store(self, q, out_ap, in_ap, key_src, sem, wkey):
        return self.P.op(q, lambda e: e.dma_start(out=out_ap, in_=in_ap), reads=[key_src], writes=[wkey], dma=sem)

    def stage_pool(self, ph, name, n, dt, width=512):
        items = []
        for i in range(n):
            t = self.sb(ph, f"{name}{i}", [128, width], dt)
            items.append((t, (name, i), f"d_{name}{i}"))
        return Rot(items)

    def rmsnorm(self, ps, srcT, Tc, gain, ones_bf, out_fn, TB=1024):
        P = self.P
        with contextlib.ExitStack() as ph:
            xs = [self.sb(ph, f"nxs{b}", [128, TB], F32) for b in range(2)]
            sq = [self.sb(ph, f"nsq{b}", [128, TB], BF16) for b in range(2)]
            rstd = self.sb(ph, "nrstd", [128, TB], F32)
            ntg = TB // 512
            it = 0
            for t0 in range(0, Tc, TB):
                for c in range(KC):
                    b = it % 2
                    it += 1
                    self.load("gpsimd", xs[b][:, :], srcT[c * 128:(c + 1) * 128, t0:t0 + TB], ("nxs", b), f"d_nxs{b}", reads=[("dram", srcT.tensor.name)])
                    P.op("scalar", (lambda b=b: lambda e: e.activation(out=sq[b][:, :], in_=xs[b][:, :], func=AF.Square))(), reads=[("nxs", b)], writes=[("nsq", b)])
                    for tg in range(ntg):
                        P.op("tensor", (lambda b=b, tg=tg, c=c: lambda e: e.matmul(ps[tg][:, :], lhsT=ones_bf[0][:, :], rhs=sq[b][:, tg * 512:(tg + 1) * 512], start=(c == 0), stop=(c == KC - 1)))(),
                             reads=[("nsq", b), ones_bf[1]], writes=[("ps", tg)])
                for tg in range(ntg):
                    sl = slice(tg * 512, (tg + 1) * 512)
                    P.op("vector", (lambda tg=tg, sl=sl: lambda e: e.tensor_scalar(out=rstd[:, sl], in0=ps[tg][:, :], scalar1=1.0 / D, scalar2=EPS, op0=ALU.mult, op1=ALU.add))(), reads=[("ps", tg)], writes=[("nrstd", tg)])
                    P.op("scalar", (lambda sl=sl: lambda e: e.activation(out=rstd[:, sl], in_=rstd[:, sl], func=AF.Sqrt))(), reads=[("nrstd", tg)], writes=[("nrstd", tg)])
                    P.op("vector", (lambda sl=sl: lambda e: e.reciprocal(out=rstd[:, sl], in_=rstd[:, sl]))(), reads=[("nrstd", tg)], writes=[("nrstd", tg)])
                for c in range(KC):
                    b = it % 2
                    it += 1
                    self.load("gpsimd", xs[b][:, :], srcT[c * 128:(c + 1) * 128, t0:t0 + TB], ("nxs", b), f"d_nxs{b}", reads=[("dram", srcT.tensor.name)])
                    out_fn(c, t0, xs[b], ("nxs", b), rstd, [("nrstd", tg) for tg in range(ntg)], TB)
            P.end_phase()

    def wbufs(self, ph, SEG=16):
        wst = [self.sb(ph, f"wst{b}", [128, SEG, 128], F32) for b in range(2)]
        wbf = [self.sb(ph, f"wbf{b}", [128, SEG, 128], BF16) for b in range(2)]
        return (wst, wbf, SEG, [0])

    def proj(self, wb, ps, res, res_keyfn, KCn, TW, chunks):
        P = self.P
        NTG = TW // 512
        wst, wbf, SEG, sic = wb
        segs = []
        for ci, (W, c0, ncol, epi) in enumerate(chunks):
            pb = ci % 2
            banks = [(ps[pb * 4 + tg], ("ps", pb * 4 + tg)) for tg in range(NTG)]
            for s0 in range(0, KCn, SEG):
                ns = min(SEG, KCn - s0)
                segs.append((ci, W, c0, ncol, epi, banks, s0, ns, s0 + ns >= KCn))

        def prefetch(i):
            ci, W, c0, ncol, epi, banks, s0, ns, last = segs[i]
            b = (sic[0] + i) % 2
            src = W[s0 * 128:(s0 + ns) * 128, c0:c0 + ncol].rearrange("(k p) n -> p k n", p=128)
            h = SEG // 2
            halves = [(0, min(ns, h))] + ([(h, ns)] if ns > h else [])
            for hh, (k0, k1) in enumerate(halves):
                P.op("sync", (lambda b=b, k0=k0, k1=k1, src=src, ncol=ncol: lambda e: e.dma_start(out=wst[b][:, k0:k1, 0:ncol], in_=src[:, k0:k1, :]))(),
                     writes=[("wst", b, hh)], dma=f"d_wst{b}{hh}")
                if hh == 0:
                    P.op("vector", (lambda b=b, k0=k0, k1=k1, ncol=ncol: lambda e: e.tensor_copy(out=wbf[b][:, k0:k1, 0:ncol], in_=wst[b][:, k0:k1, 0:ncol]))(),
                         reads=[("wst", b, hh)], writes=[("wbf", b, hh)])
                else:
                    P.op("scalar", (lambda b=b, k0=k0, k1=k1, ncol=ncol: lambda e: e.activation(out=wbf[b][:, k0:k1, 0:ncol], in_=wst[b][:, k0:k1, 0:ncol], func=AF.Copy))(),
                         reads=[("wst", b, hh)], writes=[("wbf", b, hh)])

        prefetch(0)
        for i, (ci, W, c0, ncol, epi, banks, s0, ns, last) in enumerate(segs):
            if i + 1 < len(segs):
                prefetch(i + 1)
            b = (sic[0] + i) % 2
            h = SEG // 2
            for k in range(ns):
                hh = 0 if k < h else 1
                kk = s0 + k
                for tg in range(NTG):
                    P.op("tensor", (lambda b=b, k=k, kk=kk, tg=tg, ncol=ncol, banks=banks: lambda e: e.matmul(
                        banks[tg][0][0:ncol, :], lhsT=wbf[b][:, k, 0:ncol], rhs=res[:, kk, tg * 512:(tg + 1) * 512],
                        start=(kk == 0), stop=(kk == KCn - 1)))(),
                        reads=[("wbf", b, hh), res_keyfn(kk)], writes=[banks[tg][1]])
            if last:
                epi(ci, banks)
        sic[0] += len(segs)

    def load_res(self, res, srcT, TW, t0=0, nk=KC, q="sync", kname="res", grp=4):
        tok = None
        for c0 in range(0, nk, grp):
            c1 = min(nk, c0 + grp)
            src = srcT[c0 * 128:c1 * 128, t0:t0 + TW].rearrange("(f p) t -> p f t", p=128)
            keys = [(kname, c) for c in range(c0, c1)]
            tok = self.P.op(q, (lambda c0=c0, c1=c1, src=src: lambda e: e.dma_start(out=res[:, c0:c1, :], in_=src))(),
                            reads=[("dram", srcT.tensor.name)], writes=keys, dma="d_hres")
        for c in range(nk):
            self.P.bufs[(kname, c)][0] = tok


def build_A(Tc):
    kb = KB()
    nc, P = kb.nc, kb.P
    xT = kb.dram("xT", [D, Tc], F32, "ExternalInput")
    w_in = kb.dram("w_in", [D, INC], F32, "ExternalInput")
    gmix = kb.dram("gmix", [128, KC], F32, "ExternalInput")
    bglu = kb.dram("bglu", [128, 64], F32, "ExternalInput")
    projT = kb.dram("projT", [PROJ_ROWS, Tc], BF16, "ExternalOutput")
    with kb.es:
        emit_A(kb, Tc, xT, w_in, gmix, bglu, projT)
    return nc


def emit_A(kb, Tc, xT, w_in, gmix, bglu, projT):
    nc, P = kb.nc, kb.P
    NTG = Tc // 512
    with contextlib.ExitStack() as outer:
        res = kb.sb(outer, "res", [128, KC, Tc], BF16)
        ones = kb.sb(outer, "ones", [128, 128], BF16)
        gm = kb.sb(outer, "gm", [128, KC], F32)
        bg = kb.sb(outer, "bg", [128, 64], F32)
        ps = kb.psum(outer)
        P.op("vector", lambda e: e.memset(ones[:, :], 1.0), writes=["ones"])
        kb.load("sync", gm[:, :], gmix[:, :], "gm", "d_c0")
        kb.load("sync", bg[:, :], bglu[:, :], "bg", "d_c1")

        def norm_out(c, t0, xs, xk, rstd, rks, TB):
            P.op("vector", lambda e: e.scalar_tensor_tensor(out=res[:, c, t0:t0 + TB], in0=xs[:, :], scalar=gm[:, c:c + 1], in1=rstd[:, :], op0=ALU.mult, op1=ALU.mult),
                 reads=[xk, "gm"] + rks, writes=[("res", c)])
        kb.rmsnorm(ps, xT, Tc, gm, (ones, "ones"), norm_out, TB=min(1024, Tc))

        with contextlib.ExitStack() as ph:
            stb = kb.stage_pool(ph, "stb", 4, BF16)
            stf = kb.stage_pool(ph, "stf", 2, F32)
            ast = [kb.sb(ph, f"ast{tg}", [128, 512], F32) for tg in range(NTG)]
            flip = [0]

            def mk_epi(kind, row0, bi=None):
                def epi(ci, banks, n=128):
                    def _one(tg, pt, pk):
                        dst = projT[row0:row0 + n, tg * 512:(tg + 1) * 512]
                        if kind == "a":
                            P.op("vector", lambda e: e.tensor_scalar(out=ast[tg][:, :], in0=pt[:, :], scalar1=bg[:, bi:bi + 1], scalar2=None, op0=ALU.add),
                                 reads=[pk, "bg"], writes=[("ast", tg)])
                            return
                        st, sk, ssem = stb.next()
                        if kind == "g":
                            sf, fk, _ = stf.next()
                            P.op("scalar", lambda e: e.activation(out=sf[:, :], in_=pt[:, :], func=AF.Sigmoid, bias=bg[:, 32 + bi:33 + bi], scale=1.0),
                                 reads=[pk, "bg"], writes=[fk])
                            P.op("gpsimd", lambda e: e.tensor_tensor(out=st[:, :], in0=ast[tg][:, :], in1=sf[:, :], op=ALU.mult),
                                 reads=[fk, ("ast", tg)], writes=[sk])
                        elif kind == "sig":
                            P.op("scalar", lambda e: e.activation(out=st[0:n, :], in_=pt[0:n, :], func=AF.Sigmoid), reads=[pk], writes=[sk])
                        else:
                            sc = DH ** -0.5 if kind == "q" else 1.0
                            flip[0] ^= 1
                            if flip[0]:
                                P.op("scalar", lambda e: e.activation(out=st[:, :], in_=pt[:, :], func=AF.Copy, scale=sc), reads=[pk], writes=[sk])
                            else:
                                P.op("vector", lambda e: e.tensor_scalar(out=st[:, :], in0=pt[:, :], scalar1=sc, scalar2=None, op0=ALU.mult), reads=[pk], writes=[sk])
                        kb.store("gpsimd", dst, st[0:n, :], sk, ssem, ("dram", "projT"))
                    for tg, (pt, pk) in enumerate(banks):
                        _one(tg, pt, pk)
                return epi

            chunks = []
            for i in range(32):
                chunks.append((w_in, C_Q + i * 128, 128, mk_epi("q", C_Q + i * 128)))
            for i in range(24):
                chunks.append((w_in, C_KV + i * 128, 128, mk_epi("copy", C_KV + i * 128)))
            for i in range(32):
                chunks.append((w_in, C_A + i * 128, 128, mk_epi("a", 0, i)))
                chunks.append((w_in, C_G + i * 128, 128, mk_epi("g", C_A + i * 128, i)))
            for i in range(32):
                chunks.append((w_in, C_GA + i * 128, 128, mk_epi("sig", C_GA + i * 128)))
            for i in range(32):
                chunks.append((w_in, C_GB + i * 128, 128, mk_epi("sig", C_GB + i * 128)))
            e96 = mk_epi("sig", C_BR)
            chunks.append((w_in, C_BR, 96, lambda ci, banks: e96(ci, banks, n=96)))
            kb.proj(kb.wbufs(ph), ps, res, lambda kk: ("res", kk), KC, Tc, chunks)
            P.end_phase()


def build_C(Tc, final):
    kb = KB()
    nc, P = kb.nc, kb.P
    t = {}
    t["xT"] = kb.dram("xT", [D, Tc], F32, "ExternalInput")
    t["gluT"] = kb.dram("gluT", [D, Tc + 32], BF16, "ExternalInput")
    t["attnT"] = kb.dram("attnT", [D, Tc], BF16, "ExternalInput")
    t["gaT"] = kb.dram("gaT", [D, Tc], BF16, "ExternalInput")
    t["gbT"] = kb.dram("gbT", [D, Tc], BF16, "ExternalInput")
    t["wdw"] = kb.dram("wdw", [128, KC, 31], F32, "ExternalInput")
    t["vecs"] = kb.dram("vecs", [128, 6, KC], F32, "ExternalInput")
    t["w_co"] = kb.dram("w_co", [D, D], F32, "ExternalInput")
    t["w_ao"] = kb.dram("w_ao", [D, D], F32, "ExternalInput")
    t["w_out"] = kb.dram("w_out", [D, D], F32, "ExternalInput")
    t["w_fg"] = kb.dram("w_fg", [D, DFF], F32, "ExternalInput")
    t["w_fu"] = kb.dram("w_fu", [D, DFF], F32, "ExternalInput")
    t["w_fd"] = kb.dram("w_fd", [DFF, D], F32, "ExternalInput")
    t["outT"] = kb.dram("outT", [D, Tc], F32, "ExternalOutput")
    t["zT"] = kb.dram("zT", [D, Tc], BF16, "Internal")
    t["mbT"] = kb.dram("mbT", [D, Tc], F32, "Internal")
    t["mT"] = kb.dram("mT", [D, Tc], BF16, "Internal")
    t["x1T"] = kb.dram("x1T", [D, Tc], F32, "Internal")
    t["actT"] = kb.dram("actT", [DFF, Tc], BF16, "Internal")
    if final:
        t["x2T"] = kb.dram("x2T", [D, Tc], F32, "Internal")
    with kb.es:
        emit_C(kb, Tc, final, t)
    return nc


def emit_C(kb, Tc, final, t):
    nc, P = kb.nc, kb.P
    NTG = Tc // 512
    NF = DFF // 128
    with contextlib.ExitStack() as outer:
        ones = kb.sb(outer, "ones", [128, 128], BF16)
        vec = kb.sb(outer, "vec", [128, 6, KC], F32)
        wdw = kb.sb(outer, "wdw", [128, KC, 31], F32)
        ps = kb.psum(outer)
        P.op("vector", lambda e: e.memset(ones[:, :], 1.0), writes=["ones"])
        kb.load("sync", vec[:, :, :], t["vecs"][:, :, :], "vec", "d_c0")
        kb.load("sync", wdw[:, :, :], t["wdw"][:, :, :], "wdw", "d_c1")
        P.end_phase()

        with contextlib.ExitStack() as ph:
            acc = kb.sb(ph, "cacc", [128, KC, 512], F32)
            gl = [kb.sb(ph, f"cgl{b}", [128, 544], BF16) for b in range(4)]
            cb = [kb.sb(ph, f"ccb{b}", [128, 512], BF16) for b in range(2)]
            cq = [kb.sb(ph, f"ccq{b}", [128, 512], BF16) for b in range(2)]
            mean = kb.sb(ph, "cmean", [128, 512], F32)
            rstd = kb.sb(ph, "crstd", [128, 512], F32)
            tmp = [kb.sb(ph, f"ctmp{b}", [128, 512], F32) for b in range(2)]
            zst = kb.stage_pool(ph, "zst", 2, BF16)
            for tg in range(NTG):
                for c4 in range(0, KC, 4):
                    for j in range(4):
                        c = c4 + j
                        kb.load("sync", gl[j][:, :], t["gluT"][c * 128:(c + 1) * 128, tg * 512:tg * 512 + 544], ("cgl", j), f"d_cgl{j}")
                    for k in range(31):
                        for j in range(4):
                            c = c4 + j
                            if k == 0:
                                P.op("vector", (lambda j=j, c=c: lambda e: e.tensor_scalar(out=acc[:, c, :], in0=gl[j][:, 2:514], scalar1=wdw[:, c, 0:1], scalar2=vec[:, 0, c:c + 1], op0=ALU.mult, op1=ALU.add))(),
                                     reads=[("cgl", j), "wdw", "vec"], writes=[("cacc", c)])
                            else:
                                P.op("vector", (lambda j=j, c=c, k=k: lambda e: e.scalar_tensor_tensor(out=acc[:, c, :], in0=gl[j][:, 2 + k:514 + k], scalar=wdw[:, c, k:k + 1], in1=acc[:, c, :], op0=ALU.mult, op1=ALU.add))(),
                                     reads=[("cgl", j), "wdw", ("cacc", c)], writes=[("cacc", c)])
                    for j in range(4):
                        c = c4 + j
                        b = c % 2
                        P.op("gpsimd", (lambda c=c, b=b: lambda e: e.tensor_copy(out=cb[b][:, :], in_=acc[:, c, :]))(), reads=[("cacc", c)], writes=[("ccb", b)])
                        P.op("scalar", (lambda c=c, b=b: lambda e: e.activation(out=cq[b][:, :], in_=acc[:, c, :], func=AF.Square))(), reads=[("cacc", c)], writes=[("ccq", b)])
                        P.op("tensor", (lambda c=c, b=b: lambda e: e.matmul(ps[0][:, :], lhsT=ones[:, :], rhs=cb[b][:, :], start=(c == 0), stop=(c == KC - 1)))(), reads=[("ccb", b), "ones"], writes=[("ps", 0)])
                        P.op("tensor", (lambda c=c, b=b: lambda e: e.matmul(ps[1][:, :], lhsT=ones[:, :], rhs=cq[b][:, :], start=(c == 0), stop=(c == KC - 1)))(), reads=[("ccq", b), "ones"], writes=[("ps", 1)])
                P.op("vector", lambda e: e.tensor_scalar(out=mean[:, :], in0=ps[0][:, :], scalar1=1.0 / D, scalar2=None, op0=ALU.mult), reads=[("ps", 0)], writes=["cmean"])
                P.op("vector", lambda e: e.tensor_scalar(out=rstd[:, :], in0=ps[1][:, :], scalar1=1.0 / D, scalar2=EPS, op0=ALU.mult, op1=ALU.add), reads=[("ps", 1)], writes=["crstd"])
                P.op("vector", lambda e: e.tensor_tensor(out=tmp[0][:, :], in0=mean[:, :], in1=mean[:, :], op=ALU.mult), reads=["cmean"], writes=[("ctmp", 0)])
                P.op("vector", lambda e: e.tensor_tensor(out=rstd[:, :], in0=rstd[:, :], in1=tmp[0][:, :], op=ALU.subtract), reads=["crstd", ("ctmp", 0)], writes=["crstd"])
                P.op("scalar", lambda e: e.activation(out=rstd[:, :], in_=rstd[:, :], func=AF.Sqrt), reads=["crstd"], writes=["crstd"])
                P.op("vector", lambda e: e.reciprocal(out=rstd[:, :], in_=rstd[:, :]), reads=["crstd"], writes=["crstd"])
                for c in range(KC):
                    b = c % 2
                    st, sk, ssem = zst.next()
                    P.op("vector", (lambda c=c, b=b: lambda e: e.tensor_tensor(out=tmp[b][:, :], in0=acc[:, c, :], in1=mean[:, :], op=ALU.subtract))(), reads=[("cacc", c), "cmean"], writes=[("ctmp", b)])
                    P.op("gpsimd", (lambda b=b: lambda e: e.tensor_tensor(out=tmp[b][:, :], in0=tmp[b][:, :], in1=rstd[:, :], op=ALU.mult))(), reads=[("ctmp", b), "crstd"], writes=[("ctmp", b)])
                    P.op("scalar", (lambda c=c, b=b, st=st: lambda e: e.activation(out=st[:, :], in_=tmp[b][:, :], func=AF.Silu, scale=vec[:, 1, c:c + 1], bias=vec[:, 2, c:c + 1]))(), reads=[("ctmp", b), "vec"], writes=[sk])
                    kb.store("gpsimd", t["zT"][c * 128:(c + 1) * 128, tg * 512:(tg + 1) * 512], st[:, :], sk, ssem, ("dram", "zT"))
            P.end_phase()

        with contextlib.ExitStack() as ph2:
            res = kb.sb(ph2, "res", [128, KC, Tc], BF16)
            rk = lambda kk: ("res", kk)
            with contextlib.ExitStack() as ph:
                gin = kb.stage_pool(ph, "gin", 2, BF16)
                fin = kb.stage_pool(ph, "fin", 2, F32)
                stf = kb.stage_pool(ph, "stf", 2, F32)
                stb = kb.stage_pool(ph, "stb", 2, BF16)
                tf = kb.stage_pool(ph, "tf", 2, F32)

                def epi_b(n):
                    def epi(ci, banks):
                        def _one(tg, pt, pk):
                            g_t, gk, gsem = gin.next()
                            kb.load("gpsimd", g_t[:, :], t["gbT"][n * 128:(n + 1) * 128, tg * 512:(tg + 1) * 512], gk, gsem)
                            st, sk, ssem = stf.next()
                            P.op("vector", lambda e: e.scalar_tensor_tensor(out=st[:, :], in0=pt[:, :], scalar=vec[:, 3, n:n + 1], in1=g_t[:, :], op0=ALU.add, op1=ALU.mult), reads=[pk, gk, "vec"], writes=[sk])
                            kb.store("gpsimd", t["mbT"][n * 128:(n + 1) * 128, tg * 512:(tg + 1) * 512], st[:, :], sk, ssem, ("dram", "mbT"))
                        for tg, (pt, pk) in enumerate(banks):
                            _one(tg, pt, pk)
                    return epi

                def epi_a(n):
                    def epi(ci, banks):
                        def _one(tg, pt, pk):
                            g_t, gk, gsem = gin.next()
                            kb.load("gpsimd", g_t[:, :], t["gaT"][n * 128:(n + 1) * 128, tg * 512:(tg + 1) * 512], gk, gsem)
                            f_t, fk, fsem = fin.next()
                            kb.load("gpsimd", f_t[:, :], t["mbT"][n * 128:(n + 1) * 128, tg * 512:(tg + 1) * 512], fk, fsem, reads=[("dram", "mbT")])
                            tt, tk, _ = tf.next()
                            st, sk, ssem = stb.next()
                            P.op("vector", lambda e: e.tensor_tensor(out=tt[:, :], in0=pt[:, :], in1=g_t[:, :], op=ALU.mult), reads=[pk, gk], writes=[tk])
                            P.op("gpsimd", lambda e: e.tensor_tensor(out=st[:, :], in0=tt[:, :], in1=f_t[:, :], op=ALU.add), reads=[tk, fk], writes=[sk])
                            kb.store("gpsimd", t["mT"][n * 128:(n + 1) * 128, tg * 512:(tg + 1) * 512], st[:, :], sk, ssem, ("dram", "mT"))
                        for tg, (pt, pk) in enumerate(banks):
                            _one(tg, pt, pk)
                    return epi

                def epi_o(n):
                    def epi(ci, banks):
                        def _one(tg, pt, pk):
                            f_t, fk, fsem = fin.next()
                            kb.load("gpsimd", f_t[:, :], t["xT"][n * 128:(n + 1) * 128, tg * 512:(tg + 1) * 512], fk, fsem)
                            st, sk, ssem = stf.next()
                            P.op("vector", lambda e: e.tensor_tensor(out=st[:, :], in0=pt[:, :], in1=f_t[:, :], op=ALU.add), reads=[pk, fk], writes=[sk])
                            kb.store("gpsimd", t["x1T"][n * 128:(n + 1) * 128, tg * 512:(tg + 1) * 512], st[:, :], sk, ssem, ("dram", "x1T"))
                        for tg, (pt, pk) in enumerate(banks):
                            _one(tg, pt, pk)
                    return epi

                wb = kb.wbufs(ph)
                kb.load_res(res, t["zT"], Tc)
                kb.proj(wb, ps, res, rk, KC, Tc, [(t["w_co"], n * 128, 128, epi_b(n)) for n in range(KC)])
                P.end_phase()
                kb.load_res(res, t["attnT"], Tc)
                kb.proj(wb, ps, res, rk, KC, Tc, [(t["w_ao"], n * 128, 128, epi_a(n)) for n in range(KC)])
                P.end_phase()
                kb.load_res(res, t["mT"], Tc)
                kb.proj(wb, ps, res, rk, KC, Tc, [(t["w_out"], n * 128, 128, epi_o(n)) for n in range(KC)])
                P.end_phase()

            def norm_out(c, t0, xs, xk, rstd, rks, TB):
                P.op("vector", lambda e: e.scalar_tensor_tensor(out=res[:, c, t0:t0 + TB], in0=xs[:, :], scalar=vec[:, 4, c:c + 1], in1=rstd[:, :], op0=ALU.mult, op1=ALU.mult),
                     reads=[xk, "vec"] + rks, writes=[("res", c)])
            kb.rmsnorm(ps, t["x1T"], Tc, None, (ones, "ones"), norm_out, TB=min(1024, Tc))

            with contextlib.ExitStack() as ph:
                gs = [kb.sb(ph, f"gs{tg}", [128, 512], F32) for tg in range(NTG)]
                stb = kb.stage_pool(ph, "stb", 4, BF16)

                def epi_gate(f):
                    def epi(ci, banks):
                        def _one(tg, pt, pk):
                            P.op("scalar", lambda e: e.activation(out=gs[tg][:, :], in_=pt[:, :], func=AF.Silu), reads=[pk], writes=[("gs", tg)])
                        for tg, (pt, pk) in enumerate(banks):
                            _one(tg, pt, pk)
                    return epi

                def epi_up(f):
                    def epi(ci, banks):
                        def _one(tg, pt, pk):
                            st, sk, ssem = stb.next()
                            P.op("vector", lambda e: e.tensor_tensor(out=st[:, :], in0=pt[:, :], in1=gs[tg][:, :], op=ALU.mult), reads=[pk, ("gs", tg)], writes=[sk])
                            kb.store("gpsimd", t["actT"][f * 128:(f + 1) * 128, tg * 512:(tg + 1) * 512], st[:, :], sk, ssem, ("dram", "actT"))
                        for tg, (pt, pk) in enumerate(banks):
                            _one(tg, pt, pk)
                    return epi
                chunks = []
                for f in range(NF):
                    chunks.append((t["w_fg"], f * 128, 128, epi_gate(f)))
                    chunks.append((t["w_fu"], f * 128, 128, epi_up(f)))
                kb.proj(kb.wbufs(ph), ps, res, rk, KC, Tc, chunks)
                P.end_phase()

        dstT = t["x2T"] if final else t["outT"]
        with contextlib.ExitStack() as ph:
            act = kb.sb(ph, "act", [128, NF, 512], BF16)
            fin = kb.stage_pool(ph, "fin", 2, F32)
            stf = kb.stage_pool(ph, "stf", 2, F32)
            wb = kb.wbufs(ph)
            for tg in range(NTG):
                kb.load_res(act, t["actT"], 512, t0=tg * 512, nk=NF, kname="act", grp=8)

                def epi_d(n, tg=tg):
                    def epi(ci, banks):
                        pt, pk = banks[0]
                        f_t, fk, fsem = fin.next()
                        kb.load("gpsimd", f_t[:, :], t["x1T"][n * 128:(n + 1) * 128, tg * 512:(tg + 1) * 512], fk, fsem, reads=[("dram", "x1T")])
                        st, sk, ssem = stf.next()
                        P.op("vector", lambda e: e.tensor_tensor(out=st[:, :], in0=pt[:, :], in1=f_t[:, :], op=ALU.add), reads=[pk, fk], writes=[sk])
                        kb.store("gpsimd", dstT[n * 128:(n + 1) * 128, tg * 512:(tg + 1) * 512], st[:, :], sk, ssem, ("dram", "dst"))
                    return epi
                kb.proj(wb, ps, act, lambda kk: ("act", kk), NF, 512, [(t["w_fd"], n * 128, 128, epi_d(n)) for n in range(KC)])
            P.end_phase()

        if final:
            with contextlib.ExitStack() as ph:
                stf = kb.stage_pool(ph, "stf", 2, F32, width=min(1024, Tc))

                def norm_out2(c, t0, xs, xk, rstd, rks, TB):
                    st, sk, ssem = stf.next()
                    P.op("vector", lambda e: e.scalar_tensor_tensor(out=st[:, :], in0=xs[:, :], scalar=vec[:, 5, c:c + 1], in1=rstd[:, :], op0=ALU.mult, op1=ALU.mult),
                         reads=[xk, "vec"] + rks, writes=[sk])
                    kb.store("gpsimd", t["outT"][c * 128:(c + 1) * 128, t0:t0 + TB], st[:, :], sk, ssem, ("dram", "outT"))
                kb.rmsnorm(ps, t["x2T"], Tc, None, (ones, "ones"), norm_out2, TB=min(1024, Tc))


def lay_vec(v):
    v = np.asarray(v, dtype=np.float32)
    n = v.shape[0] // 128
    return np.ascontiguousarray(v.reshape(n, 128).T)


def lay_wdw(w):
    return np.ascontiguousarray(np.asarray(w, np.float32).T.reshape(KC, 128, 31).transpose(1, 0, 2))


_NC_CACHE = {}


def get_nc(kind, *args):
    key = (kind,) + tuple(args)
    if key not in _NC_CACHE:
        _NC_CACHE[key] = {"A": build_A, "C": build_C}[kind](*args)
    return _NC_CACHE[key]


def run_A(xT_list, w_in_l, norm_mix_l, b_glu_l):
    Tc = xT_list[0].shape[1]
    nc = get_nc("A", Tc)
    gm = lay_vec(norm_mix_l)
    bg = lay_vec(b_glu_l)
    w = np.ascontiguousarray(w_in_l, dtype=np.float32)
    ins = [{"xT": np.ascontiguousarray(xT), "w_in": w, "gmix": gm, "bglu": bg} for xT in xT_list]
    res = run_bass_kernel_spmd(nc, ins, core_ids=list(range(len(ins))))
    return [r["projT"] for r in res.results]


def run_C(final, xT_list, gluT_list, attnT_list, gaT_list, gbT_list, W):
    Tc = xT_list[0].shape[1]
    nc = get_nc("C", Tc, final)
    vecs = np.ascontiguousarray(np.stack([lay_vec(W[k]) for k in ("b_dw", "conv_ln_g", "conv_ln_b", "b_conv_out", "norm_ffn", "norm_final")], axis=1))
    common = {"wdw": lay_wdw(W["w_dw"]), "vecs": vecs}
    for k, n in (("w_co", "w_conv_out"), ("w_ao", "w_attn_out"), ("w_out", "w_out"), ("w_fg", "w_ffn_gate"), ("w_fu", "w_ffn_up"), ("w_fd", "w_ffn_down")):
        common[k] = np.ascontiguousarray(W[n], dtype=np.float32)
    ins = []
    for i in range(len(xT_list)):
        d = dict(common)
        d.update({"xT": np.ascontiguousarray(xT_list[i]), "gluT": np.ascontiguousarray(gluT_list[i]), "attnT": np.ascontiguousarray(attnT_list[i]),
                  "gaT": np.ascontiguousarray(gaT_list[i]), "gbT": np.ascontiguousarray(gbT_list[i])})
        ins.append(d)
    res = run_bass_kernel_spmd(nc, ins, core_ids=list(range(len(ins))))
    return [r["outT"] for r in res.results]


def emit_B(kb, T, projT, attnT, cw, cs):
    nc, P = kb.nc, kb.P
    NQT = T // 128
    NCMP = T // 16 - 1
    NCT = (NCMP + 1 + 127) // 128
    NCP = NCT * 128
    with contextlib.ExitStack() as outer:
        identf = kb.sb(outer, "identf", [128, 128], F32)
        identb = kb.sb(outer, "identb", [128, 128], BF16)
        eext = kb.sb(outer, "eext", [128, NQT, 128], BF16)
        b31 = kb.sb(outer, "b31", [128, NH], F32)
        b31h = kb.sb(outer, "b31h", [128, NH], BF16)
        b31hf = kb.sb(outer, "b31hf", [128, NH], F32)
        b31l = kb.sb(outer, "b31l", [128, NH], BF16)
        kcmpT = [kb.sb(outer, f"kcmpT{g}", [128, NCP], BF16) for g in range(NG)]
        vcaug = [kb.sb(outer, f"vcaug{g}", [128, NCT, 193], BF16) for g in range(NG)]
        kb.load("sync", identf[:, :], cs["identf"][:, :], "identf", "d_c0")
        kb.load("sync", identb[:, :], cs["identb"][:, :], "identb", "d_c1")
        kb.load("sync", eext[0:97, :, :], cs["eext"][:, :, :], "eext", "d_c2")
        kb.load("sync", b31[:, :], cs["b31"][:, :], "b31", "d_c3")
        for g in range(NG):
            kb.load("sync", vcaug[g][:, :, 128:193], cs["vcc"].rearrange("c n k -> n c k"), ("vcaug", g), f"d_sres{g}")
        P.op("vector", lambda e: e.tensor_copy(out=b31h[:, :], in_=b31[:, :]), reads=["b31"], writes=["b31h"])
        P.op("vector", lambda e: e.tensor_copy(out=b31hf[:, :], in_=b31h[:, :]), reads=["b31h"], writes=["b31hf"])
        P.op("vector", lambda e: e.tensor_tensor(out=b31l[:, :], in0=b31[:, :], in1=b31hf[:, :], op=ALU.subtract), reads=["b31", "b31hf"], writes=["b31l"])

        with contextlib.ExitStack() as ph:
            ps = kb.psum(ph)
            xin = [kb.sb(ph, f"xin{b}", [128, T], BF16) for b in range(2)]
            w1s = kb.sb(ph, "w1s", [128, 32, 128], F32)
            w1b = kb.sb(ph, "w1b", [128, 32, 128], BF16)
            w2s = kb.sb(ph, "w2s", [128, 128], F32)
            w2b = kb.sb(ph, "w2b", [128, 128], BF16)
            poss = kb.sb(ph, "poss", [128, 32], F32)
            posb = kb.sb(ph, "posb", [128, 32], BF16)
            pbias = kb.sb(ph, "pbias", [128, 1], F32)
            u = [kb.sb(ph, f"cu{i}", [128, NCP], F32) for i in range(4)]
            geT = kb.sb(ph, "geT", [128, NCP], BF16)
            P.op("vector", lambda e: e.memset(geT[:, :], 0.0), writes=["geT"])
            it = 0
            for kind in range(2):
                kb.load("sync", w1s[:, :, :], cw["w1"][kind].rearrange("(j d) o -> d j o", d=128), "w1s", "d_wst00")
                kb.load("sync", w2s[:, :], cw["w2"][kind], "w2s", "d_wst01")
                kb.load("sync", poss[:, :], cw["posT"][kind], "poss", "d_wst10")
                P.op("vector", lambda e: e.tensor_copy(out=w1b[:, :, :], in_=w1s[:, :, :]), reads=["w1s"], writes=["w1b"])
                P.op("vector", lambda e: e.tensor_copy(out=w2b[:, :], in_=w2s[:, :]), reads=["w2s"], writes=["w2b"])
                P.op("vector", lambda e: e.tensor_copy(out=posb[:, :], in_=poss[:, :]), reads=["poss"], writes=["posb"])
                for j in range(32):
                    P.op("tensor", (lambda j=j: lambda e: e.matmul(ps[6][:, 0:1], lhsT=w1b[:, j, :], rhs=posb[:, j:j + 1], start=(j == 0), stop=(j == 31)))(), reads=["w1b", "posb"], writes=[("ps", 6)])
                P.op("vector", lambda e: e.tensor_copy(out=pbias[:, :], in_=ps[6][:, 0:1]), reads=[("ps", 6)], writes=["pbias"])
                for g in range(NG):
                    b = it % 2
                    it += 1
                    r0 = C_KV + kind * 512 + g * 128
                    kb.load("gpsimd", xin[b][:, :], projT[r0:r0 + 128, :], ("xin", b), f"d_nxs{b}", reads=[("dram", "projT")])
                    x3 = xin[b][:, :].rearrange("p (n s) -> p n s", s=16)
                    for j in range(32):
                        P.op("tensor", (lambda j=j, x3=x3: lambda e: e.matmul(ps[0][:, 0:NCMP], lhsT=w1b[:, j, :], rhs=x3[:, j // 16:j // 16 + NCMP, j % 16], start=(j == 0), stop=(j == 31)))(),
                             reads=["w1b", ("xin", b)], writes=[("ps", 0)])
                    P.op("vector", lambda e: e.tensor_scalar(out=u[0][:, 0:NCMP], in0=ps[0][:, 0:NCMP], scalar1=pbias[:, 0:1], scalar2=None, op0=ALU.add), reads=[("ps", 0), "pbias"], writes=[("cu", 0)])
                    P.op("vector", lambda e: e.tensor_tensor(out=u[1][:, 0:NCMP], in0=u[0][:, 0:NCMP], in1=u[0][:, 0:NCMP], op=ALU.mult), reads=[("cu", 0)], writes=[("cu", 1)])
                    P.op("vector", lambda e: e.tensor_scalar(out=u[1][:, 0:NCMP], in0=u[1][:, 0:NCMP], scalar1=0.044715, scalar2=1.0, op0=ALU.mult, op1=ALU.add), reads=[("cu", 1)], writes=[("cu", 1)])
                    P.op("vector", lambda e: e.tensor_tensor(out=u[2][:, 0:NCMP], in0=u[1][:, 0:NCMP], in1=u[0][:, 0:NCMP], op=ALU.mult), reads=[("cu", 1), ("cu", 0)], writes=[("cu", 2)])
                    P.op("scalar", lambda e: e.activation(out=u[3][:, 0:NCMP], in_=u[2][:, 0:NCMP], func=AF.Sigmoid, scale=1.5957691216057308), reads=[("cu", 2)], writes=[("cu", 3)])
                    P.op("vector", lambda e: e.tensor_tensor(out=geT[:, 0:NCMP], in0=u[0][:, 0:NCMP], in1=u[3][:, 0:NCMP], op=ALU.mult), reads=[("cu", 0), ("cu", 3)], writes=["geT"])
                    if kind == 0:
                        P.op("tensor", lambda e: e.matmul(ps[1][:, 0:NCP], lhsT=w2b[:, :], rhs=geT[:, :], start=True, stop=True), reads=["w2b", "geT"], writes=[("ps", 1)])
                        P.op("scalar", (lambda g=g: lambda e: e.activation(out=kcmpT[g][:, :], in_=ps[1][:, 0:NCP], func=AF.Copy))(), reads=[("ps", 1)], writes=[("kcmpT", g)])
                    else:
                        for ct in range(NCT):
                            P.op("tensor", (lambda ct=ct: lambda e: e.matmul(ps[2 + ct][:, 0:128], lhsT=geT[:, ct * 128:(ct + 1) * 128], rhs=w2b[:, :], start=True, stop=True))(), reads=["w2b", "geT"], writes=[("ps", 2 + ct)])
                            P.op("scalar", (lambda g=g, ct=ct: lambda e: e.activation(out=vcaug[g][:, ct, 0:128], in_=ps[2 + ct][:, 0:128], func=AF.Copy))(), reads=[("ps", 2 + ct)], writes=[("vcaug", g)])
            P.end_phase()

        for g in range(NG):
            with contextlib.ExitStack() as gph:
                qT = kb.sb(gph, "qT", [128, 8, T], BF16)
                ksT = kb.sb(gph, "ksT", [128, T], BF16)
                kwT = kb.sb(gph, "kwT", [128, T], BF16)
                Vs = kb.sb(gph, "Vs", [128, NQT, 129], BF16)
                Vw = kb.sb(gph, "Vw", [128, NQT, 129], BF16)
                brg = kb.sb(gph, "brg", [128, NQT, 24], F32)
                Bn = [kb.sb(gph, f"Bn{i}", [128, 8, 128], F32) for i in range(3)]
                rbuf = kb.sb(gph, "rbuf", [128, 8, 128], BF16)
                with contextlib.ExitStack() as ph:
                    vT = [kb.sb(ph, f"vT{i}", [128, T], BF16) for i in range(2)]
                    gT = kb.sb(ph, "gT", [128, T], BF16)
                    kb.uid += 1
                    self_ps = ph.enter_context(nc.psum_tensor(f"psb_{g}_{kb.uid}", [128, 8, 128], BF16))
                    kb.load("sync", qT[:, :, :], projT[C_Q + g * 1024:C_Q + (g + 1) * 1024, :].rearrange("(h d) t -> d h t", d=128), "qT", "d_c0", reads=[("dram", "projT")])
                    kb.load("sync", ksT[:, :], projT[C_KV + 1024 + g * 128:C_KV + 1024 + (g + 1) * 128, :], "ksT", "d_c1", reads=[("dram", "projT")])
                    kb.load("sync", kwT[:, :], projT[C_KV + 2048 + g * 128:C_KV + 2048 + (g + 1) * 128, :], "kwT", "d_c2", reads=[("dram", "projT")])
                    kb.load("gpsimd", vT[0][:, :], projT[C_KV + 1536 + g * 128:C_KV + 1536 + (g + 1) * 128, :], ("vT", 0), "d_nxs0", reads=[("dram", "projT")])
                    kb.load("gpsimd", vT[1][:, :], projT[C_KV + 2560 + g * 128:C_KV + 2560 + (g + 1) * 128, :], ("vT", 1), "d_nxs1", reads=[("dram", "projT")])
                    kb.load("gpsimd", gT[0:24, :], projT[C_BR + g * 24:C_BR + (g + 1) * 24, :], "gT", "d_gT", reads=[("dram", "projT")])
                    for i, nm in enumerate(("B0", "B1", "Bw4")):
                        kb.load("sync", Bn[i][:, :, :], cs[nm][:, g * 8:(g + 1) * 8, :], ("Bn", i), f"d_sres{i}")
                    P.op("vector", lambda e: e.memset(rbuf[:, :, :], 0.0), writes=["rbuf"])
                    P.op("vector", lambda e: e.memset(Vs[:, :, 128:129], 1.0), writes=["Vs1"])
                    P.op("vector", lambda e: e.memset(Vw[:, :, 128:129], 1.0), writes=["Vw1"])
                    P.op("vector", (lambda g=g: lambda e: e.tensor_copy(out=rbuf[64:65, :, :], in_=b31h[64:65, g * 8:(g + 1) * 8].unsqueeze(2).broadcast_to([1, 8, 128])))(), reads=["b31h", "rbuf"], writes=["rbuf"])
                    P.op("vector", (lambda g=g: lambda e: e.tensor_copy(out=rbuf[96:97, :, :], in_=b31l[96:97, g * 8:(g + 1) * 8].unsqueeze(2).broadcast_to([1, 8, 128])))(), reads=["b31l", "rbuf"], writes=["rbuf"])
                    for vi, Vt in enumerate((Vs, Vw)):
                        for k0 in range(0, NQT, 8):
                            for kk in range(8):
                                kt = k0 + kk
                                P.op("tensor", (lambda vi=vi, kt=kt, kk=kk: lambda e: e.transpose(out=self_ps[:, kk, :], in_=vT[vi][:, kt * 128:(kt + 1) * 128], identity=identb[:, :]))(),
                                     reads=[("vT", vi), "identb"], writes=["psb"])
                            P.op("scalar", (lambda Vt=Vt, k0=k0: lambda e: e.activation(out=Vt[:, k0:k0 + 8, 0:128], in_=self_ps[:, :, :], func=AF.Copy))(), reads=["psb"], writes=[("V", vi)])
                    for k0 in range(0, NQT, 8):
                        for kk in range(8):
                            kt = k0 + kk
                            P.op("tensor", (lambda kt=kt, kk=kk: lambda e: e.transpose(out=self_ps[:, kk, 0:24], in_=gT[0:24, kt * 128:(kt + 1) * 128], identity=identb[0:24, 0:24]))(),
                                 reads=["gT", "identb"], writes=["psb"])
                        P.op("vector", (lambda k0=k0: lambda e: e.tensor_copy(out=brg[:, k0:k0 + 8, :], in_=self_ps[:, :, 0:24]))(), reads=["psb"], writes=["brg"])
                    P.end_phase()

                with contextlib.ExitStack() as ph:
                    ps = kb.psum(ph)
                    bc = [[kb.sb(ph, f"bc{b}{ct}", [128, 8, 128], F32) for ct in range(NCT)] for b in range(2)]
                    mvfb = [kb.sb(ph, f"mvfb{b}", [128, 2, 64], F32) for b in range(2)]
                    ssb = [kb.sb(ph, f"ssb{i}", [128, 512], F32) for i in range(2)]
                    pts = [kb.sb(ph, f"pt{i}", [128, 512], BF16) for i in range(4)]
                    ocmp = kb.sb(ph, "ocmp", [128, 8, 128], F32)
                    otok = kb.sb(ph, "otok", [128, 8, 128], F32)
                    sm = kb.sb(ph, "sm", [128, 8, 8], F32)
                    imp = kb.sb(ph, "imp", [128, 64], F32)
                    imp2 = kb.sb(ph, "imp2", [128, 64], F32)
                    m8 = kb.sb(ph, "m8", [128, 16], F32)
                    selb = kb.sb(ph, "selb", [128, 64], F32)
                    ost = [kb.sb(ph, f"ost{i}", [128, 4, 128], BF16) for i in range(2)]
                    cnts = {"s": 0, "p": 0, "o": 0}

                    def qk(lhsT_ap, lkey, quad, t0, extra, near_bias):
                        si = cnts["s"] % 2
                        cnts["s"] += 1
                        sp, spk = ps[si], ("ps", si)
                        rhs = qT[:, quad * 4:(quad + 1) * 4, t0:t0 + 128]
                        P.op("tensor", lambda e: e.matmul(sp[:, :].rearrange("p (a b) -> p a b", a=4), lhsT=lhsT_ap, rhs=rhs, start=True, stop=(extra is None)), reads=[lkey, "qT"], writes=[spk])
                        if extra is not None:
                            el, er, ekeys = extra
                            P.op("tensor", lambda e: e.matmul(sp[:, :].rearrange("p (a b) -> p a b", a=4), lhsT=el, rhs=er, start=False, stop=True), reads=ekeys, writes=[spk])
                        pi = cnts["p"] % 4
                        cnts["p"] += 1
                        pt, ptk = pts[pi], ("pt", pi)
                        if near_bias is not None:
                            btile, bkey = near_bias
                            sb_, sbk = ssb[si], ("ssb", si)
                            P.op("vector", lambda e: e.tensor_tensor(out=sb_[:, :], in0=sp[:, :], in1=btile, op=ALU.add), reads=[spk, bkey], writes=[sbk])
                            P.op("scalar", lambda e: e.activation(out=pt[:, :], in_=sb_[:, :], func=AF.Exp), reads=[sbk], writes=[ptk])
                        else:
                            P.op("scalar", lambda e: e.activation(out=pt[:, :], in_=sp[:, :], func=AF.Exp), reads=[spk], writes=[ptk])
                        return pt, ptk

                    def pv(pt, ptk, obase, width, rhs_ap, rkey, first, last):
                        for hq in range(4):
                            bi = obase + hq // 2
                            c0 = (hq % 2) * 256
                            P.op("tensor", (lambda hq=hq, bi=bi, c0=c0: lambda e: e.matmul(ps[bi][:, c0:c0 + width], lhsT=pt[:, hq * 128:(hq + 1) * 128], rhs=rhs_ap, start=(first and hq % 2 == 0), stop=last))(),
                                 reads=[ptk, rkey], writes=[("ps", bi, hq % 2)])

                    def _do_qt(qt):
                        t0 = qt * 128
                        bb = qt % 2
                        nct = min(NCT, (8 * qt + 6) // 128 + 1)
                        for ct in range(nct):
                            kb.load("gpsimd", bc[bb][ct][:, :, :], cs["bcmp"][qt, ct, :, g * 8:(g + 1) * 8, :], ("bc", bb, ct), f"d_bc{bb}{ct}")
                        kb.load("gpsimd", mvfb[bb][:, :, :], cs["mvfb"][qt], ("mvfb", bb), f"d_mvfb{bb}")
                        for quad in range(2):
                            for ct in range(nct):
                                pt, ptk = qk(kcmpT[g][:, ct * 128:(ct + 1) * 128], ("kcmpT", g), quad, t0, None,
                                             (bc[bb][ct][:, quad * 4:(quad + 1) * 4, :].rearrange("p a b -> p (a b)"), ("bc", bb, ct)))
                                pv(pt, ptk, 2 + quad * 2, 193, vcaug[g][:, ct, :], ("vcaug", g), ct == 0, ct == nct - 1)
                            for hq in range(4):
                                hl = quad * 4 + hq
                                bi = 2 + quad * 2 + hq // 2
                                c0 = (hq % 2) * 256
                                ok = ("ps", bi, hq % 2)

                                def _cmp_epi(hl=hl, bi=bi, c0=c0, ok=ok):
                                    P.op("vector", lambda e: e.tensor_scalar(out=sm[:, hl, 0:1], in0=ps[bi][:, c0 + 192:c0 + 193], scalar1=1.0e-30, scalar2=None, op0=ALU.max), reads=[ok], writes=[("sm", hl)])
                                    P.op("vector", lambda e: e.reciprocal(out=sm[:, hl, 0:1], in_=sm[:, hl, 0:1]), reads=[("sm", hl)], writes=[("sm", hl)])
                                    P.op("vector", lambda e: e.tensor_tensor(out=sm[:, hl, 1:2], in0=sm[:, hl, 0:1], in1=brg[:, qt, hl * 3:hl * 3 + 1], op=ALU.mult), reads=[("sm", hl), "brg"], writes=[("sm", hl)])
                                    P.op("vector", lambda e: e.tensor_scalar(out=ocmp[:, hl, :], in0=ps[bi][:, c0:c0 + 128], scalar1=sm[:, hl, 1:2], scalar2=None, op0=ALU.mult), reads=[ok, ("sm", hl)], writes=[("ocmp", hl)])
                                    if hl == 0:
                                        P.op("vector", lambda e: e.tensor_scalar(out=imp[:, :], in0=ps[bi][:, c0 + 128:c0 + 192], scalar1=sm[:, hl, 0:1], scalar2=None, op0=ALU.mult), reads=[ok, ("sm", hl)], writes=["imp"])
                                    else:
                                        P.op("vector", lambda e: e.scalar_tensor_tensor(out=imp[:, :], in0=ps[bi][:, c0 + 128:c0 + 192], scalar=sm[:, hl, 0:1], in1=imp[:, :], op0=ALU.mult, op1=ALU.add), reads=[ok, ("sm", hl), "imp"], writes=["imp"])
                                _cmp_epi()
                        P.op("vector", lambda e: e.tensor_tensor(out=imp[:, :], in0=imp[:, :], in1=mvfb[bb][:, 0, :], op=ALU.mult), reads=["imp", ("mvfb", bb)], writes=["imp"])
                        P.op("vector", lambda e: e.tensor_tensor(out=imp[:, :], in0=imp[:, :], in1=mvfb[bb][:, 1, :], op=ALU.add), reads=["imp", ("mvfb", bb)], writes=["imp"])
                        P.op("vector", lambda e: e.max(out=m8[:, 0:8], in_=imp[:, :]), reads=["imp"], writes=["m8a"])
                        P.op("vector", lambda e: e.match_replace(out=imp2[:, :], in_to_replace=m8[:, 0:8], in_values=imp[:, :], imm_value=-3.0e9), reads=["imp", "m8a"], writes=["imp2"])
                        P.op("vector", lambda e: e.max(out=m8[:, 8:16], in_=imp2[:, :]), reads=["imp2"], writes=["m8b"])
                        P.op("vector", lambda e: e.tensor_scalar(out=selb[:, :], in0=imp[:, :], scalar1=m8[:, 15:16], scalar2=NEG, op0=ALU.is_lt, op1=ALU.mult), reads=["imp", "m8b"], writes=["selb"])
                        P.op("tensor", lambda e: e.transpose(out=ps[6][0:64, 0:128], in_=selb[:, :], identity=identf[:, :]), reads=["selb", "identf"], writes=[("ps", 6)])
                        P.op("vector", lambda e: e.tensor_copy(out=rbuf[0:64, :, :], in_=ps[6][0:64, 0:128].unsqueeze(1).broadcast_to([64, 8, 128])), reads=[("ps", 6)], writes=["rbuf"])
                        for quad in range(2):
                            rq = rbuf[:, quad * 4:(quad + 1) * 4, :]
                            tiles = []
                            for o in range(qt + 1):
                                kt = qt - o
                                if o == 0:
                                    extra, nb = None, (Bn[0][:, quad * 4:(quad + 1) * 4, :].rearrange("p a b -> p (a b)"), ("Bn", 0))
                                elif o == 1:
                                    extra, nb = (eext[0:64, kt, :], rq[0:64], ["eext", "rbuf"]), (Bn[1][:, quad * 4:(quad + 1) * 4, :].rearrange("p a b -> p (a b)"), ("Bn", 1))
                                else:
                                    extra, nb = (eext[0:97, kt, :], rq[0:97], ["eext", "rbuf"]), None
                                tiles.append((ksT[:, kt * 128:(kt + 1) * 128], "ksT", extra, nb, 2, Vs[:, kt, :], ("V", 0), o == 0, o == qt))
                            nw = min(4, qt)
                            for o in range(nw + 1):
                                kt = qt - o
                                if o == 0:
                                    extra, nb = None, (Bn[0][:, quad * 4:(quad + 1) * 4, :].rearrange("p a b -> p (a b)"), ("Bn", 0))
                                elif o == 1:
                                    extra, nb = None, (Bn[1][:, quad * 4:(quad + 1) * 4, :].rearrange("p a b -> p (a b)"), ("Bn", 1))
                                elif o == 4:
                                    extra, nb = None, (Bn[2][:, quad * 4:(quad + 1) * 4, :].rearrange("p a b -> p (a b)"), ("Bn", 2))
                                else:
                                    extra, nb = (eext[64:97, kt, :], rq[64:97], ["eext", "rbuf"]), None
                                tiles.append((kwT[:, kt * 128:(kt + 1) * 128], "kwT", extra, nb, 4, Vw[:, kt, :], ("V", 1), o == 0, o == nw))
                            prev = None
                            for (l_ap, lk, extra, nb, obase, v_ap, vk, first, last) in tiles:
                                pt, ptk = qk(l_ap, lk, quad, t0, extra, nb)
                                if prev is not None:
                                    pv(*prev)
                                prev = (pt, ptk, obase, 129, v_ap, vk, first, last)
                            pv(*prev)
                            oi = cnts["o"] % 2
                            cnts["o"] += 1
                            for hq in range(4):
                                hl = quad * 4 + hq

                                def _epi(hl=hl, hq=hq):
                                    c0 = (hq % 2) * 256
                                    bs, bw = 2 + hq // 2, 4 + hq // 2
                                    ks_, kw_ = ("ps", bs, hq % 2), ("ps", bw, hq % 2)
                                    P.op("vector", lambda e: e.reciprocal(out=sm[:, hl, 2:3], in_=ps[bs][:, c0 + 128:c0 + 129]), reads=[ks_], writes=[("sm2", hl)])
                                    P.op("vector", lambda e: e.tensor_tensor(out=sm[:, hl, 3:4], in0=sm[:, hl, 2:3], in1=brg[:, qt, hl * 3 + 1:hl * 3 + 2], op=ALU.mult), reads=[("sm2", hl), "brg"], writes=[("sm2", hl)])
                                    P.op("vector", lambda e: e.reciprocal(out=sm[:, hl, 4:5], in_=ps[bw][:, c0 + 128:c0 + 129]), reads=[kw_], writes=[("sm3", hl)])
                                    P.op("vector", lambda e: e.tensor_tensor(out=sm[:, hl, 5:6], in0=sm[:, hl, 4:5], in1=brg[:, qt, hl * 3 + 2:hl * 3 + 3], op=ALU.mult), reads=[("sm3", hl), "brg"], writes=[("sm3", hl)])
                                    P.op("vector", lambda e: e.scalar_tensor_tensor(out=otok[:, hl, :], in0=ps[bs][:, c0:c0 + 128], scalar=sm[:, hl, 3:4], in1=ocmp[:, hl, :], op0=ALU.mult, op1=ALU.add), reads=[ks_, ("sm2", hl), ("ocmp", hl)], writes=[("otok", hl)])
                                    P.op("vector", lambda e: e.scalar_tensor_tensor(out=otok[:, hl, :], in0=ps[bw][:, c0:c0 + 128], scalar=sm[:, hl, 5:6], in1=otok[:, hl, :], op0=ALU.mult, op1=ALU.add), reads=[kw_, ("sm3", hl), ("otok", hl)], writes=[("otok", hl)])
                                    P.op("tensor", lambda e: e.transpose(out=ps[7][:, hq * 128:(hq + 1) * 128], in_=otok[:, hl, :], identity=identf[:, :]), reads=[("otok", hl), "identf"], writes=[("ps", 7)])
                                _epi()
                            P.op("scalar", (lambda oi=oi: lambda e: e.activation(out=ost[oi][:, :, :].rearrange("p a b -> p (a b)"), in_=ps[7][:, :], func=AF.Copy))(), reads=[("ps", 7)], writes=[("ost", oi)])
                            r0 = g * 1024 + quad * 512
                            kb.store("sync", attnT[r0:r0 + 512, t0:t0 + 128].rearrange("(h d) q -> d h q", d=128), ost[oi][:, :, :], ("ost", oi), f"d_ost{oi}", ("dram", "attnT"))
                    for qt in range(NQT):
                        _do_qt(qt)
                    P.end_phase()


def t5_bucket_np(d):
    n = np.maximum(d, 0)
    nf = np.maximum(n, 16).astype(np.float32)
    large = 16 + (np.log(nf / np.float32(16)) / np.float32(np.log(128 / 16)) * np.float32(16)).astype(np.int32)
    large = np.minimum(large, 31)
    return np.where(n < 16, n, large)


def make_consts(T, rel_bias):
    bf = ml_dtypes.bfloat16
    NQT = T // 128
    NCMP = T // 16 - 1
    NCT = (NCMP + 1 + 127) // 128
    nsb = T // 64
    tab = np.concatenate([np.asarray(rel_bias, np.float32), np.full((1, NH), NEG, np.float32)], axis=0)
    cs = {}
    cs["identf"] = np.eye(128, dtype=np.float32)
    cs["identb"] = np.eye(128, dtype=np.float32).astype(bf)
    ee = np.zeros((97, NQT, 128), np.float32)
    for kt in range(NQT):
        if 2 * kt < 64:
            ee[2 * kt, kt, 0:64] = 1.0
        if 2 * kt + 1 < 64:
            ee[2 * kt + 1, kt, 64:128] = 1.0
    ee[64] = 1.0
    ee[96] = 1.0
    cs["eext"] = ee.astype(bf)
    cs["b31"] = np.ascontiguousarray(np.broadcast_to(tab[31][None, :], (128, NH)))
    c_start = np.arange(NCMP) * 16
    s_start = np.arange(nsb) * 64
    ov = np.minimum(c_start[:, None] + 32, s_start[None, :] + 64) - np.maximum(c_start[:, None], s_start[None, :])
    ov = np.clip(ov, 0, None) // 16
    vcc = np.zeros((NCT * 128, 65), np.float32)
    vcc[:NCMP, :nsb] = ov
    vcc[:, 64] = 1.0
    cs["vcc"] = vcc.reshape(NCT, 128, 65).astype(bf)
    kk = np.arange(128)[:, None]
    qq = np.arange(128)[None, :]
    def near(off, extra_mask=None):
        d = qq - kk + off
        idx = np.where(d >= 0, t5_bucket_np(d), 32)
        if extra_mask is not None:
            idx = np.where(extra_mask, idx, 32)
        return np.ascontiguousarray(tab[idx].transpose(0, 2, 1))
    cs["B0"] = near(0)
    cs["B1"] = near(128)
    cs["Bw4"] = near(512, extra_mask=(qq - kk + 512 < 512))
    i_all = np.arange(NCT * 128)
    t_all = np.arange(T)
    d = t_all[None, :] - (16 * i_all[:, None] + 31)
    idx = np.where((d >= 0) & (i_all[:, None] < NCMP), t5_bucket_np(d), 32)
    bc = tab[idx]
    bc = bc.reshape(NCT, 128, NQT, 128, NH).transpose(2, 0, 1, 4, 3)
    cs["bcmp"] = np.ascontiguousarray(bc)
    cur = t_all // 64
    blk = np.arange(64)[None, :]
    forced = (blk == 0) | (blk == cur[:, None]) | (blk == cur[:, None] - 1)
    valid = (blk <= cur[:, None]) & (blk < nsb)
    mv = (valid & ~forced).astype(np.float32)
    fb = np.where(~valid, -1.0e9, np.where(forced, 1.0e9, 0.0)).astype(np.float32)
    cs["mvfb"] = np.ascontiguousarray(np.stack([mv, fb], axis=1).reshape(NQT, 128, 2, 64))
    return cs


CONST_SPECS = lambda T: {
    "identf": ([128, 128], F32), "identb": ([128, 128], BF16), "eext": ([97, T // 128, 128], BF16), "b31": ([128, NH], F32),
    "vcc": ([((T // 16) + 127) // 128, 128, 65], BF16), "B0": ([128, NH, 128], F32), "B1": ([128, NH, 128], F32), "Bw4": ([128, NH, 128], F32),
    "bcmp": ([T // 128, ((T // 16) + 127) // 128, 128, NH, 128], F32), "mvfb": ([T // 128, 128, 2, 64], F32)}


def build_fused(T, L):
    kb = KB()
    nc, P = kb.nc, kb.P
    Tb = min(2048, T)
    I = "ExternalInput"
    xT = kb.dram("xT", [D, T], F32, I)
    w = {}
    for nm, shp in (("w_in", [L, D, INC]), ("gmix", [L, 128, KC]), ("bglu", [L, 128, 64]), ("posT", [L, 2, 128, 32]), ("w1", [L, 2, D, 128]), ("w2", [L, 2, 128, 128]),
                    ("wdw", [L, 128, KC, 31]), ("vecs", [L, 128, 6, KC]), ("w_co", [L, D, D]), ("w_ao", [L, D, D]), ("w_out", [L, D, D]),
                    ("w_fg", [L, D, DFF]), ("w_fu", [L, D, DFF]), ("w_fd", [L, DFF, D])):
        w[nm] = kb.dram(nm, shp, F32, I)
    cs = {k: kb.dram("c_" + k, shp, dt, I) for k, (shp, dt) in CONST_SPECS(T).items()}
    outT = kb.dram("outT", [D, T], F32, "ExternalOutput")
    N = "Internal"
    DBG = "ExternalOutput" if DEBUG else N
    projT = kb.dram("projT", [PROJ_ROWS, 32 + T], BF16, DBG)
    attnT = kb.dram("attnT", [D, T], BF16, DBG)
    xbuf = [kb.dram(f"xbuf{i}", [D, T], F32, N) for i in range(2)]
    scr = {"zT": kb.dram("zT", [D, Tb], BF16, DBG), "mbT": kb.dram("mbT", [D, Tb], F32, N), "mT": kb.dram("mT", [D, Tb], BF16, N),
           "x1T": kb.dram("x1T", [D, Tb], F32, DBG), "actT": kb.dram("actT", [DFF, Tb], BF16, N), "x2T": kb.dram("x2T", [D, Tb], F32, N)}
    with kb.es:
        with contextlib.ExitStack() as ph:
            z = kb.sb(ph, "zpad", [128, KC, 32], BF16)
            P.op("vector", lambda e: e.memset(z[:, :, :], 0.0), writes=["zpad"])
            kb.store("sync", projT[C_A:C_A + D, 0:32].rearrange("(c p) t -> p c t", p=128), z[:, :, :], "zpad", "d_c0", ("dram", "projT"))
            P.end_phase()
        src = xT
        for l in range(L):
            for t0 in range(0, T, Tb):
                emit_A(kb, Tb, src[:, t0:t0 + Tb], w["w_in"][l], w["gmix"][l], w["bglu"][l], projT[:, 32 + t0:32 + t0 + Tb])
            emit_B(kb, T, projT[:, 32:32 + T], attnT, {"w1": w["w1"][l], "w2": w["w2"][l], "posT": w["posT"][l]}, cs)
            final = (l == L - 1)
            dst = outT if final else xbuf[l % 2]
            for t0 in range(0, T, Tb):
                t = dict(scr)
                t.update({"xT": src[:, t0:t0 + Tb], "gluT": projT[C_A:C_A + D, t0:t0 + Tb + 32], "attnT": attnT[:, t0:t0 + Tb],
                          "gaT": projT[C_GA:C_GA + D, 32 + t0:32 + t0 + Tb], "gbT": projT[C_GB:C_GB + D, 32 + t0:32 + t0 + Tb],
                          "wdw": w["wdw"][l], "vecs": w["vecs"][l], "w_co": w["w_co"][l], "w_ao": w["w_ao"][l], "w_out": w["w_out"][l],
                          "w_fg": w["w_fg"][l], "w_fu": w["w_fu"][l], "w_fd": w["w_fd"][l], "outT": dst[:, t0:t0 + Tb]})
                emit_C(kb, Tb, final, t)
            src = dst
    return nc


def host_inputs(T, L, inp, cs):
    f = lambda a: np.ascontiguousarray(np.asarray(a, dtype=np.float32))
    d = {}
    d["w_in"] = f(inp["w_in"][:L])
    d["gmix"] = np.stack([lay_vec(inp["norm_mix"][l]) for l in range(L)])
    d["bglu"] = np.stack([lay_vec(inp["b_glu"][l]) for l in range(L)])
    d["posT"] = f(np.asarray(inp["cmp_pos"])[:L].transpose(0, 1, 3, 2))
    d["w1"] = f(inp["cmp_w1"][:L])
    d["w2"] = f(inp["cmp_w2"][:L])
    d["wdw"] = np.stack([lay_wdw(inp["w_dw"][l]) for l in range(L)])
    d["vecs"] = np.stack([np.stack([lay_vec(np.asarray(inp[k])[l] if k != "norm_final" else np.asarray(inp[k])) for k in ("b_dw", "conv_ln_g", "conv_ln_b", "b_conv_out", "norm_ffn", "norm_final")], axis=1) for l in range(L)])
    for k, n in (("w_co", "w_conv_out"), ("w_ao", "w_attn_out"), ("w_out", "w_out"), ("w_fg", "w_ffn_gate"), ("w_fu", "w_ffn_up"), ("w_fd", "w_ffn_down")):
        d[k] = f(inp[n][:L])
    for k, v in cs.items():
        d["c_" + k] = np.ascontiguousarray(v)
    return d


_FUSED = {}
DEBUG = False
LAST = {}


def run_fused(inp, L=2):
    x = np.asarray(inp["x"], dtype=np.float32)
    B, T, _ = x.shape
    key = (T, L)
    if key not in _FUSED:
        _FUSED[key] = build_fused(T, L)
    nc = _FUSED[key]
    cs = make_consts(T, inp["rel_bias"])
    common = host_inputs(T, L, inp, cs)
    ins = []
    for b in range(B):
        dd = dict(common)
        dd["xT"] = np.ascontiguousarray(x[b].T)
        ins.append(dd)
    res = run_bass_kernel_spmd(nc, ins, core_ids=list(range(B)))
    if DEBUG:
        LAST.update(res.results[0])
        LAST["all"] = res.results
    out = np.stack([np.ascontiguousarray(r["outT"].T) for r in res.results], axis=0)
    return out.astype(np.float32)


def kernel(**inputs):
    return run_fused(inputs, L=2)
```

```python
import contextlib
import numpy as np
import ml_dtypes
import concourse.bass as bass
import concourse.mybir as mybir
from concourse.bass_utils import run_bass_kernel_spmd

F32 = mybir.dt.float32
BF16 = mybir.dt.bfloat16
AF = mybir.ActivationFunctionType
ALU = mybir.AluOpType
ENGS = ["sync", "scalar", "vector", "gpsimd", "tensor"]

D = 4096
NH = 32
DH = 128
NG = 4
DFF = 11008
INC = 23648
KC = 32
EPS = 1e-6
NEG = -30000.0
C_Q, C_KV, C_A, C_G, C_GA, C_GB, C_BR = 0, 4096, 7168, 11264, 15360, 19456, 23552
PROJ_ROWS = 23680


class Prog:
    def __init__(self, nc, es):
        self.nc = nc
        self.es = es
        self.sems = {}
        self.cnt = {}
        self.waited = {e: {} for e in ENGS}
        self.streams = {e: [] for e in ENGS}
        self.bufs = {}
        self.last_out = []

    def _need(self, eng, dep, kind):
        if dep is None:
            return
        sem, val = dep
        if sem == "c_" + eng and (eng == "tensor" or kind != "RAW"):
            return
        if self.waited[eng].get(sem, 0) >= val:
            return
        self.waited[eng][sem] = val
        self.streams[eng].append(("wait", sem, val))

    def op(self, eng, fn, reads=(), writes=(), dma=None):
        for k in reads:
            st = self.bufs.get(k)
            if st is not None:
                self._need(eng, st[0], "RAW")
        for k in writes:
            st = self.bufs.get(k)
            if st is not None:
                self._need(eng, st[0], "WAW")
                for r in st[1].items():
                    self._need(eng, r, "WAR")
        if dma is not None:
            sem, inc = dma, 16
        else:
            sem, inc = "c_" + eng, 1
        val = self.cnt.get(sem, 0) + inc
        self.cnt[sem] = val
        self.streams[eng].append(("op", fn, sem, inc))
        tok = (sem, val)
        for k in reads:
            st = self.bufs.setdefault(k, [None, {}])
            st[1][sem] = val
        for k in writes:
            self.bufs[k] = [tok, {}]
        return tok

    def end_phase(self):
        for e in ENGS:
            for s, v in self.cnt.items():
                if self.waited[e].get(s, 0) < v:
                    self.waited[e][s] = v
                    self.streams[e].append(("wait", s, v))
        for s in self.cnt:
            if s not in self.sems:
                self.sems[s] = self.es.enter_context(self.nc.semaphore(s))
        sems = self.sems
        streams = self.streams

        def run(name):
            def body(e):
                for it in streams[name]:
                    if it[0] == "wait":
                        e.wait_ge(sems[it[1]], it[2])
                    else:
                        it[1](e).then_inc(sems[it[2]], it[3])
            return body

        with self.nc.Block() as block:
            block.sync(run("sync"))
            block.scalar(run("scalar"))
            block.vector(run("vector"))
            block.gpsimd(run("gpsimd"))
            block.tensor(run("tensor"))
        self.streams = {e: [] for e in ENGS}
        self.bufs = {}


class Rot:
    def __init__(self, items):
        self.items = items
        self.i = 0

    def next(self):
        it = self.items[self.i % len(self.items)]
        self.i += 1
        return it


class KB:
    def __init__(self):
        self.nc = bass.Bass("TRN2", target_bir_lowering=False)
        self.es = contextlib.ExitStack()
        self.P = Prog(self.nc, self.es)
        self.uid = 0

    def dram(self, name, shape, dt, kind):
        return self.nc.dram_tensor(name, list(shape), dt, kind=kind).ap()

    def sb(self, ph, name, shape, dt):
        self.uid += 1
        return ph.enter_context(self.nc.sbuf_tensor(f"{name}_{self.uid}", list(shape), dt))

    def psum(self, ph, n=8):
        self.uid += 1
        return [ph.enter_context(self.nc.psum_tensor(f"ps{i}_{self.uid}", [128, 512], F32)) for i in range(n)]

    def load(self, q, out_ap, in_ap, key, sem, reads=()):
        return self.P.op(q, lambda e: e.dma_start(out=out_ap, in_=in_ap), reads=list(reads), writes=[key], dma=sem)

    def store(self, q, out_ap, in_ap, key_src, sem, wkey):
        return self.P.op(q, lambda e: e.dma_start(out=out_ap, in_=in_ap), reads=[key_src], writes=[wkey], dma=sem)

    def stage_pool(self, ph, name, n, dt, width=512):
        items = []
        for i in range(n):
            t = self.sb(ph, f"{name}{i}", [128, width], dt)
            items.append((t, (name, i), f"d_{name}{i}"))
        return Rot(items)

    def rmsnorm(self, ps, srcT, Tc, gain, ones_bf, out_fn, TB=1024):
        P = self.P
        with contextlib.ExitStack() as ph:
            xs = [self.sb(ph, f"nxs{b}", [128, TB], F32) for b in range(2)]
            sq = [self.sb(ph, f"nsq{b}", [128, TB], BF16) for b in range(2)]
            rstd = self.sb(ph, "nrstd", [128, TB], F32)
            ntg = TB // 512
            it = 0
            for t0 in range(0, Tc, TB):
                for c in range(KC):
                    b = it % 2
                    it += 1
                    self.load("gpsimd", xs[b][:, :], srcT[c * 128:(c + 1) * 128, t0:t0 + TB], ("nxs", b), f"d_nxs{b}", reads=[("dram", srcT.tensor.name)])
                    P.op("scalar", (lambda b=b: lambda e: e.activation(out=sq[b][:, :], in_=xs[b][:, :], func=AF.Square))(), reads=[("nxs", b)], writes=[("nsq", b)])
                    for tg in range(ntg):
                        P.op("tensor", (lambda b=b, tg=tg, c=c: lambda e: e.matmul(ps[tg][:, :], lhsT=ones_bf[0][:, :], rhs=sq[b][:, tg * 512:(tg + 1) * 512], start=(c == 0), stop=(c == KC - 1)))(),
                             reads=[("nsq", b), ones_bf[1]], writes=[("ps", tg)])
                for tg in range(ntg):
                    sl = slice(tg * 512, (tg + 1) * 512)
                    P.op("vector", (lambda tg=tg, sl=sl: lambda e: e.tensor_scalar(out=rstd[:, sl], in0=ps[tg][:, :], scalar1=1.0 / D, scalar2=EPS, op0=ALU.mult, op1=ALU.add))(), reads=[("ps", tg)], writes=[("nrstd", tg)])
                    P.op("scalar", (lambda sl=sl: lambda e: e.activation(out=rstd[:, sl], in_=rstd[:, sl], func=AF.Sqrt))(), reads=[("nrstd", tg)], writes=[("nrstd", tg)])
                    P.op("vector", (lambda sl=sl: lambda e: e.reciprocal(out=rstd[:, sl], in_=rstd[:, sl]))(), reads=[("nrstd", tg)], writes=[("nrstd", tg)])
                for c in range(KC):
                    b = it % 2
                    it += 1
                    self.load("gpsimd", xs[b][:, :], srcT[c * 128:(c + 1) * 128, t0:t0 + TB], ("nxs", b), f"d_nxs{b}", reads=[("dram", srcT.tensor.name)])
                    out_fn(c, t0, xs[b], ("nxs", b), rstd, [("nrstd", tg) for tg in range(ntg)], TB)
            P.end_phase()

    def wbufs(self, ph, SEG=16):
        wst = [self.sb(ph, f"wst{b}", [128, SEG, 128], F32) for b in range(2)]
        wbf = [self.sb(ph, f"wbf{b}", [128, SEG, 128], BF16) for b in range(2)]
        return (wst, wbf, SEG, [0])

    def proj(self, wb, ps, res, res_keyfn, KCn, TW, chunks):
        P = self.P
        NTG = TW // 512
        wst, wbf, SEG, sic = wb
        segs = []
        for ci, (W, c0, ncol, epi) in enumerate(chunks):
            pb = ci % 2
            banks = [(ps[pb * 4 + tg], ("ps", pb * 4 + tg)) for tg in range(NTG)]
            for s0 in range(0, KCn, SEG):
                ns = min(SEG, KCn - s0)
                segs.append((ci, W, c0, ncol, epi, banks, s0, ns, s0 + ns >= KCn))

        def prefetch(i):
            ci, W, c0, ncol, epi, banks, s0, ns, last = segs[i]
            b = (sic[0] + i) % 2
            src = W[s0 * 128:(s0 + ns) * 128, c0:c0 + ncol].rearrange("(k p) n -> p k n", p=128)
            h = SEG // 2
            halves = [(0, min(ns, h))] + ([(h, ns)] if ns > h else [])
            for hh, (k0, k1) in enumerate(halves):
                P.op("sync", (lambda b=b, k0=k0, k1=k1, src=src, ncol=ncol: lambda e: e.dma_start(out=wst[b][:, k0:k1, 0:ncol], in_=src[:, k0:k1, :]))(),
                     writes=[("wst", b, hh)], dma=f"d_wst{b}{hh}")
                if hh == 0:
                    P.op("vector", (lambda b=b, k0=k0, k1=k1, ncol=ncol: lambda e: e.tensor_copy(out=wbf[b][:, k0:k1, 0:ncol], in_=wst[b][:, k0:k1, 0:ncol]))(),
                         reads=[("wst", b, hh)], writes=[("wbf", b, hh)])
                else:
                    P.op("scalar", (lambda b=b, k0=k0, k1=k1, ncol=ncol: lambda e: e.activation(out=wbf[b][:, k0:k1, 0:ncol], in_=wst[b][:, k0:k1, 0:ncol], func=AF.Copy))(),
                         reads=[("wst", b, hh)], writes=[("wbf", b, hh)])

        prefetch(0)
        for i, (ci, W, c0, ncol, epi, banks, s0, ns, last) in enumerate(segs):
            if i + 1 < len(segs):
                prefetch(i + 1)
            b = (sic[0] + i) % 2
            h = SEG // 2
            for k in range(ns):
                hh = 0 if k < h else 1
                kk = s0 + k
                for tg in range(NTG):
                    P.op("tensor", (lambda b=b, k=k, kk=kk, tg=tg, ncol=ncol, banks=banks: lambda e: e.matmul(
                        banks[tg][0][0:ncol, :], lhsT=wbf[b][:, k, 0:ncol], rhs=res[:, kk, tg * 512:(tg + 1) * 512],
                        start=(kk == 0), stop=(kk == KCn - 1)))(),
                        reads=[("wbf", b, hh), res_keyfn(kk)], writes=[banks[tg][1]])
            if last:
                epi(ci, banks)
        sic[0] += len(segs)

    def load_res(self, res, srcT, TW, t0=0, nk=KC, q="sync", kname="res", grp=4):
        tok = None
        for c0 in range(0, nk, grp):
            c1 = min(nk, c0 + grp)
            src = srcT[c0 * 128:c1 * 128, t0:t0 + TW].rearrange("(f p) t -> p f t", p=128)
            keys = [(kname, c) for c in range(c0, c1)]
            tok = self.P.op(q, (lambda c0=c0, c1=c1, src=src: lambda e: e.dma_start(out=res[:, c0:c1, :], in_=src))(),
                            reads=[("dram", srcT.tensor.name)], writes=keys, dma="d_hres")
        for c in range(nk):
            self.P.bufs[(kname, c)][0] = tok


def build_A(Tc):
    kb = KB()
    nc, P = kb.nc, kb.P
    xT = kb.dram("xT", [D, Tc], F32, "ExternalInput")
    w_in = kb.dram("w_in", [D, INC], F32, "ExternalInput")
    gmix = kb.dram("gmix", [128, KC], F32, "ExternalInput")
    bglu = kb.dram("bglu", [128, 64], F32, "ExternalInput")
    projT = kb.dram("projT", [PROJ_ROWS, Tc], BF16, "ExternalOutput")
    with kb.es:
        emit_A(kb, Tc, xT, w_in, gmix, bglu, projT)
    return nc


def emit_A(kb, Tc, xT, w_in, gmix, bglu, projT):
    nc, P = kb.nc, kb.P
    NTG = Tc // 512
    with contextlib.ExitStack() as outer:
        res = kb.sb(outer, "res", [128, KC, Tc], BF16)
        ones = kb.sb(outer, "ones", [128, 128], BF16)
        gm = kb.sb(outer, "gm", [128, KC], F32)
        bg = kb.sb(outer, "bg", [128, 64], F32)
        ps = kb.psum(outer)
        P.op("vector", lambda e: e.memset(ones[:, :], 1.0), writes=["ones"])
        kb.load("sync", gm[:, :], gmix[:, :], "gm", "d_c0")
        kb.load("sync", bg[:, :], bglu[:, :], "bg", "d_c1")

        def norm_out(c, t0, xs, xk, rstd, rks, TB):
            P.op("vector", lambda e: e.scalar_tensor_tensor(out=res[:, c, t0:t0 + TB], in0=xs[:, :], scalar=gm[:, c:c + 1], in1=rstd[:, :], op0=ALU.mult, op1=ALU.mult),
                 reads=[xk, "gm"] + rks, writes=[("res", c)])
        kb.rmsnorm(ps, xT, Tc, gm, (ones, "ones"), norm_out, TB=min(1024, Tc))

        with contextlib.ExitStack() as ph:
            stb = kb.stage_pool(ph, "stb", 4, BF16)
            stf = kb.stage_pool(ph, "stf", 2, F32)
            ast = [kb.sb(ph, f"ast{tg}", [128, 512], F32) for tg in range(NTG)]
            flip = [0]

            def mk_epi(kind, row0, bi=None):
                def epi(ci, banks, n=128):
                    def _one(tg, pt, pk):
                        dst = projT[row0:row0 + n, tg * 512:(tg + 1) * 512]
                        if kind == "a":
                            P.op("vector", lambda e: e.tensor_scalar(out=ast[tg][:, :], in0=pt[:, :], scalar1=bg[:, bi:bi + 1], scalar2=None, op0=ALU.add),
                                 reads=[pk, "bg"], writes=[("ast", tg)])
                            return
                        st, sk, ssem = stb.next()
                        if kind == "g":
                            sf, fk, _ = stf.next()
                            P.op("scalar", lambda e: e.activation(out=sf[:, :], in_=pt[:, :], func=AF.Sigmoid, bias=bg[:, 32 + bi:33 + bi], scale=1.0),
                                 reads=[pk, "bg"], writes=[fk])
                            P.op("gpsimd", lambda e: e.tensor_tensor(out=st[:, :], in0=ast[tg][:, :], in1=sf[:, :], op=ALU.mult),
                                 reads=[fk, ("ast", tg)], writes=[sk])
                        elif kind == "sig":
                            P.op("scalar", lambda e: e.activation(out=st[0:n, :], in_=pt[0:n, :], func=AF.Sigmoid), reads=[pk], writes=[sk])
                        else:
                            sc = DH ** -0.5 if kind == "q" else 1.0
                            flip[0] ^= 1
                            if flip[0]:
                                P.op("scalar", lambda e: e.activation(out=st[:, :], in_=pt[:, :], func=AF.Copy, scale=sc), reads=[pk], writes=[sk])
                            else:
                                P.op("vector", lambda e: e.tensor_scalar(out=st[:, :], in0=pt[:, :], scalar1=sc, scalar2=None, op0=ALU.mult), reads=[pk], writes=[sk])
                        kb.store("gpsimd", dst, st[0:n, :], sk, ssem, ("dram", "projT"))
                    for tg, (pt, pk) in enumerate(banks):
                        _one(tg, pt, pk)
                return epi

            chunks = []
            for i in range(32):
                chunks.append((w_in, C_Q + i * 128, 128, mk_epi("q", C_Q + i * 128)))
            for i in range(24):
                chunks.append((w_in, C_KV + i * 128, 128, mk_epi("copy", C_KV + i * 128)))
            for i in range(32):
                chunks.append((w_in, C_A + i * 128, 128, mk_epi("a", 0, i)))
                chunks.append((w_in, C_G + i * 128, 128, mk_epi("g", C_A + i * 128, i)))
            for i in range(32):
                chunks.append((w_in, C_GA + i * 128, 128, mk_epi("sig", C_GA + i * 128)))
            for i in range(32):
                chunks.append((w_in, C_GB + i * 128, 128, mk_epi("sig", C_GB + i * 128)))
            e96 = mk_epi("sig", C_BR)
            chunks.append((w_in, C_BR, 96, lambda ci, banks: e96(ci, banks, n=96)))
            kb.proj(kb.wbufs(ph), ps, res, lambda kk: ("res", kk), KC, Tc, chunks)
            P.end_phase()


def build_C(Tc, final):
    kb = KB()
    nc, P = kb.nc, kb.P
    t = {}
    t["xT"] = kb.dram("xT", [D, Tc], F32, "ExternalInput")
    t["gluT"] = kb.dram("gluT", [D, Tc + 32], BF16, "ExternalInput")
    t["attnT"] = kb.dram("attnT", [D, Tc], BF16, "ExternalInput")
    t["gaT"] = kb.dram("gaT", [D, Tc], BF16, "ExternalInput")
    t["gbT"] = kb.dram("gbT", [D, Tc], BF16, "ExternalInput")
    t["wdw"] = kb.dram("wdw", [128, KC, 31], F32, "ExternalInput")
    t["vecs"] = kb.dram("vecs", [128, 6, KC], F32, "ExternalInput")
    t["w_co"] = kb.dram("w_co", [D, D], F32, "ExternalInput")
    t["w_ao"] = kb.dram("w_ao", [D, D], F32, "ExternalInput")
    t["w_out"] = kb.dram("w_out", [D, D], F32, "ExternalInput")
    t["w_fg"] = kb.dram("w_fg", [D, DFF], F32, "ExternalInput")
    t["w_fu"] = kb.dram("w_fu", [D, DFF], F32, "ExternalInput")
    t["w_fd"] = kb.dram("w_fd", [DFF, D], F32, "ExternalInput")
    t["outT"] = kb.dram("outT", [D, Tc], F32, "ExternalOutput")
    t["identb"] = kb.dram("identb", [128, 128], BF16, "ExternalInput")
    t["zT"] = kb.dram("zT", [D, Tc], BF16, "Internal")
    t["mbT"] = kb.dram("mbT", [D, Tc], F32, "Internal")
    t["mT"] = kb.dram("mT", [D, Tc], BF16, "Internal")
    t["x1T"] = kb.dram("x1T", [D, Tc], F32, "Internal")
    t["actT"] = kb.dram("actT", [DFF, Tc], BF16, "Internal")
    if final:
        t["x2T"] = kb.dram("x2T", [D, Tc], F32, "Internal")
    with kb.es:
        emit_C(kb, Tc, final, t)
    return nc


def emit_C(kb, Tc, final, t):
    nc, P = kb.nc, kb.P
    NTG = Tc // 512
    NF = DFF // 128
    with contextlib.ExitStack() as outer:
        ones = kb.sb(outer, "ones", [128, 128], BF16)
        vec = kb.sb(outer, "vec", [128, 6, KC], F32)
        wdw = kb.sb(outer, "wdw", [128, KC, 31], F32)
        ps = kb.psum(outer)
        P.op("vector", lambda e: e.memset(ones[:, :], 1.0), writes=["ones"])
        kb.load("sync", vec[:, :, :], t["vecs"][:, :, :], "vec", "d_c0")
        kb.load("sync", wdw[:, :, :], t["wdw"][:, :, :], "wdw", "d_c1")
        P.end_phase()

        with contextlib.ExitStack() as ph:
            acc = kb.sb(ph, "cacc", [128, KC, 512], F32)
            gl = [kb.sb(ph, f"cgl{b}", [128, 544], BF16) for b in range(4)]
            cb = [kb.sb(ph, f"ccb{b}", [128, 512], BF16) for b in range(2)]
            cq = [kb.sb(ph, f"ccq{b}", [128, 512], BF16) for b in range(2)]
            mean = kb.sb(ph, "cmean", [128, 512], F32)
            rstd = kb.sb(ph, "crstd", [128, 512], F32)
            tmp = [kb.sb(ph, f"ctmp{b}", [128, 512], F32) for b in range(2)]
            zst = kb.stage_pool(ph, "zst", 2, BF16)
            identb = kb.sb(ph, "cident", [128, 128], BF16)
            dg = [kb.sb(ph, f"cdg{b}", [128, 31, 128], BF16) for b in range(2)]
            kb.load("sync", identb[:, :], t["identb"][:, :], "identb", "d_c2")
            for tg in range(NTG):
                for c4 in range(0, KC, 4):
                    for j in range(4):
                        c = c4 + j
                        kb.load("sync", gl[j][:, :], t["gluT"][c * 128:(c + 1) * 128, tg * 512:tg * 512 + 544], ("cgl", j), f"d_cgl{j}")
                    for j in range(4):
                        c = c4 + j
                        db = c % 2
                        pb = 2 + (c % 4)
                        P.op("vector", (lambda c=c, db=db: lambda e: e.tensor_tensor(out=dg[db][:, :, :], in0=identb[:, :].unsqueeze(1).broadcast_to([128, 31, 128]),
                                                                                 in1=wdw[:, c, :].unsqueeze(2).broadcast_to([128, 31, 128]), op=ALU.mult))(),
                             reads=["identb", "wdw"], writes=[("dg", db)])
                        for k in range(31):
                            P.op("tensor", (lambda j=j, db=db, pb=pb, k=k: lambda e: e.matmul(ps[pb][:, :], lhsT=dg[db][:, k, :], rhs=gl[j][:, 2 + k:514 + k], start=(k == 0), stop=(k == 30)))(),
                                 reads=[("dg", db), ("cgl", j)], writes=[("ps", pb)])
                        P.op("vector", (lambda c=c, pb=pb: lambda e: e.tensor_scalar(out=acc[:, c, :], in0=ps[pb][:, :], scalar1=vec[:, 0, c:c + 1], scalar2=None, op0=ALU.add))(),
                             reads=[("ps", pb), "vec"], writes=[("cacc", c)])
                    for j in range(4):
                        c = c4 + j
                        b = c % 2
                        P.op("gpsimd", (lambda c=c, b=b: lambda e: e.tensor_copy(out=cb[b][:, :], in_=acc[:, c, :]))(), reads=[("cacc", c)], writes=[("ccb", b)])
                        P.op("scalar", (lambda c=c, b=b: lambda e: e.activation(out=cq[b][:, :], in_=acc[:, c, :], func=AF.Square))(), reads=[("cacc", c)], writes=[("ccq", b)])
                        P.op("tensor", (lambda c=c, b=b: lambda e: e.matmul(ps[0][:, :], lhsT=ones[:, :], rhs=cb[b][:, :], start=(c == 0), stop=(c == KC - 1)))(), reads=[("ccb", b), "ones"], writes=[("ps", 0)])
                        P.op("tensor", (lambda c=c, b=b: lambda e: e.matmul(ps[1][:, :], lhsT=ones[:, :], rhs=cq[b][:, :], start=(c == 0), stop=(c == KC - 1)))(), reads=[("ccq", b), "ones"], writes=[("ps", 1)])
                P.op("vector", lambda e: e.tensor_scalar(out=mean[:, :], in0=ps[0][:, :], scalar1=1.0 / D, scalar2=None, op0=ALU.mult), reads=[("ps", 0)], writes=["cmean"])
                P.op("vector", lambda e: e.tensor_scalar(out=rstd[:, :], in0=ps[1][:, :], scalar1=1.0 / D, scalar2=EPS, op0=ALU.mult, op1=ALU.add), reads=[("ps", 1)], writes=["crstd"])
                P.op("vector", lambda e: e.tensor_tensor(out=tmp[0][:, :], in0=mean[:, :], in1=mean[:, :], op=ALU.mult), reads=["cmean"], writes=[("ctmp", 0)])
                P.op("vector", lambda e: e.tensor_tensor(out=rstd[:, :], in0=rstd[:, :], in1=tmp[0][:, :], op=ALU.subtract), reads=["crstd", ("ctmp", 0)], writes=["crstd"])
                P.op("scalar", lambda e: e.activation(out=rstd[:, :], in_=rstd[:, :], func=AF.Sqrt), reads=["crstd"], writes=["crstd"])
                P.op("vector", lambda e: e.reciprocal(out=rstd[:, :], in_=rstd[:, :]), reads=["crstd"], writes=["crstd"])
                for c in range(KC):
                    b = c % 2
                    st, sk, ssem = zst.next()
                    P.op("vector", (lambda c=c, b=b: lambda e: e.tensor_tensor(out=tmp[b][:, :], in0=acc[:, c, :], in1=mean[:, :], op=ALU.subtract))(), reads=[("cacc", c), "cmean"], writes=[("ctmp", b)])
                    P.op("gpsimd", (lambda b=b: lambda e: e.tensor_tensor(out=tmp[b][:, :], in0=tmp[b][:, :], in1=rstd[:, :], op=ALU.mult))(), reads=[("ctmp", b), "crstd"], writes=[("ctmp", b)])
                    P.op("scalar", (lambda c=c, b=b, st=st: lambda e: e.activation(out=st[:, :], in_=tmp[b][:, :], func=AF.Silu, scale=vec[:, 1, c:c + 1], bias=vec[:, 2, c:c + 1]))(), reads=[("ctmp", b), "vec"], writes=[sk])
                    kb.store("gpsimd", t["zT"][c * 128:(c + 1) * 128, tg * 512:(tg + 1) * 512], st[:, :], sk, ssem, ("dram", "zT"))
            P.end_phase()

        with contextlib.ExitStack() as ph2:
            res = kb.sb(ph2, "res", [128, KC, Tc], BF16)
            rk = lambda kk: ("res", kk)
            with contextlib.ExitStack() as ph:
                gin = kb.stage_pool(ph, "gin", 2, BF16)
                fin = kb.stage_pool(ph, "fin", 2, F32)
                stf = kb.stage_pool(ph, "stf", 2, F32)
                stb = kb.stage_pool(ph, "stb", 2, BF16)
                tf = kb.stage_pool(ph, "tf", 2, F32)

                def epi_b(n):
                    def epi(ci, banks):
                        def _one(tg, pt, pk):
                            g_t, gk, gsem = gin.next()
                            kb.load("gpsimd", g_t[:, :], t["gbT"][n * 128:(n + 1) * 128, tg * 512:(tg + 1) * 512], gk, gsem)
                            st, sk, ssem = stf.next()
                            P.op("vector", lambda e: e.scalar_tensor_tensor(out=st[:, :], in0=pt[:, :], scalar=vec[:, 3, n:n + 1], in1=g_t[:, :], op0=ALU.add, op1=ALU.mult), reads=[pk, gk, "vec"], writes=[sk])
                            kb.store("gpsimd", t["mbT"][n * 128:(n + 1) * 128, tg * 512:(tg + 1) * 512], st[:, :], sk, ssem, ("dram", "mbT"))
                        for tg, (pt, pk) in enumerate(banks):
                            _one(tg, pt, pk)
                    return epi

                def epi_a(n):
                    def epi(ci, banks):
                        def _one(tg, pt, pk):
                            g_t, gk, gsem = gin.next()
                            kb.load("gpsimd", g_t[:, :], t["gaT"][n * 128:(n + 1) * 128, tg * 512:(tg + 1) * 512], gk, gsem)
                            f_t, fk, fsem = fin.next()
                            kb.load("gpsimd", f_t[:, :], t["mbT"][n * 128:(n + 1) * 128, tg * 512:(tg + 1) * 512], fk, fsem, reads=[("dram", "mbT")])
                            tt, tk, _ = tf.next()
                            st, sk, ssem = stb.next()
                            P.op("vector", lambda e: e.tensor_tensor(out=tt[:, :], in0=pt[:, :], in1=g_t[:, :], op=ALU.mult), reads=[pk, gk], writes=[tk])
                            P.op("gpsimd", lambda e: e.tensor_tensor(out=st[:, :], in0=tt[:, :], in1=f_t[:, :], op=ALU.add), reads=[tk, fk], writes=[sk])
                            kb.store("gpsimd", t["mT"][n * 128:(n + 1) * 128, tg * 512:(tg + 1) * 512], st[:, :], sk, ssem, ("dram", "mT"))
                        for tg, (pt, pk) in enumerate(banks):
                            _one(tg, pt, pk)
                    return epi

                def epi_o(n):
                    def epi(ci, banks):
                        def _one(tg, pt, pk):
                            f_t, fk, fsem = fin.next()
                            kb.load("gpsimd", f_t[:, :], t["xT"][n * 128:(n + 1) * 128, tg * 512:(tg + 1) * 512], fk, fsem)
                            st, sk, ssem = stf.next()
                            P.op("vector", lambda e: e.tensor_tensor(out=st[:, :], in0=pt[:, :], in1=f_t[:, :], op=ALU.add), reads=[pk, fk], writes=[sk])
                            kb.store("gpsimd", t["x1T"][n * 128:(n + 1) * 128, tg * 512:(tg + 1) * 512], st[:, :], sk, ssem, ("dram", "x1T"))
                        for tg, (pt, pk) in enumerate(banks):
                            _one(tg, pt, pk)
                    return epi

                wb = kb.wbufs(ph)
                kb.load_res(res, t["zT"], Tc)
                kb.proj(wb, ps, res, rk, KC, Tc, [(t["w_co"], n * 128, 128, epi_b(n)) for n in range(KC)])
                P.end_phase()
                kb.load_res(res, t["attnT"], Tc)
                kb.proj(wb, ps, res, rk, KC, Tc, [(t["w_ao"], n * 128, 128, epi_a(n)) for n in range(KC)])
                P.end_phase()
                kb.load_res(res, t["mT"], Tc)
                kb.proj(wb, ps, res, rk, KC, Tc, [(t["w_out"], n * 128, 128, epi_o(n)) for n in range(KC)])
                P.end_phase()

            def norm_out(c, t0, xs, xk, rstd, rks, TB):
                P.op("vector", lambda e: e.scalar_tensor_tensor(out=res[:, c, t0:t0 + TB], in0=xs[:, :], scalar=vec[:, 4, c:c + 1], in1=rstd[:, :], op0=ALU.mult, op1=ALU.mult),
                     reads=[xk, "vec"] + rks, writes=[("res", c)])
            kb.rmsnorm(ps, t["x1T"], Tc, None, (ones, "ones"), norm_out, TB=min(1024, Tc))

            with contextlib.ExitStack() as ph:
                gs = [kb.sb(ph, f"gs{tg}", [128, 512], F32) for tg in range(NTG)]
                stb = kb.stage_pool(ph, "stb", 4, BF16)

                def epi_gate(f):
                    def epi(ci, banks):
                        def _one(tg, pt, pk):
                            P.op("scalar", lambda e: e.activation(out=gs[tg][:, :], in_=pt[:, :], func=AF.Silu), reads=[pk], writes=[("gs", tg)])
                        for tg, (pt, pk) in enumerate(banks):
                            _one(tg, pt, pk)
                    return epi

                def epi_up(f):
                    def epi(ci, banks):
                        def _one(tg, pt, pk):
                            st, sk, ssem = stb.next()
                            P.op("vector", lambda e: e.tensor_tensor(out=st[:, :], in0=pt[:, :], in1=gs[tg][:, :], op=ALU.mult), reads=[pk, ("gs", tg)], writes=[sk])
                            kb.store("gpsimd", t["actT"][f * 128:(f + 1) * 128, tg * 512:(tg + 1) * 512], st[:, :], sk, ssem, ("dram", "actT"))
                        for tg, (pt, pk) in enumerate(banks):
                            _one(tg, pt, pk)
                    return epi
                chunks = []
                for f in range(NF):
                    chunks.append((t["w_fg"], f * 128, 128, epi_gate(f)))
                    chunks.append((t["w_fu"], f * 128, 128, epi_up(f)))
                kb.proj(kb.wbufs(ph), ps, res, rk, KC, Tc, chunks)
                P.end_phase()

        dstT = t["x2T"] if final else t["outT"]
        with contextlib.ExitStack() as ph:
            act = kb.sb(ph, "act", [128, NF, 512], BF16)
            fin = kb.stage_pool(ph, "fin", 2, F32)
            stf = kb.stage_pool(ph, "stf", 2, F32)
            wb = kb.wbufs(ph)
            for tg in range(NTG):
                kb.load_res(act, t["actT"], 512, t0=tg * 512, nk=NF, kname="act", grp=8)

                def epi_d(n, tg=tg):
                    def epi(ci, banks):
                        pt, pk = banks[0]
                        f_t, fk, fsem = fin.next()
                        kb.load("gpsimd", f_t[:, :], t["x1T"][n * 128:(n + 1) * 128, tg * 512:(tg + 1) * 512], fk, fsem, reads=[("dram", "x1T")])
                        st, sk, ssem = stf.next()
                        P.op("vector", lambda e: e.tensor_tensor(out=st[:, :], in0=pt[:, :], in1=f_t[:, :], op=ALU.add), reads=[pk, fk], writes=[sk])
                        kb.store("gpsimd", dstT[n * 128:(n + 1) * 128, tg * 512:(tg + 1) * 512], st[:, :], sk, ssem, ("dram", "dst"))
                    return epi
                kb.proj(wb, ps, act, lambda kk: ("act", kk), NF, 512, [(t["w_fd"], n * 128, 128, epi_d(n)) for n in range(KC)])
            P.end_phase()

        if final:
            with contextlib.ExitStack() as ph:
                stf = kb.stage_pool(ph, "stf", 2, F32, width=min(1024, Tc))

                def norm_out2(c, t0, xs, xk, rstd, rks, TB):
                    st, sk, ssem = stf.next()
                    P.op("vector", lambda e: e.scalar_tensor_tensor(out=st[:, :], in0=xs[:, :], scalar=vec[:, 5, c:c + 1], in1=rstd[:, :], op0=ALU.mult, op1=ALU.mult),
                         reads=[xk, "vec"] + rks, writes=[sk])
                    kb.store("gpsimd", t["outT"][c * 128:(c + 1) * 128, t0:t0 + TB], st[:, :], sk, ssem, ("dram", "outT"))
                kb.rmsnorm(ps, t["x2T"], Tc, None, (ones, "ones"), norm_out2, TB=min(1024, Tc))


def lay_vec(v):
    v = np.asarray(v, dtype=np.float32)
    n = v.shape[0] // 128
    return np.ascontiguousarray(v.reshape(n, 128).T)


def lay_wdw(w):
    return np.ascontiguousarray(np.asarray(w, np.float32).T.reshape(KC, 128, 31).transpose(1, 0, 2))


_NC_CACHE = {}


def get_nc(kind, *args):
    key = (kind,) + tuple(args)
    if key not in _NC_CACHE:
        _NC_CACHE[key] = {"A": build_A, "C": build_C}[kind](*args)
    return _NC_CACHE[key]


def run_A(xT_list, w_in_l, norm_mix_l, b_glu_l):
    Tc = xT_list[0].shape[1]
    nc = get_nc("A", Tc)
    gm = lay_vec(norm_mix_l)
    bg = lay_vec(b_glu_l)
    w = np.ascontiguousarray(w_in_l, dtype=np.float32)
    ins = [{"xT": np.ascontiguousarray(xT), "w_in": w, "gmix": gm, "bglu": bg} for xT in xT_list]
    res = run_bass_kernel_spmd(nc, ins, core_ids=list(range(len(ins))))
    return [r["projT"] for r in res.results]


def run_C(final, xT_list, gluT_list, attnT_list, gaT_list, gbT_list, W):
    Tc = xT_list[0].shape[1]
    nc = get_nc("C", Tc, final)
    vecs = np.ascontiguousarray(np.stack([lay_vec(W[k]) for k in ("b_dw", "conv_ln_g", "conv_ln_b", "b_conv_out", "norm_ffn", "norm_final")], axis=1))
    common = {"wdw": lay_wdw(W["w_dw"]), "vecs": vecs}
    for k, n in (("w_co", "w_conv_out"), ("w_ao", "w_attn_out"), ("w_out", "w_out"), ("w_fg", "w_ffn_gate"), ("w_fu", "w_ffn_up"), ("w_fd", "w_ffn_down")):
        common[k] = np.ascontiguousarray(W[n], dtype=np.float32)
    ins = []
    for i in range(len(xT_list)):
        d = dict(common)
        d.update({"xT": np.ascontiguousarray(xT_list[i]), "gluT": np.ascontiguousarray(gluT_list[i]), "attnT": np.ascontiguousarray(attnT_list[i]),
                  "gaT": np.ascontiguousarray(gaT_list[i]), "gbT": np.ascontiguousarray(gbT_list[i])})
        ins.append(d)
    res = run_bass_kernel_spmd(nc, ins, core_ids=list(range(len(ins))))
    return [r["outT"] for r in res.results]


def emit_B(kb, T, projT, attnT, cw, cs):
    nc, P = kb.nc, kb.P
    NQT = T // 128
    NCMP = T // 16 - 1
    NCT = (NCMP + 1 + 127) // 128
    NCP = NCT * 128
    with contextlib.ExitStack() as outer:
        identf = kb.sb(outer, "identf", [128, 128], F32)
        identb = kb.sb(outer, "identb", [128, 128], BF16)
        eext = kb.sb(outer, "eext", [128, NQT, 128], BF16)
        b31 = kb.sb(outer, "b31", [128, NH], F32)
        b31h = kb.sb(outer, "b31h", [128, NH], BF16)
        b31hf = kb.sb(outer, "b31hf", [128, NH], F32)
        b31l = kb.sb(outer, "b31l", [128, NH], BF16)
        kcmpT = [kb.sb(outer, f"kcmpT{g}", [128, NCP], BF16) for g in range(NG)]
        vcaug = [kb.sb(outer, f"vcaug{g}", [128, NCT, 193], BF16) for g in range(NG)]
        kb.load("sync", identf[:, :], cs["identf"][:, :], "identf", "d_c0")
        kb.load("sync", identb[:, :], cs["identb"][:, :], "identb", "d_c1")
        kb.load("sync", eext[0:97, :, :], cs["eext"][:, :, :], "eext", "d_c2")
        kb.load("sync", b31[:, :], cs["b31"][:, :], "b31", "d_c3")
        for g in range(NG):
            kb.load("sync", vcaug[g][:, :, 128:193], cs["vcc"].rearrange("c n k -> n c k"), ("vcaug", g), f"d_sres{g}")
        P.op("vector", lambda e: e.tensor_copy(out=b31h[:, :], in_=b31[:, :]), reads=["b31"], writes=["b31h"])
        P.op("vector", lambda e: e.tensor_copy(out=b31hf[:, :], in_=b31h[:, :]), reads=["b31h"], writes=["b31hf"])
        P.op("vector", lambda e: e.tensor_tensor(out=b31l[:, :], in0=b31[:, :], in1=b31hf[:, :], op=ALU.subtract), reads=["b31", "b31hf"], writes=["b31l"])

        with contextlib.ExitStack() as ph:
            ps = kb.psum(ph)
            xin = [kb.sb(ph, f"xin{b}", [128, T], BF16) for b in range(2)]
            w1s = kb.sb(ph, "w1s", [128, 32, 128], F32)
            w1b = kb.sb(ph, "w1b", [128, 32, 128], BF16)
            w2s = kb.sb(ph, "w2s", [128, 128], F32)
            w2b = kb.sb(ph, "w2b", [128, 128], BF16)
            poss = kb.sb(ph, "poss", [128, 32], F32)
            posb = kb.sb(ph, "posb", [128, 32], BF16)
            pbias = kb.sb(ph, "pbias", [128, 1], F32)
            u = [kb.sb(ph, f"cu{i}", [128, NCP], F32) for i in range(4)]
            geT = kb.sb(ph, "geT", [128, NCP], BF16)
            P.op("vector", lambda e: e.memset(geT[:, :], 0.0), writes=["geT"])
            it = 0
            for kind in range(2):
                kb.load("sync", w1s[:, :, :], cw["w1"][kind].rearrange("(j d) o -> d j o", d=128), "w1s", "d_wst00")
                kb.load("sync", w2s[:, :], cw["w2"][kind], "w2s", "d_wst01")
                kb.load("sync", poss[:, :], cw["posT"][kind], "poss", "d_wst10")
                P.op("vector", lambda e: e.tensor_copy(out=w1b[:, :, :], in_=w1s[:, :, :]), reads=["w1s"], writes=["w1b"])
                P.op("vector", lambda e: e.tensor_copy(out=w2b[:, :], in_=w2s[:, :]), reads=["w2s"], writes=["w2b"])
                P.op("vector", lambda e: e.tensor_copy(out=posb[:, :], in_=poss[:, :]), reads=["poss"], writes=["posb"])
                for j in range(32):
                    P.op("tensor", (lambda j=j: lambda e: e.matmul(ps[6][:, 0:1], lhsT=w1b[:, j, :], rhs=posb[:, j:j + 1], start=(j == 0), stop=(j == 31)))(), reads=["w1b", "posb"], writes=[("ps", 6)])
                P.op("vector", lambda e: e.tensor_copy(out=pbias[:, :], in_=ps[6][:, 0:1]), reads=[("ps", 6)], writes=["pbias"])
                for g in range(NG):
                    b = it % 2
                    it += 1
                    r0 = C_KV + kind * 512 + g * 128
                    kb.load("gpsimd", xin[b][:, :], projT[r0:r0 + 128, :], ("xin", b), f"d_nxs{b}", reads=[("dram", "projT")])
                    x3 = xin[b][:, :].rearrange("p (n s) -> p n s", s=16)
                    for j in range(32):
                        P.op("tensor", (lambda j=j, x3=x3: lambda e: e.matmul(ps[0][:, 0:NCMP], lhsT=w1b[:, j, :], rhs=x3[:, j // 16:j // 16 + NCMP, j % 16], start=(j == 0), stop=(j == 31)))(),
                             reads=["w1b", ("xin", b)], writes=[("ps", 0)])
                    P.op("vector", lambda e: e.tensor_scalar(out=u[0][:, 0:NCMP], in0=ps[0][:, 0:NCMP], scalar1=pbias[:, 0:1], scalar2=None, op0=ALU.add), reads=[("ps", 0), "pbias"], writes=[("cu", 0)])
                    P.op("vector", lambda e: e.tensor_tensor(out=u[1][:, 0:NCMP], in0=u[0][:, 0:NCMP], in1=u[0][:, 0:NCMP], op=ALU.mult), reads=[("cu", 0)], writes=[("cu", 1)])
                    P.op("vector", lambda e: e.tensor_scalar(out=u[1][:, 0:NCMP], in0=u[1][:, 0:NCMP], scalar1=0.044715, scalar2=1.0, op0=ALU.mult, op1=ALU.add), reads=[("cu", 1)], writes=[("cu", 1)])
                    P.op("vector", lambda e: e.tensor_tensor(out=u[2][:, 0:NCMP], in0=u[1][:, 0:NCMP], in1=u[0][:, 0:NCMP], op=ALU.mult), reads=[("cu", 1), ("cu", 0)], writes=[("cu", 2)])
                    P.op("scalar", lambda e: e.activation(out=u[3][:, 0:NCMP], in_=u[2][:, 0:NCMP], func=AF.Sigmoid, scale=1.5957691216057308), reads=[("cu", 2)], writes=[("cu", 3)])
                    P.op("vector", lambda e: e.tensor_tensor(out=geT[:, 0:NCMP], in0=u[0][:, 0:NCMP], in1=u[3][:, 0:NCMP], op=ALU.mult), reads=[("cu", 0), ("cu", 3)], writes=["geT"])
                    if kind == 0:
                        P.op("tensor", lambda e: e.matmul(ps[1][:, 0:NCP], lhsT=w2b[:, :], rhs=geT[:, :], start=True, stop=True), reads=["w2b", "geT"], writes=[("ps", 1)])
                        P.op("scalar", (lambda g=g: lambda e: e.activation(out=kcmpT[g][:, :], in_=ps[1][:, 0:NCP], func=AF.Copy))(), reads=[("ps", 1)], writes=[("kcmpT", g)])
                    else:
                        for ct in range(NCT):
                            P.op("tensor", (lambda ct=ct: lambda e: e.matmul(ps[2 + ct][:, 0:128], lhsT=geT[:, ct * 128:(ct + 1) * 128], rhs=w2b[:, :], start=True, stop=True))(), reads=["w2b", "geT"], writes=[("ps", 2 + ct)])
                            P.op("scalar", (lambda g=g, ct=ct: lambda e: e.activation(out=vcaug[g][:, ct, 0:128], in_=ps[2 + ct][:, 0:128], func=AF.Copy))(), reads=[("ps", 2 + ct)], writes=[("vcaug", g)])
            P.end_phase()

        for g in range(NG):
            with contextlib.ExitStack() as gph:
                qT = kb.sb(gph, "qT", [128, 8, T], BF16)
                ksT = kb.sb(gph, "ksT", [128, T], BF16)
                kwT = kb.sb(gph, "kwT", [128, T], BF16)
                Vs = kb.sb(gph, "Vs", [128, NQT, 129], BF16)
                Vw = kb.sb(gph, "Vw", [128, NQT, 129], BF16)
                brg = kb.sb(gph, "brg", [128, NQT, 24], F32)
                Bn = [kb.sb(gph, f"Bn{i}", [128, 8, 128], F32) for i in range(3)]
                rbuf = kb.sb(gph, "rbuf", [128, 8, 128], BF16)
                with contextlib.ExitStack() as ph:
                    vT = [kb.sb(ph, f"vT{i}", [128, T], BF16) for i in range(2)]
                    gT = kb.sb(ph, "gT", [128, T], BF16)
                    kb.uid += 1
                    self_ps = ph.enter_context(nc.psum_tensor(f"psb_{g}_{kb.uid}", [128, 8, 128], BF16))
                    kb.load("sync", qT[:, :, :], projT[C_Q + g * 1024:C_Q + (g + 1) * 1024, :].rearrange("(h d) t -> d h t", d=128), "qT", "d_c0", reads=[("dram", "projT")])
                    kb.load("sync", ksT[:, :], projT[C_KV + 1024 + g * 128:C_KV + 1024 + (g + 1) * 128, :], "ksT", "d_c1", reads=[("dram", "projT")])
                    kb.load("sync", kwT[:, :], projT[C_KV + 2048 + g * 128:C_KV + 2048 + (g + 1) * 128, :], "kwT", "d_c2", reads=[("dram", "projT")])
                    kb.load("gpsimd", vT[0][:, :], projT[C_KV + 1536 + g * 128:C_KV + 1536 + (g + 1) * 128, :], ("vT", 0), "d_nxs0", reads=[("dram", "projT")])
                    kb.load("gpsimd", vT[1][:, :], projT[C_KV + 2560 + g * 128:C_KV + 2560 + (g + 1) * 128, :], ("vT", 1), "d_nxs1", reads=[("dram", "projT")])
                    kb.load("gpsimd", gT[0:24, :], projT[C_BR + g * 24:C_BR + (g + 1) * 24, :], "gT", "d_gT", reads=[("dram", "projT")])
                    for i, nm in enumerate(("B0", "B1", "Bw4")):
                        kb.load("sync", Bn[i][:, :, :], cs[nm][:, g * 8:(g + 1) * 8, :], ("Bn", i), f"d_sres{i}")
                    P.op("vector", lambda e: e.memset(rbuf[:, :, :], 0.0), writes=["rbuf"])
                    P.op("vector", lambda e: e.memset(Vs[:, :, 128:129], 1.0), writes=["Vs1"])
                    P.op("vector", lambda e: e.memset(Vw[:, :, 128:129], 1.0), writes=["Vw1"])
                    P.op("vector", (lambda g=g: lambda e: e.tensor_copy(out=rbuf[64:65, :, :], in_=b31h[64:65, g * 8:(g + 1) * 8].unsqueeze(2).broadcast_to([1, 8, 128])))(), reads=["b31h", "rbuf"], writes=["rbuf"])
                    P.op("vector", (lambda g=g: lambda e: e.tensor_copy(out=rbuf[96:97, :, :], in_=b31l[96:97, g * 8:(g + 1) * 8].unsqueeze(2).broadcast_to([1, 8, 128])))(), reads=["b31l", "rbuf"], writes=["rbuf"])
                    for vi, Vt in enumerate((Vs, Vw)):
                        for k0 in range(0, NQT, 8):
                            for kk in range(8):
                                kt = k0 + kk
                                P.op("tensor", (lambda vi=vi, kt=kt, kk=kk: lambda e: e.transpose(out=self_ps[:, kk, :], in_=vT[vi][:, kt * 128:(kt + 1) * 128], identity=identb[:, :]))(),
                                     reads=[("vT", vi), "identb"], writes=["psb"])
                            P.op("scalar", (lambda Vt=Vt, k0=k0: lambda e: e.activation(out=Vt[:, k0:k0 + 8, 0:128], in_=self_ps[:, :, :], func=AF.Copy))(), reads=["psb"], writes=[("V", vi)])
                    for k0 in range(0, NQT, 8):
                        for kk in range(8):
                            kt = k0 + kk
                            P.op("tensor", (lambda kt=kt, kk=kk: lambda e: e.transpose(out=self_ps[:, kk, 0:24], in_=gT[0:24, kt * 128:(kt + 1) * 128], identity=identb[0:24, 0:24]))(),
                                 reads=["gT", "identb"], writes=["psb"])
                        P.op("vector", (lambda k0=k0: lambda e: e.tensor_copy(out=brg[:, k0:k0 + 8, :], in_=self_ps[:, :, 0:24]))(), reads=["psb"], writes=["brg"])
                    P.end_phase()

                with contextlib.ExitStack() as ph:
                    ps = kb.psum(ph)
                    bc = [[kb.sb(ph, f"bc{b}{ct}", [128, 8, 128], F32) for ct in range(NCT)] for b in range(2)]
                    mvfb = [kb.sb(ph, f"mvfb{b}", [128, 2, 64], F32) for b in range(2)]
                    ssb = [kb.sb(ph, f"ssb{i}", [128, 512], F32) for i in range(2)]
                    pts = [kb.sb(ph, f"pt{i}", [128, 512], BF16) for i in range(4)]
                    ocmp = kb.sb(ph, "ocmp", [128, 8, 128], F32)
                    otok = kb.sb(ph, "otok", [128, 8, 128], F32)
                    sm = kb.sb(ph, "sm", [128, 8, 8], F32)
                    imp = kb.sb(ph, "imp", [128, 64], F32)
                    imp2 = kb.sb(ph, "imp2", [128, 64], F32)
                    m8 = kb.sb(ph, "m8", [128, 16], F32)
                    selb = kb.sb(ph, "selb", [128, 64], F32)
                    ost = [kb.sb(ph, f"ost{i}", [128, 4, 128], BF16) for i in range(2)]
                    cnts = {"s": 0, "p": 0, "o": 0}

                    def qk(lhsT_ap, lkey, quad, t0, extra, near_bias):
                        si = cnts["s"] % 2
                        cnts["s"] += 1
                        sp, spk = ps[si], ("ps", si)
                        rhs = qT[:, quad * 4:(quad + 1) * 4, t0:t0 + 128]
                        P.op("tensor", lambda e: e.matmul(sp[:, :].rearrange("p (a b) -> p a b", a=4), lhsT=lhsT_ap, rhs=rhs, start=True, stop=(extra is None)), reads=[lkey, "qT"], writes=[spk])
                        if extra is not None:
                            el, er, ekeys = extra
                            P.op("tensor", lambda e: e.matmul(sp[:, :].rearrange("p (a b) -> p a b", a=4), lhsT=el, rhs=er, start=False, stop=True), reads=ekeys, writes=[spk])
                        pi = cnts["p"] % 4
                        cnts["p"] += 1
                        pt, ptk = pts[pi], ("pt", pi)
                        if near_bias is not None:
                            btile, bkey = near_bias
                            sb_, sbk = ssb[si], ("ssb", si)
                            P.op("vector", lambda e: e.tensor_tensor(out=sb_[:, :], in0=sp[:, :], in1=btile, op=ALU.add), reads=[spk, bkey], writes=[sbk])
                            P.op("scalar", lambda e: e.activation(out=pt[:, :], in_=sb_[:, :], func=AF.Exp), reads=[sbk], writes=[ptk])
                        else:
                            P.op("scalar", lambda e: e.activation(out=pt[:, :], in_=sp[:, :], func=AF.Exp), reads=[spk], writes=[ptk])
                        return pt, ptk

                    def pv(pt, ptk, obase, width, rhs_ap, rkey, first, last):
                        for hq in range(4):
                            bi = obase + hq // 2
                            c0 = (hq % 2) * 256
                            P.op("tensor", (lambda hq=hq, bi=bi, c0=c0: lambda e: e.matmul(ps[bi][:, c0:c0 + width], lhsT=pt[:, hq * 128:(hq + 1) * 128], rhs=rhs_ap, start=(first and hq % 2 == 0), stop=last))(),
                                 reads=[ptk, rkey], writes=[("ps", bi, hq % 2)])

                    def _do_qt(qt):
                        t0 = qt * 128
                        bb = qt % 2
                        nct = min(NCT, (8 * qt + 6) // 128 + 1)
                        for ct in range(nct):
                            kb.load("gpsimd", bc[bb][ct][:, :, :], cs["bcmp"][qt, ct, :, g * 8:(g + 1) * 8, :], ("bc", bb, ct), f"d_bc{bb}{ct}")
                        kb.load("gpsimd", mvfb[bb][:, :, :], cs["mvfb"][qt], ("mvfb", bb), f"d_mvfb{bb}")
                        for quad in range(2):
                            for ct in range(nct):
                                pt, ptk = qk(kcmpT[g][:, ct * 128:(ct + 1) * 128], ("kcmpT", g), quad, t0, None,
                                             (bc[bb][ct][:, quad * 4:(quad + 1) * 4, :].rearrange("p a b -> p (a b)"), ("bc", bb, ct)))
                                pv(pt, ptk, 2 + quad * 2, 193, vcaug[g][:, ct, :], ("vcaug", g), ct == 0, ct == nct - 1)
                            for hq in range(4):
                                hl = quad * 4 + hq
                                bi = 2 + quad * 2 + hq // 2
                                c0 = (hq % 2) * 256
                                ok = ("ps", bi, hq % 2)

                                def _cmp_epi(hl=hl, bi=bi, c0=c0, ok=ok):
                                    P.op("vector", lambda e: e.tensor_scalar(out=sm[:, hl, 0:1], in0=ps[bi][:, c0 + 192:c0 + 193], scalar1=1.0e-30, scalar2=None, op0=ALU.max), reads=[ok], writes=[("sm", hl)])
                                    P.op("vector", lambda e: e.reciprocal(out=sm[:, hl, 0:1], in_=sm[:, hl, 0:1]), reads=[("sm", hl)], writes=[("sm", hl)])
                                    P.op("vector", lambda e: e.tensor_tensor(out=sm[:, hl, 1:2], in0=sm[:, hl, 0:1], in1=brg[:, qt, hl * 3:hl * 3 + 1], op=ALU.mult), reads=[("sm", hl), "brg"], writes=[("sm", hl)])
                                    P.op("vector", lambda e: e.tensor_scalar(out=ocmp[:, hl, :], in0=ps[bi][:, c0:c0 + 128], scalar1=sm[:, hl, 1:2], scalar2=None, op0=ALU.mult), reads=[ok, ("sm", hl)], writes=[("ocmp", hl)])
                                    if hl == 0:
                                        P.op("vector", lambda e: e.tensor_scalar(out=imp[:, :], in0=ps[bi][:, c0 + 128:c0 + 192], scalar1=sm[:, hl, 0:1], scalar2=None, op0=ALU.mult), reads=[ok, ("sm", hl)], writes=["imp"])
                                    else:
                                        P.op("vector", lambda e: e.scalar_tensor_tensor(out=imp[:, :], in0=ps[bi][:, c0 + 128:c0 + 192], scalar=sm[:, hl, 0:1], in1=imp[:, :], op0=ALU.mult, op1=ALU.add), reads=[ok, ("sm", hl), "imp"], writes=["imp"])
                                _cmp_epi()
                        P.op("vector", lambda e: e.tensor_tensor(out=imp[:, :], in0=imp[:, :], in1=mvfb[bb][:, 0, :], op=ALU.mult), reads=["imp", ("mvfb", bb)], writes=["imp"])
                        P.op("vector", lambda e: e.tensor_tensor(out=imp[:, :], in0=imp[:, :], in1=mvfb[bb][:, 1, :], op=ALU.add), reads=["imp", ("mvfb", bb)], writes=["imp"])
                        P.op("vector", lambda e: e.max(out=m8[:, 0:8], in_=imp[:, :]), reads=["imp"], writes=["m8a"])
                        P.op("vector", lambda e: e.match_replace(out=imp2[:, :], in_to_replace=m8[:, 0:8], in_values=imp[:, :], imm_value=-3.0e9), reads=["imp", "m8a"], writes=["imp2"])
                        P.op("vector", lambda e: e.max(out=m8[:, 8:16], in_=imp2[:, :]), reads=["imp2"], writes=["m8b"])
                        P.op("vector", lambda e: e.tensor_scalar(out=selb[:, :], in0=imp[:, :], scalar1=m8[:, 15:16], scalar2=NEG, op0=ALU.is_lt, op1=ALU.mult), reads=["imp", "m8b"], writes=["selb"])
                        P.op("tensor", lambda e: e.transpose(out=ps[6][0:64, 0:128], in_=selb[:, :], identity=identf[:, :]), reads=["selb", "identf"], writes=[("ps", 6)])
                        P.op("vector", lambda e: e.tensor_copy(out=rbuf[0:64, :, :], in_=ps[6][0:64, 0:128].unsqueeze(1).broadcast_to([64, 8, 128])), reads=[("ps", 6)], writes=["rbuf"])
                        for quad in range(2):
                            rq = rbuf[:, quad * 4:(quad + 1) * 4, :]
                            tiles = []
                            for o in range(qt + 1):
                                kt = qt - o
                                if o == 0:
                                    extra, nb = None, (Bn[0][:, quad * 4:(quad + 1) * 4, :].rearrange("p a b -> p (a b)"), ("Bn", 0))
                                elif o == 1:
                                    extra, nb = (eext[0:64, kt, :], rq[0:64], ["eext", "rbuf"]), (Bn[1][:, quad * 4:(quad + 1) * 4, :].rearrange("p a b -> p (a b)"), ("Bn", 1))
                                else:
                                    extra, nb = (eext[0:97, kt, :], rq[0:97], ["eext", "rbuf"]), None
                                tiles.append((ksT[:, kt * 128:(kt + 1) * 128], "ksT", extra, nb, 2, Vs[:, kt, :], ("V", 0), o == 0, o == qt))
                            nw = min(4, qt)
                            for o in range(nw + 1):
                                kt = qt - o
                                if o == 0:
                                    extra, nb = None, (Bn[0][:, quad * 4:(quad + 1) * 4, :].rearrange("p a b -> p (a b)"), ("Bn", 0))
                                elif o == 1:
                                    extra, nb = None, (Bn[1][:, quad * 4:(quad + 1) * 4, :].rearrange("p a b -> p (a b)"), ("Bn", 1))
                                elif o == 4:
                                    extra, nb = None, (Bn[2][:, quad * 4:(quad + 1) * 4, :].rearrange("p a b -> p (a b)"), ("Bn", 2))
                                else:
                                    extra, nb = (eext[64:97, kt, :], rq[64:97], ["eext", "rbuf"]), None
                                tiles.append((kwT[:, kt * 128:(kt + 1) * 128], "kwT", extra, nb, 4, Vw[:, kt, :], ("V", 1), o == 0, o == nw))
                            prev = None
                            for (l_ap, lk, extra, nb, obase, v_ap, vk, first, last) in tiles:
                                pt, ptk = qk(l_ap, lk, quad, t0, extra, nb)
                                if prev is not None:
                                    pv(*prev)
                                prev = (pt, ptk, obase, 129, v_ap, vk, first, last)
                            pv(*prev)
                            oi = cnts["o"] % 2
                            cnts["o"] += 1
                            for hq in range(4):
                                hl = quad * 4 + hq

                                def _epi(hl=hl, hq=hq):
                                    c0 = (hq % 2) * 256
                                    bs, bw = 2 + hq // 2, 4 + hq // 2
                                    ks_, kw_ = ("ps", bs, hq % 2), ("ps", bw, hq % 2)
                                    P.op("vector", lambda e: e.reciprocal(out=sm[:, hl, 2:3], in_=ps[bs][:, c0 + 128:c0 + 129]), reads=[ks_], writes=[("sm2", hl)])
                                    P.op("vector", lambda e: e.tensor_tensor(out=sm[:, hl, 3:4], in0=sm[:, hl, 2:3], in1=brg[:, qt, hl * 3 + 1:hl * 3 + 2], op=ALU.mult), reads=[("sm2", hl), "brg"], writes=[("sm2", hl)])
                                    P.op("vector", lambda e: e.reciprocal(out=sm[:, hl, 4:5], in_=ps[bw][:, c0 + 128:c0 + 129]), reads=[kw_], writes=[("sm3", hl)])
                                    P.op("vector", lambda e: e.tensor_tensor(out=sm[:, hl, 5:6], in0=sm[:, hl, 4:5], in1=brg[:, qt, hl * 3 + 2:hl * 3 + 3], op=ALU.mult), reads=[("sm3", hl), "brg"], writes=[("sm3", hl)])
                                    P.op("vector", lambda e: e.scalar_tensor_tensor(out=otok[:, hl, :], in0=ps[bs][:, c0:c0 + 128], scalar=sm[:, hl, 3:4], in1=ocmp[:, hl, :], op0=ALU.mult, op1=ALU.add), reads=[ks_, ("sm2", hl), ("ocmp", hl)], writes=[("otok", hl)])
                                    P.op("vector", lambda e: e.scalar_tensor_tensor(out=otok[:, hl, :], in0=ps[bw][:, c0:c0 + 128], scalar=sm[:, hl, 5:6], in1=otok[:, hl, :], op0=ALU.mult, op1=ALU.add), reads=[kw_, ("sm3", hl), ("otok", hl)], writes=[("otok", hl)])
                                    P.op("tensor", lambda e: e.transpose(out=ps[7][:, hq * 128:(hq + 1) * 128], in_=otok[:, hl, :], identity=identf[:, :]), reads=[("otok", hl), "identf"], writes=[("ps", 7)])
                                _epi()
                            P.op("scalar", (lambda oi=oi: lambda e: e.activation(out=ost[oi][:, :, :].rearrange("p a b -> p (a b)"), in_=ps[7][:, :], func=AF.Copy))(), reads=[("ps", 7)], writes=[("ost", oi)])
                            r0 = g * 1024 + quad * 512
                            kb.store("sync", attnT[r0:r0 + 512, t0:t0 + 128].rearrange("(h d) q -> d h q", d=128), ost[oi][:, :, :], ("ost", oi), f"d_ost{oi}", ("dram", "attnT"))
                    for qt in range(NQT):
                        _do_qt(qt)
                    P.end_phase()


def t5_bucket_np(d):
    n = np.maximum(d, 0)
    nf = np.maximum(n, 16).astype(np.float32)
    large = 16 + (np.log(nf / np.float32(16)) / np.float32(np.log(128 / 16)) * np.float32(16)).astype(np.int32)
    large = np.minimum(large, 31)
    return np.where(n < 16, n, large)


def make_consts(T, rel_bias):
    bf = ml_dtypes.bfloat16
    NQT = T // 128
    NCMP = T // 16 - 1
    NCT = (NCMP + 1 + 127) // 128
    nsb = T // 64
    tab = np.concatenate([np.asarray(rel_bias, np.float32), np.full((1, NH), NEG, np.float32)], axis=0)
    cs = {}
    cs["identf"] = np.eye(128, dtype=np.float32)
    cs["identb"] = np.eye(128, dtype=np.float32).astype(bf)
    ee = np.zeros((97, NQT, 128), np.float32)
    for kt in range(NQT):
        if 2 * kt < 64:
            ee[2 * kt, kt, 0:64] = 1.0
        if 2 * kt + 1 < 64:
            ee[2 * kt + 1, kt, 64:128] = 1.0
    ee[64] = 1.0
    ee[96] = 1.0
    cs["eext"] = ee.astype(bf)
    cs["b31"] = np.ascontiguousarray(np.broadcast_to(tab[31][None, :], (128, NH)))
    c_start = np.arange(NCMP) * 16
    s_start = np.arange(nsb) * 64
    ov = np.minimum(c_start[:, None] + 32, s_start[None, :] + 64) - np.maximum(c_start[:, None], s_start[None, :])
    ov = np.clip(ov, 0, None) // 16
    vcc = np.zeros((NCT * 128, 65), np.float32)
    vcc[:NCMP, :nsb] = ov
    vcc[:, 64] = 1.0
    cs["vcc"] = vcc.reshape(NCT, 128, 65).astype(bf)
    kk = np.arange(128)[:, None]
    qq = np.arange(128)[None, :]
    def near(off, extra_mask=None):
        d = qq - kk + off
        idx = np.where(d >= 0, t5_bucket_np(d), 32)
        if extra_mask is not None:
            idx = np.where(extra_mask, idx, 32)
        return np.ascontiguousarray(tab[idx].transpose(0, 2, 1))
    cs["B0"] = near(0)
    cs["B1"] = near(128)
    cs["Bw4"] = near(512, extra_mask=(qq - kk + 512 < 512))
    i_all = np.arange(NCT * 128)
    t_all = np.arange(T)
    d = t_all[None, :] - (16 * i_all[:, None] + 31)
    idx = np.where((d >= 0) & (i_all[:, None] < NCMP), t5_bucket_np(d), 32)
    bc = tab[idx]
    bc = bc.reshape(NCT, 128, NQT, 128, NH).transpose(2, 0, 1, 4, 3)
    cs["bcmp"] = np.ascontiguousarray(bc)
    cur = t_all // 64
    blk = np.arange(64)[None, :]
    forced = (blk == 0) | (blk == cur[:, None]) | (blk == cur[:, None] - 1)
    valid = (blk <= cur[:, None]) & (blk < nsb)
    mv = (valid & ~forced).astype(np.float32)
    fb = np.where(~valid, -1.0e9, np.where(forced, 1.0e9, 0.0)).astype(np.float32)
    cs["mvfb"] = np.ascontiguousarray(np.stack([mv, fb], axis=1).reshape(NQT, 128, 2, 64))
    return cs


CONST_SPECS = lambda T: {
    "identf": ([128, 128], F32), "identb": ([128, 128], BF16), "eext": ([97, T // 128, 128], BF16), "b31": ([128, NH], F32),
    "vcc": ([((T // 16) + 127) // 128, 128, 65], BF16), "B0": ([128, NH, 128], F32), "B1": ([128, NH, 128], F32), "Bw4": ([128, NH, 128], F32),
    "bcmp": ([T // 128, ((T // 16) + 127) // 128, 128, NH, 128], F32), "mvfb": ([T // 128, 128, 2, 64], F32)}


def build_fused(T, L):
    kb = KB()
    nc, P = kb.nc, kb.P
    Tb = min(2048, T)
    I = "ExternalInput"
    xT = kb.dram("xT", [D, T], F32, I)
    w = {}
    for nm, shp in (("w_in", [L, D, INC]), ("gmix", [L, 128, KC]), ("bglu", [L, 128, 64]), ("posT", [L, 2, 128, 32]), ("w1", [L, 2, D, 128]), ("w2", [L, 2, 128, 128]),
                    ("wdw", [L, 128, KC, 31]), ("vecs", [L, 128, 6, KC]), ("w_co", [L, D, D]), ("w_ao", [L, D, D]), ("w_out", [L, D, D]),
                    ("w_fg", [L, D, DFF]), ("w_fu", [L, D, DFF]), ("w_fd", [L, DFF, D])):
        w[nm] = kb.dram(nm, shp, F32, I)
    cs = {k: kb.dram("c_" + k, shp, dt, I) for k, (shp, dt) in CONST_SPECS(T).items()}
    outT = kb.dram("outT", [D, T], F32, "ExternalOutput")
    N = "Internal"
    DBG = "ExternalOutput" if DEBUG else N
    projT = kb.dram("projT", [PROJ_ROWS, 32 + T], BF16, DBG)
    attnT = kb.dram("attnT", [D, T], BF16, DBG)
    xbuf = [kb.dram(f"xbuf{i}", [D, T], F32, N) for i in range(2)]
    scr = {"zT": kb.dram("zT", [D, Tb], BF16, DBG), "mbT": kb.dram("mbT", [D, Tb], F32, N), "mT": kb.dram("mT", [D, Tb], BF16, N),
           "x1T": kb.dram("x1T", [D, Tb], F32, DBG), "actT": kb.dram("actT", [DFF, Tb], BF16, N), "x2T": kb.dram("x2T", [D, Tb], F32, N)}
    with kb.es:
        with contextlib.ExitStack() as ph:
            z = kb.sb(ph, "zpad", [128, KC, 32], BF16)
            P.op("vector", lambda e: e.memset(z[:, :, :], 0.0), writes=["zpad"])
            kb.store("sync", projT[C_A:C_A + D, 0:32].rearrange("(c p) t -> p c t", p=128), z[:, :, :], "zpad", "d_c0", ("dram", "projT"))
            P.end_phase()
        src = xT
        for l in range(L):
            for t0 in range(0, T, Tb):
                emit_A(kb, Tb, src[:, t0:t0 + Tb], w["w_in"][l], w["gmix"][l], w["bglu"][l], projT[:, 32 + t0:32 + t0 + Tb])
            emit_B(kb, T, projT[:, 32:32 + T], attnT, {"w1": w["w1"][l], "w2": w["w2"][l], "posT": w["posT"][l]}, cs)
            final = (l == L - 1)
            dst = outT if final else xbuf[l % 2]
            for t0 in range(0, T, Tb):
                t = dict(scr)
                t.update({"xT": src[:, t0:t0 + Tb], "gluT": projT[C_A:C_A + D, t0:t0 + Tb + 32], "attnT": attnT[:, t0:t0 + Tb],
                          "gaT": projT[C_GA:C_GA + D, 32 + t0:32 + t0 + Tb], "gbT": projT[C_GB:C_GB + D, 32 + t0:32 + t0 + Tb],
                          "wdw": w["wdw"][l], "vecs": w["vecs"][l], "w_co": w["w_co"][l], "w_ao": w["w_ao"][l], "w_out": w["w_out"][l],
                          "w_fg": w["w_fg"][l], "w_fu": w["w_fu"][l], "w_fd": w["w_fd"][l], "outT": dst[:, t0:t0 + Tb], "identb": cs["identb"]})
                emit_C(kb, Tb, final, t)
            src = dst
    return nc


def host_inputs(T, L, inp, cs):
    f = lambda a: np.ascontiguousarray(np.asarray(a, dtype=np.float32))
    d = {}
    d["w_in"] = f(inp["w_in"][:L])
    d["gmix"] = np.stack([lay_vec(inp["norm_mix"][l]) for l in range(L)])
    d["bglu"] = np.stack([lay_vec(inp["b_glu"][l]) for l in range(L)])
    d["posT"] = f(np.asarray(inp["cmp_pos"])[:L].transpose(0, 1, 3, 2))
    d["w1"] = f(inp["cmp_w1"][:L])
    d["w2"] = f(inp["cmp_w2"][:L])
    d["wdw"] = np.stack([lay_wdw(inp["w_dw"][l]) for l in range(L)])
    d["vecs"] = np.stack([np.stack([lay_vec(np.asarray(inp[k])[l] if k != "norm_final" else np.asarray(inp[k])) for k in ("b_dw", "conv_ln_g", "conv_ln_b", "b_conv_out", "norm_ffn", "norm_final")], axis=1) for l in range(L)])
    for k, n in (("w_co", "w_conv_out"), ("w_ao", "w_attn_out"), ("w_out", "w_out"), ("w_fg", "w_ffn_gate"), ("w_fu", "w_ffn_up"), ("w_fd", "w_ffn_down")):
        d[k] = f(inp[n][:L])
    for k, v in cs.items():
        d["c_" + k] = np.ascontiguousarray(v)
    return d


_FUSED = {}
DEBUG = False
LAST = {}


def run_fused(inp, L=2):
    x = np.asarray(inp["x"], dtype=np.float32)
    B, T, _ = x.shape
    key = (T, L)
    if key not in _FUSED:
        _FUSED[key] = build_fused(T, L)
    nc = _FUSED[key]
    cs = make_consts(T, inp["rel_bias"])
    common = host_inputs(T, L, inp, cs)
    ins = []
    for b in range(B):
        dd = dict(common)
        dd["xT"] = np.ascontiguousarray(x[b].T)
        ins.append(dd)
    res = run_bass_kernel_spmd(nc, ins, core_ids=list(range(B)))
    if DEBUG:
        LAST.update(res.results[0])
        LAST["all"] = res.results
    out = np.stack([np.ascontiguousarray(r["outT"].T) for r in res.results], axis=0)
    return out.astype(np.float32)


def kernel(**inputs):
    return run_fused(inputs, L=2)
```
